# Optimizing a Trainium2 kernel written in Bass

```python
import math
import jax
import jax.numpy as jnp
from jax import lax
import numpy as np

D_MODEL = 1024
BATCH = 8
SEQ = 2048
DEPTH = 2

GRID_W = 64
CTX_LEN = 256
N_MIXERS = 2
N_MLA = (DEPTH + 1) // 2
N_HYENA = DEPTH // 2

MLA_HEADS = 16
QK_NOPE = 64
QK_ROPE = 32
QK_DIM = QK_NOPE + QK_ROPE
V_HEAD = 64
Q_LORA = 512
KV_LORA = 256
ROPE_THETA = 10000.0
Q_BLOCK = 128

HY_BANDS = 16
HY_POS_DIM = 1 + 2 * HY_BANDS
HY_FILTER_HIDDEN = 64
HY_DECAY_SLOW = math.log(100.0) / 1.5
HY_DECAY_FAST = math.log(100.0) / 0.3
HY_FILTER_INIT = 0.02

D_FF = 2816

EPS = 1e-6
F32 = jnp.float32

kernel_name = 'hybrid_mla_hyena_prefix_dit'


def rmsnorm(x, g):
    xf = x.astype(F32)
    y = xf * lax.rsqrt(jnp.mean(xf * xf, axis=-1, keepdims=True) + EPS)
    return (y * g.astype(F32)).astype(x.dtype)


def modulate(h, shift, scale):
    return h * (1 + scale) + shift


def dwconv3(x, w, b):
    xp = jnp.pad(x, ((0, 0), (1, 1), (0, 0)))
    return xp[:, :-2] * w[0] + xp[:, 1:-1] * w[1] + xp[:, 2:] * w[2] + b


def axial_rope_tables(rows):
    row = jnp.repeat(jnp.arange(rows, dtype=F32), GRID_W)
    col = jnp.tile(jnp.arange(GRID_W, dtype=F32), rows)
    half = QK_ROPE // 2
    inv_freq = ROPE_THETA ** (-jnp.arange(0, half, 2, dtype=F32) / half)
    ang_r = row[:, None] * inv_freq
    ang_c = col[:, None] * inv_freq
    ang = jnp.concatenate([ang_r, ang_r, ang_c, ang_c], axis=-1)
    return jnp.cos(ang), jnp.sin(ang)


def apply_axial_rope(x, cos, sin):
    q = QK_ROPE // 4
    xf = x.astype(F32)
    rot = jnp.concatenate([-xf[..., q:2 * q], xf[..., :q], -xf[..., 3 * q:], xf[..., 2 * q:3 * q]], axis=-1)
    return (xf * cos + rot * sin).astype(x.dtype)


def mla_queries(h, w_dq, g_q, w_uq, cos, sin):
    B, L, _ = h.shape
    cq = rmsnorm(h @ w_dq, g_q)
    q = (cq @ w_uq).reshape(B, L, MLA_HEADS, QK_DIM)
    q_nope, q_pe = q[..., :QK_NOPE], q[..., QK_NOPE:]
    if cos is not None:
        q_pe = apply_axial_rope(q_pe, cos[None, :, None, :], sin[None, :, None, :])
    return jnp.concatenate([q_nope, q_pe], axis=-1)


def mla_keys_values(h, w_dkv, g_kv, w_uk, w_uv, cos, sin):
    B, L, _ = h.shape
    kv = h @ w_dkv
    ckv = rmsnorm(kv[..., :KV_LORA], g_kv)
    k_pe = kv[..., KV_LORA:]
    if cos is not None:
        k_pe = apply_axial_rope(k_pe, cos[None], sin[None])
    k_nope = (ckv @ w_uk).reshape(B, L, MLA_HEADS, QK_NOPE)
    v = (ckv @ w_uv).reshape(B, L, MLA_HEADS, V_HEAD)
    k_pe = jnp.broadcast_to(k_pe[:, :, None, :], (B, L, MLA_HEADS, QK_ROPE))
    return jnp.concatenate([k_nope, k_pe], axis=-1), v


def block_attention(q, k, v):
    B, Lq, H, dk = q.shape
    nblk = Lq // Q_BLOCK
    scale = 1.0 / math.sqrt(dk)
    qb = q.reshape(B, nblk, Q_BLOCK, H, dk).transpose(1, 0, 2, 3, 4)

    def one_block(qblk):
        s = jnp.einsum('bqhd,bkhd->bhqk', qblk, k).astype(F32) * scale
        p = jax.nn.softmax(s, axis=-1).astype(v.dtype)
        return jnp.einsum('bhqk,bkhd->bqhd', p, v)

    o = lax.map(one_block, qb)
    return o.transpose(1, 0, 2, 3, 4).reshape(B, Lq, H * v.shape[-1])


def hyena_pos_features(L):
    t = jnp.linspace(0.0, 1.0, L, dtype=F32)
    w = 2.0 * math.pi * jnp.arange(L, dtype=F32) / L
    f = jnp.linspace(1e-4, HY_BANDS - 1, HY_BANDS, dtype=F32)
    z = jnp.concatenate([t[:, None], jnp.cos(w[:, None] * f), -jnp.sin(w[:, None] * f)], axis=-1)
    return t, z


def hyena_filters(L, f_w1, f_b1, f_freq1, f_w2, f_b2, f_freq2, f_w3, decay):
    t, z = hyena_pos_features(L)
    h = jnp.sin(f_freq1.astype(F32) * (z @ f_w1.astype(F32) + f_b1.astype(F32)))
    h = jnp.sin(f_freq2.astype(F32) * (h @ f_w2.astype(F32) + f_b2.astype(F32)))
    h = h @ f_w3.astype(F32)
    window = jnp.exp(-t[:, None] * jnp.abs(decay.astype(F32))[None, :])
    d = h.shape[1] // 2
    return h[:, :d] * window, h[:, d:] * window


def two_sided_fftconv(u, h_fwd, h_bwd):
    L = u.shape[1]
    filt = jnp.concatenate([h_fwd, jnp.zeros((1, h_fwd.shape[1]), F32), h_bwd[1:][::-1]], axis=0)
    filt_f = jnp.fft.rfft(filt, n=2 * L, axis=0)
    u_f = jnp.fft.rfft(u.astype(F32), n=2 * L, axis=1)
    y = jnp.fft.irfft(u_f * filt_f[None], n=2 * L, axis=1)[:, :L]
    return y.astype(u.dtype)


def hyena_mixer(u, w_in, b_in, conv_w, conv_b, f_w1, f_b1, f_freq1, f_w2, f_b2, f_freq2, f_w3, decay, d_bias, w_out, b_out):
    L = u.shape[1]
    z = dwconv3(u @ w_in + b_in, conv_w, conv_b)
    x1, x2, v = jnp.split(z, 3, axis=-1)
    h_fwd, h_bwd = hyena_filters(L, f_w1, f_b1, f_freq1, f_w2, f_b2, f_freq2, f_w3, decay)
    v = v * x2
    y = (two_sided_fftconv(v, h_fwd, h_bwd) + v * d_bias) * x1
    return y @ w_out + b_out


def conv_ffn(h, w_up, conv_w, conv_b, w_down):
    z = dwconv3(h @ w_up, conv_w, conv_b)
    a, g = jnp.split(z, 2, axis=-1)
    return (jax.nn.silu(g) * a) @ w_down


def setup_inputs(seed: int = 0) -> dict:
    key = jax.random.key(seed)
    ks = jax.random.split(key, 40)
    D = D_MODEL

    def nrm(i, shape, scale):
        return jax.random.normal(ks[i], shape, F32) * scale

    def gain(i, shape):
        return 1.0 + nrm(i, shape, 0.05)

    decay_base = jnp.linspace(HY_DECAY_SLOW, HY_DECAY_FAST, D, dtype=F32)
    return {
        'x': nrm(0, (BATCH, SEQ, D), 1.0),
        'c': nrm(1, (BATCH, D), 1.0),
        'ctx': nrm(2, (BATCH, CTX_LEN, D), 1.0),
        'c_ctx': nrm(3, (D,), 1.0),
        'mod_w': nrm(4, (DEPTH, D, 6 * D), D ** -0.5),
        'mod_b': nrm(5, (DEPTH, 6 * D), 0.02),
        'norm_mix_g': gain(6, (DEPTH, D)),
        'norm_ffn_g': gain(7, (DEPTH, D)),
        'mla_w_dq': nrm(8, (N_MLA, D, Q_LORA), D ** -0.5),
        'mla_g_q': gain(9, (N_MLA, Q_LORA)),
        'mla_w_uq': nrm(10, (N_MLA, Q_LORA, MLA_HEADS * QK_DIM), Q_LORA ** -0.5),
        'mla_w_dkv': nrm(11, (N_MLA, D, KV_LORA + QK_ROPE), D ** -0.5),
        'mla_g_kv': gain(12, (N_MLA, KV_LORA)),
        'mla_w_uk': nrm(13, (N_MLA, KV_LORA, MLA_HEADS * QK_NOPE), KV_LORA ** -0.5),
        'mla_w_uv': nrm(14, (N_MLA, KV_LORA, MLA_HEADS * V_HEAD), KV_LORA ** -0.5),
        'mla_w_o': nrm(15, (N_MLA, MLA_HEADS * V_HEAD, D), (MLA_HEADS * V_HEAD) ** -0.5),
        'hy_w_in': nrm(16, (N_HYENA, D, 3 * D), D ** -0.5),
        'hy_b_in': nrm(17, (N_HYENA, 3 * D), 0.02),
        'hy_conv_w': nrm(18, (N_HYENA, 3, 3 * D), 3 ** -0.5),
        'hy_conv_b': nrm(19, (N_HYENA, 3 * D), 0.02),
        'hy_f_w1': nrm(20, (N_HYENA, HY_POS_DIM, HY_FILTER_HIDDEN), HY_POS_DIM ** -0.5),
        'hy_f_b1': nrm(21, (N_HYENA, HY_FILTER_HIDDEN), 0.1),
        'hy_f_freq1': gain(22, (N_HYENA, HY_FILTER_HIDDEN)),
        'hy_f_w2': nrm(23, (N_HYENA, HY_FILTER_HIDDEN, HY_FILTER_HIDDEN), HY_FILTER_HIDDEN ** -0.5),
        'hy_f_b2': nrm(24, (N_HYENA, HY_FILTER_HIDDEN), 0.1),
        'hy_f_freq2': gain(25, (N_HYENA, HY_FILTER_HIDDEN)),
        'hy_f_w3': nrm(26, (N_HYENA, HY_FILTER_HIDDEN, 2 * D), HY_FILTER_INIT),
        'hy_decay': decay_base[None, :] * (1.0 + nrm(27, (N_HYENA, D), 0.05)),
        'hy_d_bias': nrm(28, (N_HYENA, D), 1.0),
        'hy_w_out': nrm(29, (N_HYENA, D, D), D ** -0.5),
        'hy_b_out': nrm(30, (N_HYENA, D), 0.02),
        'ffn_w_up': nrm(31, (DEPTH, D, 2 * D_FF), D ** -0.5),
        'ffn_conv_w': nrm(32, (DEPTH, 3, 2 * D_FF), 3 ** -0.5),
        'ffn_conv_b': nrm(33, (DEPTH, 2 * D_FF), 0.02),
        'ffn_w_down': nrm(34, (DEPTH, D_FF, D), D_FF ** -0.5),
        'final_g': gain(35, (D,)),
    }


def reference(x, c, ctx, c_ctx, mod_w, mod_b, norm_mix_g, norm_ffn_g,
              mla_w_dq, mla_g_q, mla_w_uq, mla_w_dkv, mla_g_kv, mla_w_uk, mla_w_uv, mla_w_o,
              hy_w_in, hy_b_in, hy_conv_w, hy_conv_b, hy_f_w1, hy_f_b1, hy_f_freq1, hy_f_w2, hy_f_b2,
              hy_f_freq2, hy_f_w3, hy_decay, hy_d_bias, hy_w_out, hy_b_out,
              ffn_w_up, ffn_conv_w, ffn_conv_b, ffn_w_down, final_g):
    L = x.shape[1]
    ROWS = L // GRID_W
    cos, sin = axial_rope_tables(ROWS)
    silu_c = jax.nn.silu(c)
    silu_cc = jax.nn.silu(c_ctx)
    h_ctx = ctx
    for i in range(DEPTH):
        last = i == DEPTH - 1
        j = i // N_MIXERS
        mod_x = silu_c @ mod_w[i] + mod_b[i]
        mod_c = silu_cc @ mod_w[i] + mod_b[i]
        sh1, sc1, g1, sh2, sc2, g2 = [m[:, None, :] for m in jnp.split(mod_x, 6, axis=-1)]
        csh1, csc1, cg1, csh2, csc2, cg2 = jnp.split(mod_c, 6, axis=-1)
        hx = modulate(rmsnorm(x, norm_mix_g[i]), sh1, sc1)
        if i % N_MIXERS == 0:
            hc = modulate(rmsnorm(h_ctx, norm_mix_g[i]), csh1, csc1)
            qx = mla_queries(hx, mla_w_dq[j], mla_g_q[j], mla_w_uq[j], cos, sin)
            kx, vx = mla_keys_values(hx, mla_w_dkv[j], mla_g_kv[j], mla_w_uk[j], mla_w_uv[j], cos, sin)
            kc, vc = mla_keys_values(hc, mla_w_dkv[j], mla_g_kv[j], mla_w_uk[j], mla_w_uv[j], None, None)
            ox = block_attention(qx, jnp.concatenate([kc, kx], axis=1), jnp.concatenate([vc, vx], axis=1))
            x = x + g1 * (ox @ mla_w_o[j])
            if not last:
                qc = mla_queries(hc, mla_w_dq[j], mla_g_q[j], mla_w_uq[j], None, None)
                oc = block_attention(qc, kc, vc)
                h_ctx = h_ctx + cg1 * (oc @ mla_w_o[j])
        else:
            hy = (hy_w_in[j], hy_b_in[j], hy_conv_w[j], hy_conv_b[j], hy_f_w1[j], hy_f_b1[j], hy_f_freq1[j],
                  hy_f_w2[j], hy_f_b2[j], hy_f_freq2[j], hy_f_w3[j], hy_decay[j], hy_d_bias[j],
                  hy_w_out[j], hy_b_out[j])
            x = x + g1 * hyena_mixer(hx, *hy)
            if not last:
                hc = modulate(rmsnorm(h_ctx, norm_mix_g[i]), csh1, csc1)
                h_ctx = h_ctx + cg1 * hyena_mixer(hc, *hy)
        ffn = (ffn_w_up[i], ffn_conv_w[i], ffn_conv_b[i], ffn_w_down[i])
        x = x + g2 * conv_ffn(modulate(rmsnorm(x, norm_ffn_g[i]), sh2, sc2), *ffn)
        if not last:
            h_ctx = h_ctx + cg2 * conv_ffn(modulate(rmsnorm(h_ctx, norm_ffn_g[i]), csh2, csc2), *ffn)
    return rmsnorm(x, final_g)
```

```python
import contextlib
import math
import numpy as np
import ml_dtypes
import concourse.bass as bass
import concourse.mybir as mybir
from concourse.bass_utils import run_bass_kernel_spmd

F32 = mybir.dt.float32
BF16 = mybir.dt.bfloat16
ALU = mybir.AluOpType
AF = mybir.ActivationFunctionType

D = 1024
L = 2048
LC = 256
LK = LC + L
NH = 16
DFF = 2816
EPS = 1e-6
NCORES = 8
MAGIC = 12582912.0


class Reg:
    __slots__ = ("w", "rd")

    def __init__(self):
        self.w = None
        self.rd = []


def regs(*shape):
    if len(shape) == 1:
        return [Reg() for _ in range(shape[0])]
    return [regs(*shape[1:]) for _ in range(shape[0])]


class DSem:
    def __init__(self, sem):
        self.sem = sem
        self.count = 0


class Prog:
    ENG = ("pe", "act", "dve", "pool", "sp")

    def __init__(self, nc, stack):
        self.nc = nc
        self.stack = stack
        self.ops = {e: [] for e in self.ENG}
        self.cnt = {e: 0 for e in self.ENG}
        self.seen = {e: {} for e in self.ENG}
        self.esem = {e: stack.enter_context(nc.semaphore("s_" + e)) for e in self.ENG}
        self.dsems = []

    def dsem(self, name):
        d = DSem(self.stack.enter_context(self.nc.semaphore(name)))
        self.dsems.append(d)
        return d

    def _deps(self, eng, reads, writes, skip_self):
        need = {}

        def add(ev):
            key, val = ev
            if skip_self and key == eng:
                return
            if self.seen[eng].get(key, 0) >= val:
                return
            if need.get(key, 0) < val:
                need[key] = val

        for r in reads:
            if r.w is not None:
                add(r.w)
        for w in writes:
            if w.w is not None:
                add(w.w)
            for ev in w.rd:
                add(ev)
        waits = []
        for key, val in need.items():
            self.seen[eng][key] = val
            waits.append((self.esem[key] if isinstance(key, str) else key.sem, val))
        return waits

    def _commit(self, ev, reads, writes):
        for r in reads:
            if len(r.rd) > 24:
                r.rd = r.rd[-24:] if False else r.rd
            r.rd.append(ev)
        for w in writes:
            w.w = ev
            w.rd = []

    def op(self, eng, fn, reads=(), writes=()):
        waits = self._deps(eng, reads, writes, eng == "pe")
        self.cnt[eng] += 1
        ev = (eng, self.cnt[eng])
        sem = self.esem[eng]

        def emit(e):
            for s, v in waits:
                e.wait_ge(s, v)
            fn(e).then_inc(sem, 1)

        self.ops[eng].append(emit)
        self._commit(ev, reads, writes)

    def dma(self, q, dsem, pairs, reads=(), writes=()):
        waits = self._deps(q, reads, writes, False)
        dsem.count += 16 * len(pairs)
        ev = (dsem, dsem.count)
        sem = dsem.sem

        def emit(e):
            for s, v in waits:
                e.wait_ge(s, v)
            for out, in_ in pairs:
                e.dma_start(out=out, in_=in_).then_inc(sem, 16)

        self.ops[q].append(emit)
        self._commit(ev, reads, writes)

    def barrier(self):
        for e in self.ENG:
            waits = []
            for e2 in self.ENG:
                v = self.cnt[e2]
                if v > self.seen[e].get(e2, 0):
                    self.seen[e][e2] = v
                    waits.append((self.esem[e2], v))
            for d in self.dsems:
                if d.count > self.seen[e].get(d, 0):
                    self.seen[e][d] = d.count
                    waits.append((d.sem, d.count))

            def emit(eh, waits=waits):
                for s, v in waits:
                    eh.wait_ge(s, v)

            self.ops[e].append(emit)

    def finish(self):
        self.barrier()
        ops = self.ops
        with self.nc.Block() as block:
            @block.tensor
            def _(e):
                for f in ops["pe"]:
                    f(e)

            @block.scalar
            def _(e):
                for f in ops["act"]:
                    f(e)

            @block.vector
            def _(e):
                for f in ops["dve"]:
                    f(e)

            @block.gpsimd
            def _(e):
                for f in ops["pool"]:
                    f(e)

            @block.sync
            def _(e):
                for f in ops["sp"]:
                    f(e)


def _layout():
    cols = {}
    off = [0]

    def add(name, n):
        cols[name] = off[0]
        off[0] += n

    add("c", 8)
    add("cc", 8)
    for i in range(2):
        add(f"modb{i}", 48)
    for i in range(2):
        add(f"gmix{i}", 8)
        add(f"gffn{i}", 8)
    add("gfin", 8)
    add("gq", 4)
    add("gkv", 2)
    add("hbin", 24)
    for j in range(3):
        add(f"hcw{j}", 24)
    add("hcb", 24)
    add("hbout", 8)
    for i in range(2):
        for j in range(3):
            add(f"fcw{i}{j}", 44)
        add(f"fcb{i}", 44)
    for n in ("fb1", "ffr1", "fb2", "ffr2"):
        add(n, 1)
    return cols, off[0]


PCOL, PR = _layout()


def _colsof(v):
    v = np.asarray(v, np.float32).reshape(-1)
    if v.size < 128:
        v = np.concatenate([v, np.zeros(128 - v.size, np.float32)])
    return v.reshape(-1, 128).T


def _pack_pvec(inp, b):
    pv = np.zeros((128, PR), np.float32)

    def put(name, v):
        c = _colsof(v)
        pv[:, PCOL[name]:PCOL[name] + c.shape[1]] = c

    put("c", inp["c"][b])
    put("cc", inp["c_ctx"])
    for i in range(2):
        put(f"modb{i}", inp["mod_b"][i])
        put(f"gmix{i}", inp["norm_mix_g"][i])
        put(f"gffn{i}", inp["norm_ffn_g"][i])
    put("gfin", inp["final_g"])
    put("gq", inp["mla_g_q"][0])
    put("gkv", inp["mla_g_kv"][0])
    put("hbin", inp["hy_b_in"][0])
    for j in range(3):
        put(f"hcw{j}", inp["hy_conv_w"][0, j])
    put("hcb", inp["hy_conv_b"][0])
    put("hbout", inp["hy_b_out"][0])
    for i in range(2):
        for j in range(3):
            put(f"fcw{i}{j}", inp["ffn_conv_w"][i, j])
        put(f"fcb{i}", inp["ffn_conv_b"][i])
    put("fb1", inp["hy_f_b1"][0])
    put("ffr1", inp["hy_f_freq1"][0])
    put("fb2", inp["hy_f_b2"][0])
    put("ffr2", inp["hy_f_freq2"][0])
    return pv


_CONST = {}


def _consts():
    if _CONST:
        return _CONST
    f32 = np.float32
    rows = L // 64
    row = np.repeat(np.arange(rows, dtype=f32), 64)
    col = np.tile(np.arange(64, dtype=f32), rows)
    half = 16
    inv = (f32(10000.0) ** (-np.arange(0, half, 2, dtype=f32) / f32(half))).astype(f32)
    ar = row[:, None] * inv
    ac = col[:, None] * inv
    ang = np.concatenate([ar, ar, ac, ac], axis=-1).astype(f32)
    tbl = np.zeros((128, 2, L), f32)
    tbl[64:96, 0, :] = np.cos(ang).T
    tbl[64:96, 1, :] = np.sin(ang).T
    t = np.linspace(0.0, 1.0, L, dtype=f32)
    w = (f32(2.0 * math.pi) * np.arange(L, dtype=f32) / f32(L)).astype(f32)
    fr = np.linspace(1e-4, 15, 16, dtype=f32)
    z = np.concatenate([t[:, None], np.cos(w[:, None] * fr), -np.sin(w[:, None] * fr)], axis=-1).astype(f32)
    zT = np.ascontiguousarray(z.T)
    negt = np.ascontiguousarray((-t).reshape(16, 128).T)
    tt = np.arange(L, dtype=np.float64)
    ff = np.arange(L, dtype=np.float64) + 0.5
    th = 2.0 * np.pi * np.outer(tt, ff) / 4096.0
    C = np.cos(th)
    S = np.sin(th)
    fwd = np.concatenate([C, S], axis=1)
    FWm = fwd.reshape(16, 128, 32, 128).transpose(2, 1, 0, 3)
    FWm = np.ascontiguousarray(FWm).reshape(32, 128, 2048).astype(ml_dtypes.bfloat16)
    inv_m = np.concatenate([C.T, -S.T], axis=0)
    IVm = inv_m.reshape(32, 128, 8, 256).transpose(2, 1, 0, 3)
    IVm = np.ascontiguousarray(IVm).reshape(8, 128, 32 * 256).astype(ml_dtypes.bfloat16)
    ident = np.eye(128, dtype=f32).astype(ml_dtypes.bfloat16)
    _CONST.update(tbl=tbl, zT=zT, negt=negt, fw=FWm, iv=IVm, ident=ident)
    return _CONST


WEIGHT_NAMES = ["mod_w", "mla_w_dq", "mla_w_uq", "mla_w_dkv", "mla_w_uk", "mla_w_uv", "mla_w_o",
                "hy_w_in", "hy_f_w1", "hy_f_w2", "hy_f_w3", "hy_w_out", "ffn_w_up", "ffn_w_down"]
WEIGHT_SHAPES = {
    "mod_w": (2, 1024, 6144), "mla_w_dq": (1, 1024, 512), "mla_w_uq": (1, 512, 1536),
    "mla_w_dkv": (1, 1024, 288), "mla_w_uk": (1, 256, 1024), "mla_w_uv": (1, 256, 1024),
    "mla_w_o": (1, 1024, 1024), "hy_w_in": (1, 1024, 3072), "hy_f_w1": (1, 33, 64),
    "hy_f_w2": (1, 64, 64), "hy_f_w3": (1, 64, 2048), "hy_w_out": (1, 1024, 1024),
    "ffn_w_up": (2, 1024, 5632), "ffn_w_down": (2, 2816, 1024),
}


def KB(x):
    return int(round(x * 256))


def build(stage="all"):
    nc = bass.Bass("TRN2", target_bir_lowering=False)

    def din(name, shape, dt=F32):
        return nc.dram_tensor(name, list(shape), dt, kind="ExternalInput").ap()

    xT = din("xT", [D, L])
    ctxT = din("ctxT", [D, LC])
    pvec = din("pvec", [128, PR])
    rows_d = din("rows", [2, D])
    tbl_d = din("tbl", [128, 3, L])[:, 0:2, :]
    zT_d = din("zT", [33, L])
    negt_d = din("negt", [128, 16])
    fw_d = din("fw", [32, 129, 2048], BF16)[:, 0:128, :]
    iv_d = din("iv", [8, 129, 32 * 256], BF16)[:, 0:128, :]
    ident_d = din("ident", [128, 128], BF16)
    Wd = {}
    for n in WEIGHT_NAMES:
        a, r, c = WEIGHT_SHAPES[n]
        Wd[n] = din(n, [a, r + 1, c])[:, 0:r, :]
    outT = nc.dram_tensor("outT", [D, L], F32, kind="ExternalOutput").ap()
    xpark = nc.dram_tensor("xpark", [128, 8 * L], F32, kind="Internal").ap()

    with contextlib.ExitStack() as st:
        P = Prog(nc, st)
        NW = KB(207.5)
        AR = st.enter_context(nc.sbuf_tensor("arena", [128, NW], F32))
        PS = st.enter_context(nc.psum_tensor("ps", [128, 8, 512], F32))

        def _shape(ap, shape):
            if len(shape) == 1:
                return ap
            if len(shape) == 2:
                return ap.rearrange("p (a b) -> p a b", a=shape[0])
            if len(shape) == 3:
                return ap.rearrange("p (a b c) -> p a b c", a=shape[0], b=shape[1])
            raise ValueError

        def f32v(off, shape):
            n = int(np.prod(shape))
            assert off + n <= NW, (off, n)
            return _shape(AR[:, off:off + n], shape)

        def bf16v(off, shape):
            n = int(np.prod(shape))
            assert n % 2 == 0 and off + n // 2 <= NW, (off, n)
            return _shape(AR[:, off:off + n // 2].bitcast(BF16), shape)

        rPS = regs(8)
        bank_ctr = [0]

        def nb():
            b = bank_ctr[0] % 6
            bank_ctr[0] += 1
            return b

        lbank_ctr = [0]

        def nbl():
            b = 6 + lbank_ctr[0] % 2
            lbank_ctr[0] += 1
            return b

        def PSb(b):
            return PS[:, b, :]

        def mm(out, lhsT, rhs, start, stop, reads, writes):
            P.op("pe", lambda e: e.matmul(out, lhsT=lhsT, rhs=rhs, start=start, stop=stop), reads, writes)

        def act(out, in_, func, reads, writes, eng="act", **kw):
            P.op(eng, lambda e: e.activation(out=out, in_=in_, func=func, **kw), reads, writes)

        def tt(eng, out, in0, in1, op, reads, writes):
            P.op(eng, lambda e: e.tensor_tensor(out=out, in0=in0, in1=in1, op=op), reads, writes)

        def ts(eng, out, in0, s1, s2, op0, op1, reads, writes):
            P.op(eng, lambda e: e.tensor_scalar(out=out, in0=in0, scalar1=s1, scalar2=s2, op0=op0, op1=op1), reads, writes)

        def stt(eng, out, in0, scalar, in1, op0, op1, reads, writes):
            P.op(eng, lambda e: e.scalar_tensor_tensor(out=out, in0=in0, scalar=scalar, in1=in1, op0=op0, op1=op1), reads, writes)

        def cp(eng, out, in_, reads, writes):
            P.op(eng, lambda e: e.tensor_copy(out=out, in_=in_), reads, writes)

        def memset(eng, ap, val, writes):
            P.op(eng, lambda e: e.memset(ap, val), (), writes)

        def recip(out, in_, reads, writes):
            P.op("dve", lambda e: e.reciprocal(out=out, in_=in_), reads, writes)

        o = KB(200)
        PV = f32v(o, [PR]); o += PR
        MODX = f32v(o, [2, 48]); o += 96
        MODC = f32v(o, [48]); o += 48
        DER = f32v(o, [2, 4, 8]); o += 64
        DERC = f32v(o, [8]); o += 8
        ONESF = f32v(o, [128]); o += 128
        o = (o + 15) // 16 * 16
        SC = bf16v(o, [8, 2]); o += 8
        ONESB = bf16v(o, [128]); o += 64
        IDN = bf16v(o, [128]); o += 64
        assert o <= NW
        rPV, rMODX, rMODC, rDER, rSC, rCONST = Reg(), regs(2), Reg(), regs(2), Reg(), Reg()

        def pv(name, j=0, n=1):
            return PV[:, PCOL[name] + j:PCOL[name] + j + n]

        d_misc = P.dsem("d_misc")
        d_x = P.dsem("d_x")
        d_out = P.dsem("d_out")

        X = f32v(0, [8, L])
        rX = regs(8, 4)
        A = bf16v(KB(64), [8, L])
        rA = regs(8, 4)

        def allr(rr):
            return [r for row in rr for r in row]

        P.dma("sp", d_misc, [(PV, pvec), (IDN, ident_d)], writes=[rPV, rCONST])
        P.dma("sp", d_x, [(X[:, c, :], xT[c * 128:(c + 1) * 128, :]) for c in range(8)], writes=allr(rX))
        memset("dve", ONESF, 1.0, [rCONST])
        memset("dve", ONESB, 1.0, [rCONST])
        act(SC[:, :, 0], pv("c", 0, 8), AF.Silu, [rPV], [rSC])
        act(SC[:, :, 1], pv("cc", 0, 8), AF.Silu, [rPV], [rSC])

        MWS = [bf16v(KB(96) + s * KB(4), [8, 256]) for s in range(2)]
        rMWS = regs(2)
        d_mws = [P.dsem("d_mw0"), P.dsem("d_mw1")]
        mod_state = {"slot": 0}

        def mod_groups(i, g0, g1, with_ctx):
            bank = nbl()
            psm = PSb(bank)[:, 0:96].rearrange("p (j v) -> p j v", v=2)
            for g in range(g0, g1):
                s = mod_state["slot"] % 2
                mod_state["slot"] += 1
                src = Wd["mod_w"][i, :, g * 256:(g + 1) * 256].rearrange("(kc p) n -> p kc n", p=128)
                P.dma("pool", d_mws[s], [(MWS[s], src)], writes=[rMWS[s]])
                for jj in range(2):
                    j = g * 2 + jj
                    for kc in range(8):
                        mm(psm[:, j, :], MWS[s][:, kc, jj * 128:(jj + 1) * 128], SC[:, kc, :], kc == 0, kc == 7,
                           [rMWS[s], rSC], [rPS[bank]])
            j0, j1 = g0 * 2, g1 * 2
            tt("dve", MODX[:, i, j0:j1], psm[:, j0:j1, 0], pv(f"modb{i}", j0, j1 - j0), ALU.add, [rPS[bank], rPV], [rMODX[i]])
            if with_ctx:
                tt("dve", MODC[:, j0:j1], psm[:, j0:j1, 1], pv(f"modb{i}", j0, j1 - j0), ALU.add, [rPS[bank], rPV], [rMODC])

        def mod_derive(i, part):
            if part == 1:
                stt("dve", DER[:, i, 0, :], MODX[:, i, 8:16], 1.0, pv(f"gmix{i}", 0, 8), ALU.add, ALU.mult, [rMODX[i], rPV], [rDER[i]])
                if i == 0:
                    stt("dve", DERC, MODC[:, 8:16], 1.0, pv("gmix0", 0, 8), ALU.add, ALU.mult, [rMODC, rPV], [rDER[i]])
            else:
                stt("dve", DER[:, i, 1, :], MODX[:, i, 32:40], 1.0, pv(f"gffn{i}", 0, 8), ALU.add, ALU.mult, [rMODX[i], rPV], [rDER[i]])
                if i == 1:
                    tt("dve", DER[:, i, 2, :], MODX[:, i, 16:24], pv("hbout", 0, 8), ALU.mult, [rMODX[i], rPV], [rDER[i]])

        def rmsnorm_fm(src, rsrc, nch, Dn, N, ntb, emit_out, tmp_off):
            SQ = [bf16v(tmp_off + s * 256, [512]) for s in range(2)]
            RS = f32v(tmp_off + 512, [512])
            TT = [f32v(tmp_off + 1024 + s * 512, [512]) for s in range(2)]
            rSQ, rRS, rTT = regs(2), Reg(), regs(2)
            for tb in range(ntb):
                bank = nb()
                for c in range(nch):
                    s = c % 2
                    act(SQ[s][:, :N], src(c, tb), AF.Square, rsrc(c, tb), [rSQ[s]])
                    mm(PSb(bank)[:, :N], ONESB, SQ[s][:, :N], c == 0, c == nch - 1, [rSQ[s], rCONST], [rPS[bank]])
                ts("dve", RS[:, :N], PSb(bank)[:, :N], 1.0 / Dn, EPS, ALU.mult, ALU.add, [rPS[bank]], [rRS])
                act(RS[:, :N], RS[:, :N], AF.Sqrt, [rRS], [rRS])
                recip(RS[:, :N], RS[:, :N], [rRS], [rRS])
                for c in range(nch):
                    s = c % 2
                    tt("dve", TT[s][:, :N], src(c, tb), RS[:, :N], ALU.mult, rsrc(c, tb) + [rRS], [rTT[s]])
                    emit_out(c, tb, TT[s][:, :N], rTT[s])

        dbg_done = [False]

        def dbg_dump_fm(name, tile_fn, regs_list):
            if stage != name:
                return
            P.barrier()
            stg = f32v(KB(196), [512])
            rS = Reg()
            for c in range(8):
                for tb in range(4):
                    cp("dve", stg, tile_fn(c, tb), regs_list(c, tb), [rS])
                    P.dma("sp", d_out, [(outT[c * 128:(c + 1) * 128, tb * 512:(tb + 1) * 512], stg)], reads=[rS])
            dbg_done[0] = True

        def dbg_blocks(name, blocks):
            if stage != name:
                return
            P.barrier()
            stg = f32v(KB(196), [512])
            rS = Reg()
            for row0, col0, src in blocks:
                p, n = src.shape[0], src.shape[1]
                bp = src.base_partition() if callable(getattr(src, "base_partition", None)) else 0
                cp("dve", stg[bp:bp + p, 0:n], src, [], [rS])
                P.dma("sp", d_out, [(outT[row0:row0 + p, col0:col0 + n], stg[bp:bp + p, 0:n])], reads=[rS])
            dbg_done[0] = True

        mod_groups(0, 0, 8, True)
        mod_derive(0, 1)

        TBL = f32v(KB(124), [2, L])
        rTBL = Reg()
        XC = f32v(KB(112), [8, LC])
        AC = bf16v(KB(120), [8, LC])
        rXC, rAC = regs(8), regs(8)
        P.dma("sp", d_misc, [(TBL[64:96], tbl_d[64:96])], writes=[rTBL])
        P.dma("sp", d_misc, [(XC[:, c, :], ctxT[c * 128:(c + 1) * 128, :]) for c in range(8)], writes=rXC)

        WDQ = bf16v(KB(170), [8, 512])
        WDKV = bf16v(KB(178), [8, 288])
        WKPE = bf16v(KB(182.5), [8, 96])
        WKROT = bf16v(KB(184), [8, 96])
        rWDQ, rWDKV, rWKPE, rWKROT = Reg(), Reg(), Reg(), Reg()
        d_w = P.dsem("d_w")
        P.dma("pool", d_w, [(WDQ, Wd["mla_w_dq"][0].rearrange("(kc p) n -> p kc n", p=128))], writes=[rWDQ])
        P.dma("pool", d_w, [(WDKV, Wd["mla_w_dkv"][0].rearrange("(kc p) n -> p kc n", p=128))], writes=[rWDKV])

        def build_rot(dst, src, eng, reads, writes):
            def neg(o, i):
                P.op(eng, lambda e: e.tensor_scalar_mul(out=o, in0=i, scalar1=-1.0), reads, writes)
            neg(dst[..., 0:8], src[..., 8:16])
            cp(eng, dst[..., 8:16], src[..., 0:8], reads, writes)
            neg(dst[..., 16:24], src[..., 24:32])
            cp(eng, dst[..., 24:32], src[..., 16:24], reads, writes)

        memset("dve", WKPE, 0.0, [rWKPE])
        memset("dve", WKROT, 0.0, [rWKROT])
        cp("dve", WKPE[:, :, 64:96], WDKV[:, :, 256:288], [rWDKV], [rWKPE])
        build_rot(WKROT[:, :, 64:96], WDKV[:, :, 256:288], "dve", [rWDKV], [rWKROT])

        NT = KB(104)

        def out_A(c, tb, T, rT):
            act(A[:, c, tb * 512:(tb + 1) * 512], T, AF.Identity, [rT, rDER[0], rMODX[0]], [rA[c][tb]],
                scale=DER[:, 0, 0, c:c + 1], bias=MODX[:, 0, c:c + 1])

        rmsnorm_fm(lambda c, tb: X[:, c, tb * 512:(tb + 1) * 512], lambda c, tb: [rX[c][tb]], 8, D, 512, 4, out_A, NT)

        def out_AC(c, tb, T, rT):
            act(AC[:, c, :], T, AF.Identity, [rT, rDER[0], rMODC], [rAC[c]], scale=DERC[:, c:c + 1], bias=MODC[:, c:c + 1])

        rmsnorm_fm(lambda c, tb: XC[:, c, :], lambda c, tb: [rXC[c]], 8, D, LC, 1, out_AC, NT)

        dbg_dump_fm("hx0", lambda c, tb: A[:, c, tb * 512:(tb + 1) * 512], lambda c, tb: [rA[c][tb]])

        CQN = bf16v(KB(140), [4, L])
        rCQN = regs(4)
        CKVN = bf16v(KB(156), [2, LK])
        rCKVN = regs(5)
        KT = bf16v(KB(165.5), [LK])
        rKT = Reg()
        T1 = f32v(KB(195.5), [512])
        T2 = f32v(KB(197.5), [512])
        rT1, rT2 = Reg(), Reg()

        def key_slice(kb):
            return (0, LC) if kb == 0 else (LC + (kb - 1) * 512, LC + kb * 512)

        if not dbg_done[0]:
            for tb in range(4):
                banks = []
                for m in range(4):
                    bk = nb()
                    banks.append(bk)
                    for kc in range(8):
                        mm(PSb(bk), WDQ[:, kc, m * 128:(m + 1) * 128], A[:, kc, tb * 512:(tb + 1) * 512], kc == 0, kc == 7,
                           [rWDQ, rA[kc][tb]], [rPS[bk]])

                def out_cq(c, tb_, T, rT, tb=tb):
                    act(CQN[:, c, tb * 512:(tb + 1) * 512], T, AF.Identity, [rT, rPV], [rCQN[c]], scale=pv("gq", c, 1), bias=0.0)

                rmsnorm_fm(lambda c, tb_, banks=banks: PSb(banks[c]), lambda c, tb_, banks=banks: [rPS[banks[c]]], 4, 512, 512, 1, out_cq, NT)

            for kb in range(5):
                k0, k1 = key_slice(kb)
                N = k1 - k0
                if kb == 0:
                    def rhs(kc):
                        return AC[:, kc, :], [rAC[kc]]
                else:
                    def rhs(kc, kb=kb):
                        return A[:, kc, (kb - 1) * 512:kb * 512], [rA[kc][kb - 1]]
                banks = []
                for m in range(2):
                    bk = nb()
                    banks.append(bk)
                    for kc in range(8):
                        r_ap, r_rg = rhs(kc)
                        mm(PSb(bk)[:, :N], WDKV[:, kc, m * 128:(m + 1) * 128], r_ap, kc == 0, kc == 7, [rWDKV] + r_rg, [rPS[bk]])
                bpe = nb()
                for kc in range(8):
                    r_ap, r_rg = rhs(kc)
                    mm(PSb(bpe)[0:96, :N], WKPE[:, kc, :], r_ap, kc == 0, kc == 7, [rWKPE] + r_rg, [rPS[bpe]])
                if kb > 0:
                    brot = nb()
                    for kc in range(8):
                        r_ap, r_rg = rhs(kc)
                        mm(PSb(brot)[0:96, :N], WKROT[:, kc, :], r_ap, kc == 0, kc == 7, [rWKROT] + r_rg, [rPS[brot]])
                    t0 = (kb - 1) * 512
                    tt("dve", T1[64:96, :], PSb(bpe)[64:96, :], TBL[64:96, 0, t0:t0 + 512], ALU.mult, [rPS[bpe], rTBL], [rT1])
                    tt("dve", T2[64:96, :], PSb(brot)[64:96, :], TBL[64:96, 1, t0:t0 + 512], ALU.mult, [rPS[brot], rTBL], [rT2])
                    tt("dve", KT[64:96, k0:k1], T1[64:96, :], T2[64:96, :], ALU.add, [rT1, rT2], [rKT])
                else:
                    cp("dve", KT[64:96, k0:k1], PSb(bpe)[64:96, :N], [rPS[bpe]], [rKT])

                def out_ckv(c, tb_, T, rT, k0=k0, k1=k1, kb=kb):
                    act(CKVN[:, c, k0:k1], T, AF.Identity, [rT, rPV], [rCKVN[kb]], scale=pv("gkv", c, 1), bias=0.0)

                rmsnorm_fm(lambda c, tb_, banks=banks, N=N: PSb(banks[c])[:, :N], lambda c, tb_, banks=banks: [rPS[banks[c]]],
                           2, 256, N, 1, out_ckv, NT)

            mod_groups(0, 8, 24, False)
            mod_derive(0, 2)
            P.barrier()

            blks = []
            for m in range(2):
                for tb in range(4):
                    blks.append((m * 128, tb * 512, CKVN[:, m, 256 + tb * 512:256 + (tb + 1) * 512]))
            for tb in range(4):
                blks.append((256 + 64, tb * 512, KT[64:96, 256 + tb * 512:256 + (tb + 1) * 512]))
            blks.append((384, 0, CKVN[:, 0, 0:256]))
            blks.append((384, 256, CKVN[:, 1, 0:256]))
            blks.append((384 + 64, 512, KT[64:96, 0:256]))
            blks.append((384, 768, AC[:, 0, :]))
            for m in range(4):
                for tb in range(4):
                    blks.append((512 + m * 128, tb * 512, CQN[:, m, tb * 512:(tb + 1) * 512]))
            dbg_blocks("kvdbg", blks)
        if not dbg_done[0]:
            WUQ = bf16v(KB(170), [4, 1536])
            WUK = bf16v(KB(182), [2, 1024])
            WUV = bf16v(KB(186), [2, 1024])
            ROT = [bf16v(KB(190) + s * KB(0.75), [4, 96]) for s in range(2)]
            QT = bf16v(KB(191.5), [L])
            VP = [bf16v(KB(104) + s * KB(4.5), [18, 128]) for s in range(2)]
            PT = [bf16v(KB(113) + s * KB(1), [512]) for s in range(3)]
            OA = f32v(KB(116), [512])
            RC = f32v(KB(118), [512])
            rWUQ, rWUK, rWUV, rROT, rQT, rVP, rPT, rOA, rRC = Reg(), Reg(), Reg(), regs(2), regs(4), regs(2), regs(3), Reg(), Reg()
            P.dma("pool", d_w, [(WUQ, Wd["mla_w_uq"][0].rearrange("(kc p) n -> p kc n", p=128))], writes=[rWUQ])
            P.dma("pool", d_w, [(WUK, Wd["mla_w_uk"][0].rearrange("(kc p) n -> p kc n", p=128))], writes=[rWUK])
            P.dma("pool", d_w, [(WUV, Wd["mla_w_uv"][0].rearrange("(kc p) n -> p kc n", p=128))], writes=[rWUV])
            for s in range(2):
                memset("dve", ROT[s], 0.0, [rROT[s]])
                memset("dve", VP[s], 0.0, [rVP[s]])
            memset("dve", VP[0][:, :, 64:65], 1.0, [rVP[0]])
            memset("dve", VP[1][:, :, 0:1], 1.0, [rVP[1]])
            WUQh = WUQ.rearrange("p k (h d) -> p k h d", h=NH)
            scale = 1.0 / math.sqrt(96.0)
            mod1_next = [0]

            for h in range(NH):
                par = h % 2
                po = 64 * par
                build_rot(ROT[par][:, :, 64:96], WUQh[:, :, h, 64:96], "dve", [rWUQ], [rROT[par]])
                for kb in range(5):
                    k0, k1 = key_slice(kb)
                    bk = nb()
                    for kc in range(2):
                        mm(PSb(bk)[0:64, :k1 - k0], WUK[:, kc, h * 64:(h + 1) * 64], CKVN[:, kc, k0:k1], kc == 0, kc == 1,
                           [rWUK, rCKVN[kb]], [rPS[bk]])
                    cp("dve", KT[0:64, k0:k1], PSb(bk)[0:64, :k1 - k0], [rPS[bk]], [rKT])
                for g in range(5):
                    kts = list(range(g * 4, min(18, g * 4 + 4)))
                    bk = nb()
                    for i, kt in enumerate(kts):
                        kb = 0 if kt < 2 else 1 + (kt - 2) // 4
                        for kc in range(2):
                            mm(PSb(bk)[:, i * 64:(i + 1) * 64], CKVN[:, kc, kt * 128:(kt + 1) * 128], WUV[:, kc, h * 64:(h + 1) * 64],
                               kc == 0, kc == 1, [rWUV, rCKVN[kb]], [rPS[bk]])
                    n = len(kts)
                    cp("dve", VP[par][:, kts[0]:kts[0] + n, po:po + 64],
                       PSb(bk)[:, 0:n * 64].rearrange("p (a b) -> p a b", a=n), [rPS[bk]], [rVP[par]])
                for qb in range(4):
                    q0 = qb * 512
                    bq, br = nb(), nb()
                    for kc in range(4):
                        mm(PSb(bq)[0:96, :], WUQh[:, kc, h, :], CQN[:, kc, q0:q0 + 512], kc == 0, kc == 3, [rWUQ, rCQN[kc]], [rPS[bq]])
                    for kc in range(4):
                        mm(PSb(br)[0:96, :], ROT[par][:, kc, :], CQN[:, kc, q0:q0 + 512], kc == 0, kc == 3, [rROT[par], rCQN[kc]], [rPS[br]])
                    act(QT[0:64, q0:q0 + 512], PSb(bq)[0:64, :], AF.Copy, [rPS[bq]], [rQT[qb]])
                    tt("dve", T1[64:96, :], PSb(bq)[64:96, :], TBL[64:96, 0, q0:q0 + 512], ALU.mult, [rPS[bq], rTBL], [rT1])
                    tt("dve", T2[64:96, :], PSb(br)[64:96, :], TBL[64:96, 1, q0:q0 + 512], ALU.mult, [rPS[br], rTBL], [rT2])
                    tt("dve", QT[64:96, q0:q0 + 512], T1[64:96, :], T2[64:96, :], ALU.add, [rT1, rT2], [rQT[qb]])
                    bo = nbl()
                    LA = 2
                    sb = {}
                    for it in range(18 + LA):
                        if it < 18:
                            kt = it
                            bs = nb()
                            sb[kt] = bs
                            mm(PSb(bs), KT[0:96, kt * 128:(kt + 1) * 128], QT[0:96, q0:q0 + 512], True, True, [rKT, rQT[qb]], [rPS[bs]])
                            act(PT[kt % 3], PSb(bs), AF.Exp, [rPS[bs]], [rPT[kt % 3]], scale=scale)
                        if it >= LA:
                            kt = it - LA
                            mm(PSb(bo), VP[par][:, kt, :], PT[kt % 3], kt == 0, kt == 17, [rVP[par], rPT[kt % 3]], [rPS[bo]])
                    srow = 64 - po
                    act(OA, PSb(bo), AF.Copy, [rPS[bo]], [rOA])
                    recip(RC[srow:srow + 1, :], OA[srow:srow + 1, :], [rOA], [rRC])
                    bb = nb()
                    mm(PSb(bb), ONESF[srow:srow + 1, :], RC[srow:srow + 1, :], True, True, [rRC, rCONST], [rPS[bb]])
                    tt("dve", A[po:po + 64, h // 2, q0:q0 + 512], OA[po:po + 64, :], PSb(bb)[po:po + 64, :], ALU.mult,
                       [rOA, rPS[bb]], [rA[h // 2][qb]])
                if mod1_next[0] < 24:
                    mod_groups(1, mod1_next[0], mod1_next[0] + 2, False)
                    mod1_next[0] += 2
            while mod1_next[0] < 24:
                mod_groups(1, mod1_next[0], mod1_next[0] + 2, False)
                mod1_next[0] += 2
            mod_derive(1, 1)
            mod_derive(1, 2)

            dbg_dump_fm("ox", lambda c, tb: A[:, c, tb * 512:(tb + 1) * 512], lambda c, tb: [rA[c][tb]])

        if not dbg_done[0]:
            P.barrier()
            WO = bf16v(KB(170), [8, 1024])
            rWO = Reg()
            P.dma("pool", d_w, [(WO, Wd["mla_w_o"][0].rearrange("(kc p) n -> p kc n", p=128))], writes=[rWO])
            for m in range(8):
                for tb in range(4):
                    bk = nb()
                    for kc in range(8):
                        mm(PSb(bk), WO[:, kc, m * 128:(m + 1) * 128], A[:, kc, tb * 512:(tb + 1) * 512], kc == 0, kc == 7,
                           [rWO, rA[kc][tb]], [rPS[bk]])
                    xs = X[:, m, tb * 512:(tb + 1) * 512]
                    stt("dve", xs, PSb(bk), MODX[:, 0, 16 + m:17 + m], xs, ALU.mult, ALU.add, [rPS[bk], rMODX[0], rX[m][tb]], [rX[m][tb]])
            dbg_dump_fm("xmix0", lambda c, tb: X[:, c, tb * 512:(tb + 1) * 512], lambda c, tb: [rX[c][tb]])

        def ffn(i, Xv, rXv, Av, rAv, Uo, To, Wo_):
            P.barrier()

            def out_Af(c, tb, T, rT):
                act(Av[:, c, tb * 512:(tb + 1) * 512], T, AF.Identity, [rT, rDER[i], rMODX[i]], [rAv[c][tb]],
                    scale=DER[:, i, 1, c:c + 1], bias=MODX[:, i, 24 + c:25 + c])

            rmsnorm_fm(lambda c, tb: Xv[:, c, tb * 512:(tb + 1) * 512], lambda c, tb: [rXv[c][tb]], 8, D, 512, 4, out_Af, To + KB(24.5))
            U = bf16v(Uo, [11, L])
            rU = regs(11)
            Z = f32v(To, [L + 2])
            YA = f32v(To + KB(8) + 16, [L])
            YG = f32v(To + KB(16) + 32, [L])
            rZ, rYA, rYG = Reg(), Reg(), Reg()
            WUP = [bf16v(Wo_ + s * KB(4), [8, 256]) for s in range(3)]
            rWUP = regs(3)
            d_wup = [P.dsem(f"d_wup{i}{s}") for s in range(3)]
            WDN = [bf16v(Wo_ + KB(12) + s * KB(2.75), [11, 128]) for s in range(2)]
            rWDN = regs(2)
            d_wdn = [P.dsem(f"d_wdn{i}{s}") for s in range(2)]
            memset("dve", Z[:, 0:1], 0.0, [rZ])
            memset("dve", Z[:, L + 1:L + 2], 0.0, [rZ])
            wup = Wd["ffn_w_up"][i]
            wdn = Wd["ffn_w_down"][i]
            pair = 0
            for hf in range(2):
                for jj in range(11):
                    j = hf * 11 + jj
                    s = pair % 3
                    pair += 1
                    P.dma("pool", d_wup[s], [
                        (WUP[s][:, :, 0:128], wup[:, j * 128:(j + 1) * 128].rearrange("(kc p) n -> p kc n", p=128)),
                        (WUP[s][:, :, 128:256], wup[:, DFF + j * 128:DFF + (j + 1) * 128].rearrange("(kc p) n -> p kc n", p=128)),
                    ], writes=[rWUP[s]])
                    for part, Y, rY in ((0, YA, rYA), (1, YG, rYG)):
                        ch = j if part == 0 else 22 + j
                        banks = []
                        for tb in range(4):
                            bk = nb()
                            banks.append(bk)
                            for kc in range(8):
                                mm(PSb(bk), WUP[s][:, kc, part * 128:(part + 1) * 128], Av[:, kc, tb * 512:(tb + 1) * 512],
                                   kc == 0, kc == 7, [rWUP[s], rAv[kc][tb]], [rPS[bk]])
                        w0 = pv(f"fcw{i}0", ch)
                        w1 = pv(f"fcw{i}1", ch)
                        w2 = pv(f"fcw{i}2", ch)
                        bb_ = pv(f"fcb{i}", ch)
                        for tb in range(4):
                            act(Z[:, 1 + tb * 512:1 + (tb + 1) * 512], PSb(banks[tb]), AF.Copy, [rPS[banks[tb]]], [rZ])
                        act(Y, Z[:, 1:L + 1], AF.Identity, [rZ, rPV], [rY], scale=w1, bias=bb_)
                        stt("dve", Y, Z[:, 0:L], w0, Y, ALU.mult, ALU.add, [rZ, rPV, rY], [rY])
                        stt("dve", Y, Z[:, 2:L + 2], w2, Y, ALU.mult, ALU.add, [rZ, rPV, rY], [rY])
                    act(YG, YG, AF.Silu, [rYG], [rYG])
                    tt("dve", U[:, jj, :], YG, YA, ALU.mult, [rYG, rYA], [rU[jj]])
                for m in range(8):
                    s = m % 2
                    P.dma("pool", d_wdn[s], [(WDN[s], wdn[hf * 1408:(hf + 1) * 1408, m * 128:(m + 1) * 128].rearrange("(kc p) n -> p kc n", p=128))],
                          writes=[rWDN[s]])
                    for tb in range(4):
                        bk = nb()
                        for kc in range(11):
                            mm(PSb(bk), WDN[s][:, kc, :], U[:, kc, tb * 512:(tb + 1) * 512], kc == 0, kc == 10, [rWDN[s], rU[kc]], [rPS[bk]])
                        xs = Xv[:, m, tb * 512:(tb + 1) * 512]
                        stt("dve", xs, PSb(bk), MODX[:, i, 40 + m:41 + m], xs, ALU.mult, ALU.add, [rPS[bk], rMODX[i], rXv[m][tb]], [rXv[m][tb]])

        if not dbg_done[0]:
            ffn(0, X, rX, A, rA, KB(96), KB(140), KB(173))
            dbg_dump_fm("xffn0", lambda c, tb: X[:, c, tb * 512:(tb + 1) * 512], lambda c, tb: [rX[c][tb]])

        X2 = f32v(KB(128), [8, L])
        rX2 = regs(8, 4)
        A2 = bf16v(KB(96), [8, L])
        rA2 = regs(8, 4)
        if not dbg_done[0]:
            hyena_ok = True
            P.barrier()

            def out_Ah(c, tb, T, rT):
                act(A[:, c, tb * 512:(tb + 1) * 512], T, AF.Identity, [rT, rDER[1], rMODX[1]], [rA[c][tb]],
                    scale=DER[:, 1, 0, c:c + 1], bias=MODX[:, 1, c:c + 1])

            rmsnorm_fm(lambda c, tb: X[:, c, tb * 512:(tb + 1) * 512], lambda c, tb: [rX[c][tb]], 8, D, 512, 4, out_Ah, KB(160))
            dbg_dump_fm("hx1", lambda c, tb: A[:, c, tb * 512:(tb + 1) * 512], lambda c, tb: [rA[c][tb]])
            P.barrier()

        if not dbg_done[0]:
            d_park = P.dsem("d_park")
            P.dma("sp", d_park, [(xpark, AR[:, 0:8 * L])], reads=allr(rX))

            X1 = bf16v(KB(96), [8, L])
            rX1 = regs(8, 4)
            VTOK = bf16v(KB(128), [16, D])
            rVTOK = regs(16)
            Z = f32v(KB(160), [L + 2])
            Y1 = f32v(KB(168) + 16, [L])
            Y2 = f32v(KB(176) + 32, [L])
            VT = [bf16v(KB(184) + 64 + s * KB(1), [512]) for s in range(2)]
            WIN = [bf16v(KB(186.5) + s * KB(6), [8, 384]) for s in range(2)]
            rZ, rY1, rY2, rVT, rWIN = Reg(), Reg(), Reg(), regs(2), regs(2)
            d_win = [P.dsem("d_win0"), P.dsem("d_win1")]
            memset("dve", Z[:, 0:1], 0.0, [rZ])
            memset("dve", Z[:, L + 1:L + 2], 0.0, [rZ])
            win = Wd["hy_w_in"][0]
            for j in range(8):
                s = j % 2
                P.dma("pool", d_win[s], [(WIN[s][:, :, k * 128:(k + 1) * 128],
                                          win[:, k * 1024 + j * 128:k * 1024 + (j + 1) * 128].rearrange("(kc p) n -> p kc n", p=128))
                                         for k in range(3)], writes=[rWIN[s]])
                for k, Y, rY in ((0, Y1, rY1), (1, Y1, rY1), (2, Y2, rY2)):
                    ch = k * 8 + j
                    banks = []
                    for tb in range(4):
                        bk = nb()
                        banks.append(bk)
                        for kc in range(8):
                            mm(PSb(bk), WIN[s][:, kc, k * 128:(k + 1) * 128], A[:, kc, tb * 512:(tb + 1) * 512], kc == 0, kc == 7,
                               [rWIN[s], rA[kc][tb]], [rPS[bk]])
                    for tb in range(4):
                        act(Z[:, 1 + tb * 512:1 + (tb + 1) * 512], PSb(banks[tb]), AF.Identity, [rPS[banks[tb]], rPV], [rZ],
                            scale=1.0, bias=pv("hbin", ch))
                    act(Y, Z[:, 1:L + 1], AF.Identity, [rZ, rPV], [rY], scale=pv("hcw1", ch), bias=pv("hcb", ch))
                    stt("dve", Y, Z[:, 0:L], pv("hcw0", ch), Y, ALU.mult, ALU.add, [rZ, rPV, rY], [rY])
                    if k == 0:
                        for tb in range(4):
                            sl = slice(tb * 512, (tb + 1) * 512)
                            stt("dve", X1[:, j, sl], Z[:, 2 + tb * 512:2 + (tb + 1) * 512], pv("hcw2", ch), Y[:, sl], ALU.mult, ALU.add,
                                [rZ, rPV, rY], [rX1[j][tb]])
                    else:
                        stt("dve", Y, Z[:, 2:L + 2], pv("hcw2", ch), Y, ALU.mult, ALU.add, [rZ, rPV, rY], [rY])
                for tb in range(4):
                    sl = slice(tb * 512, (tb + 1) * 512)
                    tt("dve", VT[tb % 2], Y2[:, sl], Y1[:, sl], ALU.mult, [rY1, rY2], [rVT[tb % 2]])
                    bk = nb()
                    pbf = PSb(bk).bitcast(BF16)
                    for q in range(4):
                        P.op("pe", lambda e, q=q, pbf=pbf, tb=tb: e.transpose(pbf[:, q * 128:(q + 1) * 128], VT[tb % 2][:, q * 128:(q + 1) * 128], IDN),
                             [rVT[tb % 2], rCONST], [rPS[bk]])
                    cp("dve", VTOK[:, tb * 4:tb * 4 + 4, j * 128:(j + 1) * 128], pbf[:, 0:512].rearrange("p (a b) -> p a b", a=4),
                       [rPS[bk]], [rVTOK[tb * 4 + q] for q in range(4)])
            dbg_dump_fm("hy_x1", lambda c, tb: X1[:, c, tb * 512:(tb + 1) * 512], lambda c, tb: [rX1[c][tb]])

        if not dbg_done[0]:
            P.barrier()
            H2 = bf16v(KB(160), [L])
            FW3 = bf16v(KB(164), [2048])
            ABSD = f32v(KB(168), [512])
            DBR = f32v(KB(170), [512])
            WINT = f32v(KB(172), [512])
            HTMP = f32v(KB(174), [512])
            ZT = f32v(KB(176), [L])
            H1 = f32v(KB(184), [L])
            TMPA = f32v(KB(192), [512])
            TMPB = f32v(KB(194), [512])
            FW1 = f32v(KB(196), [64])
            FW2 = f32v(KB(196.25), [64])
            NEGT = f32v(KB(196.5), [16])
            FB = f32v(KB(196.5) + 16, [2])
            rZT, rH1, rH2, rTA, rTB, rFW, rNEGT, rFB, rABSD, rDBR, rWINt, rHT = [Reg() for _ in range(12)]
            P.dma("sp", d_misc, [(ZT[0:33, :], zT_d), (FW1[0:33, :], Wd["hy_f_w1"][0]), (FW2[0:64, :], Wd["hy_f_w2"][0]),
                                 (NEGT, negt_d)], writes=[rZT, rFW, rNEGT])
            P.dma("pool", d_w, [(FW3[0:64, :], Wd["hy_f_w3"][0])], writes=[rFW])
            tt("dve", FB[0:64, 0:1], pv("fb1")[0:64, :], pv("ffr1")[0:64, :], ALU.mult, [rPV], [rFB])
            tt("dve", FB[0:64, 1:2], pv("fb2")[0:64, :], pv("ffr2")[0:64, :], ALU.mult, [rPV], [rFB])

            def sin_layer(lhsT, rhs_fn, rrhs, frq, fbcol, out_fn, rout):
                for tb in range(4):
                    sl = slice(tb * 512, (tb + 1) * 512)
                    bk = nb()
                    mm(PSb(bk)[0:64, :], lhsT, rhs_fn(sl), True, True, [rFW, rrhs], [rPS[bk]])
                    act(TMPA[0:64, :], PSb(bk)[0:64, :], AF.Identity, [rPS[bk], rPV, rFB], [rTA], scale=frq, bias=fbcol)
                    ts("dve", TMPB[0:64, :], TMPA[0:64, :], 1.0 / (2 * math.pi), MAGIC, ALU.mult, ALU.add, [rTA], [rTB])
                    ts("dve", TMPB[0:64, :], TMPB[0:64, :], MAGIC, -2.0 * math.pi, ALU.subtract, ALU.mult, [rTB], [rTB])
                    tt("dve", TMPA[0:64, :], TMPA[0:64, :], TMPB[0:64, :], ALU.add, [rTA, rTB], [rTA])
                    ts("dve", TMPA[0:64, :], TMPA[0:64, :], 3.1415925, -3.1415925, ALU.min, ALU.max, [rTA], [rTA])
                    act(out_fn(sl), TMPA[0:64, :], AF.Sin, [rTA], [rout])

            sin_layer(FW1[0:33, :], lambda sl: ZT[0:33, sl], rZT, pv("ffr1")[0:64, :], FB[0:64, 0:1], lambda sl: H1[0:64, sl], rH1)
            sin_layer(FW2[0:64, :], lambda sl: H1[0:64, sl], rH1, pv("ffr2")[0:64, :], FB[0:64, 1:2], lambda sl: H2[0:64, sl], rH2)
            P.barrier()

            HFB = bf16v(KB(64), [16, 2, 512])
            rHFB = regs(16)
            Y = bf16v(0, [32, D])
            rY = regs(32, 2)
            FWS = [bf16v(KB(176) + s * KB(4), [16, 128]) for s in range(2)]
            rFWS = regs(2)
            d_fw = [P.dsem("d_fw0"), P.dsem("d_fw1")]
            MT = [f32v(KB(184) + s * KB(2), [512]) for s in range(6)]
            rMT = regs(6)
            fwk = 0

            for cb in range(2):
                c0 = cb * 512
                P.dma("sp", d_misc, [(DBR[0:1, :], rows_d[1:2, c0:c0 + 512]),
                                     (ABSD, rows_d[0:1, c0:c0 + 512].to_broadcast([128, 512]))], writes=[rDBR, rABSD])
                act(ABSD, ABSD, AF.Abs, [rABSD], [rABSD])
                for st_ in range(16):
                    bk, bk2 = nb(), nb()
                    hs = H2[0:64, st_ * 128:(st_ + 1) * 128]
                    mm(PSb(bk), hs, FW3[0:64, c0:c0 + 512], True, True, [rH2, rFW], [rPS[bk]])
                    mm(PSb(bk2), hs, FW3[0:64, 1024 + c0:1024 + c0 + 512], True, True, [rH2, rFW], [rPS[bk2]])
                    act(WINT, ABSD, AF.Exp, [rABSD, rNEGT], [rWINt], scale=NEGT[:, st_:st_ + 1])
                    if st_ == 0:
                        tt("dve", HTMP, PSb(bk), WINT, ALU.mult, [rPS[bk], rWINt], [rHT])
                        tt("dve", HTMP[0:1, :], HTMP[0:1, :], DBR[0:1, :], ALU.add, [rHT, rDBR], [rHT])
                        cp("dve", HFB[:, st_, 0, :], HTMP, [rHT], [rHFB[st_]])
                        tt("dve", HFB[:, st_, 1, :], PSb(bk2), WINT, ALU.mult, [rPS[bk2], rWINt], [rHFB[st_]])
                        memset("dve", HFB[0:1, st_, 1, :], 0.0, [rHFB[st_]])
                    else:
                        tt("dve", HFB[:, st_, 0, :], PSb(bk), WINT, ALU.mult, [rPS[bk], rWINt], [rHFB[st_]])
                        tt("dve", HFB[:, st_, 1, :], PSb(bk2), WINT, ALU.mult, [rPS[bk2], rWINt], [rHFB[st_]])
                for c in range(16):
                    pb = {}
                    for part in range(2):
                        fm = part * 16 + c
                        s = fwk % 2
                        fwk += 1
                        P.dma("sp", d_fw[s], [(FWS[s].rearrange("p a b -> p (a b)"), fw_d[fm])], writes=[rFWS[s]])
                        bv, bf_, bb_ = nb(), nb(), nb()
                        pb[part] = (bv, bf_, bb_)
                        for tc in range(16):
                            mm(PSb(bv), FWS[s][:, tc, :], VTOK[:, tc, c0:c0 + 512], tc == 0, tc == 15, [rFWS[s], rVTOK[tc]], [rPS[bv]])
                        for tc in range(16):
                            mm(PSb(bf_), FWS[s][:, tc, :], HFB[:, tc, 0, :], tc == 0, tc == 15, [rFWS[s], rHFB[tc]], [rPS[bf_]])
                        for tc in range(16):
                            mm(PSb(bb_), FWS[s][:, tc, :], HFB[:, tc, 1, :], tc == 0, tc == 15, [rFWS[s], rHFB[tc]], [rPS[bb_]])
                    (vc, fc, bc), (vs, fs, bs) = pb[0], pb[1]
                    S1, S2, KRE, KIM, M1, M2 = MT
                    r1, r2, rkre, rkim, m1, m2 = rMT
                    act(S1, PSb(fc), AF.Copy, [rPS[fc]], [r1])
                    act(S2, PSb(fs), AF.Copy, [rPS[fs]], [r2])
                    tt("dve", KRE, PSb(bc), S1, ALU.add, [rPS[bc], r1], [rkre])
                    tt("dve", KIM, PSb(bs), S2, ALU.subtract, [rPS[bs], r2], [rkim])
                    tt("dve", M1, PSb(vc), KRE, ALU.mult, [rPS[vc], rkre], [m1])
                    tt("dve", M2, PSb(vs), KIM, ALU.mult, [rPS[vs], rkim], [m2])
                    tt("dve", Y[:, c, c0:c0 + 512], M1, M2, ALU.add, [m1, m2], [rY[c][cb]])
                    tt("dve", M1, PSb(vc), KIM, ALU.mult, [rPS[vc], rkim, m1], [m1])
                    tt("dve", M2, PSb(vs), KRE, ALU.mult, [rPS[vs], rkre, m2], [m2])
                    tt("dve", Y[:, 16 + c, c0:c0 + 512], M1, M2, ALU.subtract, [m1, m2], [rY[16 + c][cb]])

            P.barrier()
            P.dma("sp", d_x, [(AR[:, KB(128):KB(128) + 8 * L], xpark)], writes=allr(rX2))
            IVS = [bf16v(KB(64) + s * KB(16), [32, 256]) for s in range(2)]
            rIVS = regs(2)
            d_iv = [P.dsem("d_iv0"), P.dsem("d_iv1")]
            for tq in range(8):
                s = tq % 2
                P.dma("sp", d_iv[s], [(IVS[s].rearrange("p a b -> p (a b)"), iv_d[tq])], writes=[rIVS[s]])
                for chc in range(8):
                    bk = nb()
                    for kc in range(32):
                        mm(PSb(bk)[:, 0:256], Y[:, kc, chc * 128:(chc + 1) * 128], IVS[s][:, kc, :], kc == 0, kc == 31,
                           [rY[kc][chc // 4], rIVS[s]], [rPS[bk]])
                    xs = X1[:, chc, tq * 256:(tq + 1) * 256]
                    stt("dve", xs, PSb(bk)[:, 0:256], 1.0 / 2048.0, xs, ALU.mult, ALU.mult, [rPS[bk], rX1[chc][tq // 2]], [rX1[chc][tq // 2]])
            dbg_dump_fm("hy_y", lambda c, tb: X1[:, c, tb * 512:(tb + 1) * 512], lambda c, tb: [rX1[c][tb]])

        if not dbg_done[0]:
            P.barrier()
            WOUT = bf16v(KB(64), [8, 1024])
            rWOUT = Reg()
            TO = [f32v(KB(80) + s * KB(2), [512]) for s in range(2)]
            rTO = regs(2)
            P.dma("pool", d_w, [(WOUT, Wd["hy_w_out"][0].rearrange("(kc p) n -> p kc n", p=128))], writes=[rWOUT])
            k = 0
            for m in range(8):
                for tb in range(4):
                    bk = nb()
                    for kc in range(8):
                        mm(PSb(bk), WOUT[:, kc, m * 128:(m + 1) * 128], X1[:, kc, tb * 512:(tb + 1) * 512], kc == 0, kc == 7,
                           [rWOUT, rX1[kc][tb]], [rPS[bk]])
                    s = k % 2
                    k += 1
                    act(TO[s], PSb(bk), AF.Identity, [rPS[bk], rMODX[1], rDER[1]], [rTO[s]], scale=MODX[:, 1, 16 + m:17 + m], bias=DER[:, 1, 2, m:m + 1])
                    xs = X2[:, m, tb * 512:(tb + 1) * 512]
                    tt("dve", xs, xs, TO[s], ALU.add, [rTO[s], rX2[m][tb]], [rX2[m][tb]])
            dbg_dump_fm("xmix1", lambda c, tb: X2[:, c, tb * 512:(tb + 1) * 512], lambda c, tb: [rX2[c][tb]])

        if not dbg_done[0]:
            ffn(1, X2, rX2, A2, rA2, 0, KB(44), KB(77))
            dbg_dump_fm("xffn1", lambda c, tb: X2[:, c, tb * 512:(tb + 1) * 512], lambda c, tb: [rX2[c][tb]])

        if not dbg_done[0]:
            P.barrier()
            OS = [f32v(KB(16) + s * KB(2), [512]) for s in range(4)]
            rOS = regs(4)
            kk = [0]

            def out_final(c, tb, T, rT):
                s = kk[0] % 4
                kk[0] += 1
                act(OS[s], T, AF.Identity, [rT, rPV], [rOS[s]], scale=pv("gfin", c), bias=0.0)
                P.dma("sp", d_out, [(outT[c * 128:(c + 1) * 128, tb * 512:(tb + 1) * 512], OS[s])], reads=[rOS[s]])

            rmsnorm_fm(lambda c, tb: X2[:, c, tb * 512:(tb + 1) * 512], lambda c, tb: [rX2[c][tb]], 8, D, 512, 4, out_final, 0)

        P.finish()
    return nc


_NC_CACHE = {}


def _get_nc(stage="all"):
    if stage not in _NC_CACHE:
        _NC_CACHE[stage] = build(stage)
    return _NC_CACHE[stage]


def make_in_maps(inp):
    cst = _consts()
    shared = {n: np.ascontiguousarray(np.asarray(inp[n], np.float32)) for n in WEIGHT_NAMES}
    rows = np.ascontiguousarray(np.stack([np.asarray(inp["hy_decay"][0], np.float32), np.asarray(inp["hy_d_bias"][0], np.float32)]))
    big = dict(shared)
    big.update({k: cst[k] for k in ("tbl", "fw", "iv")})
    small = {k: cst[k] for k in cst if k not in big}
    maps = []
    for b in range(NCORES):
        m = dict(small)
        for k, v in big.items():
            tag = np.full(v.shape[:-2] + (1, v.shape[-1]), b, dtype=v.dtype)
            m[k] = np.concatenate([v, tag], axis=-2)
        m["rows"] = rows
        m["xT"] = np.ascontiguousarray(np.asarray(inp["x"][b], np.float32).T)
        m["ctxT"] = np.ascontiguousarray(np.asarray(inp["ctx"][b], np.float32).T)
        m["pvec"] = _pack_pvec(inp, b)
        maps.append(m)
    return maps


def kernel(**inputs):
    inp = {k: np.asarray(v) for k, v in inputs.items()}
    nc = _get_nc("all")
    res = run_bass_kernel_spmd(nc, make_in_maps(inp), core_ids=list(range(NCORES)))
    out = np.stack([np.ascontiguousarray(r["outT"].T) for r in res.results], axis=0)
    return out.astype(np.float32)
```

```python
import contextlib
import math
import numpy as np
import ml_dtypes
import concourse.bass as bass
import concourse.mybir as mybir
from concourse.bass_utils import run_bass_kernel_spmd

F32 = mybir.dt.float32
BF16 = mybir.dt.bfloat16
ALU = mybir.AluOpType
AF = mybir.ActivationFunctionType

D = 1024
L = 2048
LC = 256
LK = LC + L
NH = 16
DFF = 2816
EPS = 1e-6
NCORES = 8
MAGIC = 12582912.0


class Reg:
    __slots__ = ("w", "rd")

    def __init__(self):
        self.w = None
        self.rd = []


def regs(*shape):
    if len(shape) == 1:
        return [Reg() for _ in range(shape[0])]
    return [regs(*shape[1:]) for _ in range(shape[0])]


class DSem:
    def __init__(self, sem):
        self.sem = sem
        self.count = 0


class Prog:
    ENG = ("pe", "act", "dve", "pool", "sp")

    def __init__(self, nc, stack):
        self.nc = nc
        self.stack = stack
        self.ops = {e: [] for e in self.ENG}
        self.cnt = {e: 0 for e in self.ENG}
        self.seen = {e: {} for e in self.ENG}
        self.esem = {e: stack.enter_context(nc.semaphore("s_" + e)) for e in self.ENG}
        self.dsems = []

    def dsem(self, name):
        d = DSem(self.stack.enter_context(self.nc.semaphore(name)))
        self.dsems.append(d)
        return d

    def _deps(self, eng, reads, writes, skip_self):
        need = {}

        def add(ev):
            key, val = ev
            if skip_self and key == eng:
                return
            if self.seen[eng].get(key, 0) >= val:
                return
            if need.get(key, 0) < val:
                need[key] = val

        for r in reads:
            if r.w is not None:
                add(r.w)
        for w in writes:
            if w.w is not None:
                add(w.w)
            for ev in w.rd:
                add(ev)
        waits = []
        for key, val in need.items():
            self.seen[eng][key] = val
            waits.append((self.esem[key] if isinstance(key, str) else key.sem, val))
        return waits

    def _commit(self, ev, reads, writes):
        for r in reads:
            if len(r.rd) > 24:
                r.rd = r.rd[-24:] if False else r.rd
            r.rd.append(ev)
        for w in writes:
            w.w = ev
            w.rd = []

    def op(self, eng, fn, reads=(), writes=()):
        waits = self._deps(eng, reads, writes, eng == "pe")
        self.cnt[eng] += 1
        ev = (eng, self.cnt[eng])
        sem = self.esem[eng]

        def emit(e):
            for s, v in waits:
                e.wait_ge(s, v)
            fn(e).then_inc(sem, 1)

        self.ops[eng].append(emit)
        self._commit(ev, reads, writes)

    def dma(self, q, dsem, pairs, reads=(), writes=()):
        waits = self._deps(q, reads, writes, False)
        dsem.count += 16 * len(pairs)
        ev = (dsem, dsem.count)
        sem = dsem.sem

        def emit(e):
            for s, v in waits:
                e.wait_ge(s, v)
            for out, in_ in pairs:
                e.dma_start(out=out, in_=in_).then_inc(sem, 16)

        self.ops[q].append(emit)
        self._commit(ev, reads, writes)

    def barrier(self):
        for e in self.ENG:
            waits = []
            for e2 in self.ENG:
                v = self.cnt[e2]
                if v > self.seen[e].get(e2, 0):
                    self.seen[e][e2] = v
                    waits.append((self.esem[e2], v))
            for d in self.dsems:
                if d.count > self.seen[e].get(d, 0):
                    self.seen[e][d] = d.count
                    waits.append((d.sem, d.count))

            def emit(eh, waits=waits):
                for s, v in waits:
                    eh.wait_ge(s, v)

            self.ops[e].append(emit)

    def finish(self):
        self.barrier()
        ops = self.ops
        with self.nc.Block() as block:
            @block.tensor
            def _(e):
                for f in ops["pe"]:
                    f(e)

            @block.scalar
            def _(e):
                for f in ops["act"]:
                    f(e)

            @block.vector
            def _(e):
                for f in ops["dve"]:
                    f(e)

            @block.gpsimd
            def _(e):
                for f in ops["pool"]:
                    f(e)

            @block.sync
            def _(e):
                for f in ops["sp"]:
                    f(e)


def _layout():
    cols = {}
    off = [0]

    def add(name, n):
        cols[name] = off[0]
        off[0] += n

    add("c", 8)
    add("cc", 8)
    for i in range(2):
        add(f"modb{i}", 48)
    for i in range(2):
        add(f"gmix{i}", 8)
        add(f"gffn{i}", 8)
    add("gfin", 8)
    add("gq", 4)
    add("gkv", 2)
    add("hbin", 24)
    for j in range(3):
        add(f"hcw{j}", 24)
    add("hcb", 24)
    add("hbout", 8)
    for i in range(2):
        for j in range(3):
            add(f"fcw{i}{j}", 44)
        add(f"fcb{i}", 44)
    for n in ("fb1", "ffr1", "fb2", "ffr2"):
        add(n, 1)
    return cols, off[0]


PCOL, PR = _layout()


def _colsof(v):
    v = np.asarray(v, np.float32).reshape(-1)
    if v.size < 128:
        v = np.concatenate([v, np.zeros(128 - v.size, np.float32)])
    return v.reshape(-1, 128).T


def _pack_pvec(inp, b):
    pv = np.zeros((128, PR), np.float32)

    def put(name, v):
        c = _colsof(v)
        pv[:, PCOL[name]:PCOL[name] + c.shape[1]] = c

    put("c", inp["c"][b])
    put("cc", inp["c_ctx"])
    for i in range(2):
        put(f"modb{i}", inp["mod_b"][i])
        put(f"gmix{i}", inp["norm_mix_g"][i])
        put(f"gffn{i}", inp["norm_ffn_g"][i])
    put("gfin", inp["final_g"])
    put("gq", inp["mla_g_q"][0])
    put("gkv", inp["mla_g_kv"][0])
    put("hbin", inp["hy_b_in"][0])
    for j in range(3):
        put(f"hcw{j}", inp["hy_conv_w"][0, j])
    put("hcb", inp["hy_conv_b"][0])
    put("hbout", inp["hy_b_out"][0])
    for i in range(2):
        for j in range(3):
            put(f"fcw{i}{j}", inp["ffn_conv_w"][i, j])
        put(f"fcb{i}", inp["ffn_conv_b"][i])
    put("fb1", inp["hy_f_b1"][0])
    put("ffr1", inp["hy_f_freq1"][0])
    put("fb2", inp["hy_f_b2"][0])
    put("ffr2", inp["hy_f_freq2"][0])
    return pv


_CONST = {}


def _consts():
    if _CONST:
        return _CONST
    f32 = np.float32
    rows = L // 64
    row = np.repeat(np.arange(rows, dtype=f32), 64)
    col = np.tile(np.arange(64, dtype=f32), rows)
    half = 16
    inv = (f32(10000.0) ** (-np.arange(0, half, 2, dtype=f32) / f32(half))).astype(f32)
    ar = row[:, None] * inv
    ac = col[:, None] * inv
    ang = np.concatenate([ar, ar, ac, ac], axis=-1).astype(f32)
    tbl = np.zeros((128, 2, L), f32)
    tbl[64:96, 0, :] = np.cos(ang).T
    tbl[64:96, 1, :] = np.sin(ang).T
    t = np.linspace(0.0, 1.0, L, dtype=f32)
    w = (f32(2.0 * math.pi) * np.arange(L, dtype=f32) / f32(L)).astype(f32)
    fr = np.linspace(1e-4, 15, 16, dtype=f32)
    z = np.concatenate([t[:, None], np.cos(w[:, None] * fr), -np.sin(w[:, None] * fr)], axis=-1).astype(f32)
    zT = np.ascontiguousarray(z.T)
    negt = np.ascontiguousarray((-t).reshape(16, 128).T)
    tt = np.arange(L, dtype=np.float64)
    ff = np.arange(L, dtype=np.float64) + 0.5
    th = 2.0 * np.pi * np.outer(tt, ff) / 4096.0
    C = np.cos(th)
    S = np.sin(th)
    fwd = np.concatenate([C, S], axis=1)
    FWm = fwd.reshape(16, 128, 32, 128).transpose(2, 1, 0, 3)
    FWm = np.ascontiguousarray(FWm).reshape(32, 128, 2048).astype(ml_dtypes.bfloat16)
    inv_m = np.concatenate([C.T, -S.T], axis=0)
    IVm = inv_m.reshape(32, 128, 8, 256).transpose(2, 1, 0, 3)
    IVm = np.ascontiguousarray(IVm).reshape(8, 128, 32 * 256).astype(ml_dtypes.bfloat16)
    ident = np.eye(128, dtype=f32).astype(ml_dtypes.bfloat16)
    _CONST.update(tbl=tbl, zT=zT, negt=negt, fw=FWm, iv=IVm, ident=ident)
    return _CONST


WEIGHT_NAMES = ["mod_w", "mla_w_dq", "mla_w_uq", "mla_w_dkv", "mla_w_uk", "mla_w_uv", "mla_w_o",
                "hy_w_in", "hy_f_w1", "hy_f_w2", "hy_f_w3", "hy_w_out", "ffn_w_up", "ffn_w_down"]
WEIGHT_SHAPES = {
    "mod_w": (2, 1024, 6144), "mla_w_dq": (1, 1024, 512), "mla_w_uq": (1, 512, 1536),
    "mla_w_dkv": (1, 1024, 288), "mla_w_uk": (1, 256, 1024), "mla_w_uv": (1, 256, 1024),
    "mla_w_o": (1, 1024, 1024), "hy_w_in": (1, 1024, 3072), "hy_f_w1": (1, 33, 64),
    "hy_f_w2": (1, 64, 64), "hy_f_w3": (1, 64, 2048), "hy_w_out": (1, 1024, 1024),
    "ffn_w_up": (2, 1024, 5632), "ffn_w_down": (2, 2816, 1024),
}


def KB(x):
    return int(round(x * 256))


def build(stage="all"):
    nc = bass.Bass("TRN2", target_bir_lowering=False)

    def din(name, shape, dt=F32):
        return nc.dram_tensor(name, list(shape), dt, kind="ExternalInput").ap()

    xT = din("xT", [D, L])
    ctxT = din("ctxT", [D, LC])
    pvec = din("pvec", [128, PR])
    rows_d = din("rows", [2, D])
    tbl_d = din("tbl", [128, 3, L])[:, 0:2, :]
    zT_d = din("zT", [33, L])
    negt_d = din("negt", [128, 16])
    fw_d = din("fw", [32, 129, 2048], BF16)[:, 0:128, :]
    iv_d = din("iv", [8, 129, 32 * 256], BF16)[:, 0:128, :]
    ident_d = din("ident", [128, 128], BF16)
    Wd = {}
    for n in WEIGHT_NAMES:
        a, r, c = WEIGHT_SHAPES[n]
        Wd[n] = din(n, [a, r + 1, c])[:, 0:r, :]
    outT = nc.dram_tensor("outT", [D, L], F32, kind="ExternalOutput").ap()
    xpark = nc.dram_tensor("xpark", [128, 8 * L], F32, kind="Internal").ap()

    with contextlib.ExitStack() as st:
        P = Prog(nc, st)
        NW = KB(207.5)
        AR = st.enter_context(nc.sbuf_tensor("arena", [128, NW], F32))
        PS = st.enter_context(nc.psum_tensor("ps", [128, 8, 512], F32))

        def _shape(ap, shape):
            if len(shape) == 1:
                return ap
            if len(shape) == 2:
                return ap.rearrange("p (a b) -> p a b", a=shape[0])
            if len(shape) == 3:
                return ap.rearrange("p (a b c) -> p a b c", a=shape[0], b=shape[1])
            raise ValueError

        def f32v(off, shape):
            n = int(np.prod(shape))
            assert off + n <= NW, (off, n)
            return _shape(AR[:, off:off + n], shape)

        def bf16v(off, shape):
            n = int(np.prod(shape))
            assert n % 2 == 0 and off + n // 2 <= NW, (off, n)
            return _shape(AR[:, off:off + n // 2].bitcast(BF16), shape)

        rPS = regs(8)
        bank_ctr = [0]

        def nb():
            b = bank_ctr[0] % 6
            bank_ctr[0] += 1
            return b

        lbank_ctr = [0]

        def nbl():
            b = 6 + lbank_ctr[0] % 2
            lbank_ctr[0] += 1
            return b

        def PSb(b):
            return PS[:, b, :]

        def mm(out, lhsT, rhs, start, stop, reads, writes):
            P.op("pe", lambda e: e.matmul(out, lhsT=lhsT, rhs=rhs, start=start, stop=stop), reads, writes)

        def act(out, in_, func, reads, writes, eng="act", **kw):
            P.op(eng, lambda e: e.activation(out=out, in_=in_, func=func, **kw), reads, writes)

        def tt(eng, out, in0, in1, op, reads, writes):
            P.op(eng, lambda e: e.tensor_tensor(out=out, in0=in0, in1=in1, op=op), reads, writes)

        def ts(eng, out, in0, s1, s2, op0, op1, reads, writes):
            P.op(eng, lambda e: e.tensor_scalar(out=out, in0=in0, scalar1=s1, scalar2=s2, op0=op0, op1=op1), reads, writes)

        def stt(eng, out, in0, scalar, in1, op0, op1, reads, writes):
            P.op(eng, lambda e: e.scalar_tensor_tensor(out=out, in0=in0, scalar=scalar, in1=in1, op0=op0, op1=op1), reads, writes)

        def cp(eng, out, in_, reads, writes):
            P.op(eng, lambda e: e.tensor_copy(out=out, in_=in_), reads, writes)

        def memset(eng, ap, val, writes):
            P.op(eng, lambda e: e.memset(ap, val), (), writes)

        def recip(out, in_, reads, writes):
            P.op("dve", lambda e: e.reciprocal(out=out, in_=in_), reads, writes)

        o = KB(200)
        PV = f32v(o, [PR]); o += PR
        MODX = f32v(o, [2, 48]); o += 96
        MODC = f32v(o, [48]); o += 48
        DER = f32v(o, [2, 4, 8]); o += 64
        DERC = f32v(o, [8]); o += 8
        ONESF = f32v(o, [128]); o += 128
        o = (o + 15) // 16 * 16
        SC = bf16v(o, [8, 2]); o += 8
        ONESB = bf16v(o, [128]); o += 64
        IDN = bf16v(o, [128]); o += 64
        assert o <= NW
        rPV, rMODX, rMODC, rDER, rSC, rCONST = Reg(), regs(2), Reg(), regs(2), Reg(), Reg()

        def pv(name, j=0, n=1):
            return PV[:, PCOL[name] + j:PCOL[name] + j + n]

        d_misc = P.dsem("d_misc")
        d_x = P.dsem("d_x")
        d_out = P.dsem("d_out")

        X = f32v(0, [8, L])
        rX = regs(8, 4)
        A = bf16v(KB(64), [8, L])
        rA = regs(8, 4)

        def allr(rr):
            return [r for row in rr for r in row]

        P.dma("sp", d_misc, [(PV, pvec), (IDN, ident_d)], writes=[rPV, rCONST])
        P.dma("sp", d_x, [(X[:, c, :], xT[c * 128:(c + 1) * 128, :]) for c in range(8)], writes=allr(rX))
        memset("dve", ONESF, 1.0, [rCONST])
        memset("dve", ONESB, 1.0, [rCONST])
        act(SC[:, :, 0], pv("c", 0, 8), AF.Silu, [rPV], [rSC])
        act(SC[:, :, 1], pv("cc", 0, 8), AF.Silu, [rPV], [rSC])

        MWS = [bf16v(KB(96) + s * KB(4), [8, 256]) for s in range(2)]
        rMWS = regs(2)
        d_mws = [P.dsem("d_mw0"), P.dsem("d_mw1")]
        mod_state = {"slot": 0}

        def mod_groups(i, g0, g1, with_ctx):
            bank = nbl()
            psm = PSb(bank)[:, 0:96].rearrange("p (j v) -> p j v", v=2)
            for g in range(g0, g1):
                s = mod_state["slot"] % 2
                mod_state["slot"] += 1
                src = Wd["mod_w"][i, :, g * 256:(g + 1) * 256].rearrange("(kc p) n -> p kc n", p=128)
                P.dma("pool", d_mws[s], [(MWS[s], src)], writes=[rMWS[s]])
                for jj in range(2):
                    j = g * 2 + jj
                    for kc in range(8):
                        mm(psm[:, j, :], MWS[s][:, kc, jj * 128:(jj + 1) * 128], SC[:, kc, :], kc == 0, kc == 7,
                           [rMWS[s], rSC], [rPS[bank]])
            j0, j1 = g0 * 2, g1 * 2
            tt("dve", MODX[:, i, j0:j1], psm[:, j0:j1, 0], pv(f"modb{i}", j0, j1 - j0), ALU.add, [rPS[bank], rPV], [rMODX[i]])
            if with_ctx:
                tt("dve", MODC[:, j0:j1], psm[:, j0:j1, 1], pv(f"modb{i}", j0, j1 - j0), ALU.add, [rPS[bank], rPV], [rMODC])

        def mod_derive(i, part):
            if part == 1:
                stt("dve", DER[:, i, 0, :], MODX[:, i, 8:16], 1.0, pv(f"gmix{i}", 0, 8), ALU.add, ALU.mult, [rMODX[i], rPV], [rDER[i]])
                if i == 0:
                    stt("dve", DERC, MODC[:, 8:16], 1.0, pv("gmix0", 0, 8), ALU.add, ALU.mult, [rMODC, rPV], [rDER[i]])
            else:
                stt("dve", DER[:, i, 1, :], MODX[:, i, 32:40], 1.0, pv(f"gffn{i}", 0, 8), ALU.add, ALU.mult, [rMODX[i], rPV], [rDER[i]])
                if i == 1:
                    tt("dve", DER[:, i, 2, :], MODX[:, i, 16:24], pv("hbout", 0, 8), ALU.mult, [rMODX[i], rPV], [rDER[i]])

        def rmsnorm_fm(src, rsrc, nch, Dn, N, ntb, emit_out, tmp_off):
            SQ = [bf16v(tmp_off + s * 256, [512]) for s in range(2)]
            RS = f32v(tmp_off + 512, [512])
            TT = [f32v(tmp_off + 1024 + s * 512, [512]) for s in range(2)]
            rSQ, rRS, rTT = regs(2), Reg(), regs(2)
            for tb in range(ntb):
                bank = nb()
                for c in range(nch):
                    s = c % 2
                    act(SQ[s][:, :N], src(c, tb), AF.Square, rsrc(c, tb), [rSQ[s]])
                    mm(PSb(bank)[:, :N], ONESB, SQ[s][:, :N], c == 0, c == nch - 1, [rSQ[s], rCONST], [rPS[bank]])
                ts("dve", RS[:, :N], PSb(bank)[:, :N], 1.0 / Dn, EPS, ALU.mult, ALU.add, [rPS[bank]], [rRS])
                act(RS[:, :N], RS[:, :N], AF.Sqrt, [rRS], [rRS])
                recip(RS[:, :N], RS[:, :N], [rRS], [rRS])
                for c in range(nch):
                    s = c % 2
                    tt("dve", TT[s][:, :N], src(c, tb), RS[:, :N], ALU.mult, rsrc(c, tb) + [rRS], [rTT[s]])
                    emit_out(c, tb, TT[s][:, :N], rTT[s])

        dbg_done = [False]

        def dbg_dump_fm(name, tile_fn, regs_list):
            if stage != name:
                return
            P.barrier()
            stg = f32v(KB(196), [512])
            rS = Reg()
            for c in range(8):
                for tb in range(4):
                    cp("dve", stg, tile_fn(c, tb), regs_list(c, tb), [rS])
                    P.dma("sp", d_out, [(outT[c * 128:(c + 1) * 128, tb * 512:(tb + 1) * 512], stg)], reads=[rS])
            dbg_done[0] = True

        def dbg_blocks(name, blocks):
            if stage != name:
                return
            P.barrier()
            stg = f32v(KB(196), [512])
            rS = Reg()
            for row0, col0, src in blocks:
                p, n = src.shape[0], src.shape[1]
                bp = src.base_partition() if callable(getattr(src, "base_partition", None)) else 0
                cp("dve", stg[bp:bp + p, 0:n], src, [], [rS])
                P.dma("sp", d_out, [(outT[row0:row0 + p, col0:col0 + n], stg[bp:bp + p, 0:n])], reads=[rS])
            dbg_done[0] = True

        mod_groups(0, 0, 8, True)
        mod_derive(0, 1)

        TBL = f32v(KB(124), [2, L])
        rTBL = Reg()
        XC = f32v(KB(112), [8, LC])
        AC = bf16v(KB(120), [8, LC])
        rXC, rAC = regs(8), regs(8)
        P.dma("sp", d_misc, [(TBL[64:96], tbl_d[64:96])], writes=[rTBL])
        P.dma("sp", d_misc, [(XC[:, c, :], ctxT[c * 128:(c + 1) * 128, :]) for c in range(8)], writes=rXC)

        WDQ = bf16v(KB(170), [8, 512])
        WDKV = bf16v(KB(178), [8, 288])
        WKPE = bf16v(KB(182.5), [8, 96])
        WKROT = bf16v(KB(184), [8, 96])
        rWDQ, rWDKV, rWKPE, rWKROT = Reg(), Reg(), Reg(), Reg()
        d_w = P.dsem("d_w")
        P.dma("pool", d_w, [(WDQ, Wd["mla_w_dq"][0].rearrange("(kc p) n -> p kc n", p=128))], writes=[rWDQ])
        P.dma("pool", d_w, [(WDKV, Wd["mla_w_dkv"][0].rearrange("(kc p) n -> p kc n", p=128))], writes=[rWDKV])

        def build_rot(dst, src, eng, reads, writes):
            def neg(o, i):
                P.op(eng, lambda e: e.tensor_scalar_mul(out=o, in0=i, scalar1=-1.0), reads, writes)
            neg(dst[..., 0:8], src[..., 8:16])
            cp(eng, dst[..., 8:16], src[..., 0:8], reads, writes)
            neg(dst[..., 16:24], src[..., 24:32])
            cp(eng, dst[..., 24:32], src[..., 16:24], reads, writes)

        memset("dve", WKPE, 0.0, [rWKPE])
        memset("dve", WKROT, 0.0, [rWKROT])
        cp("dve", WKPE[:, :, 64:96], WDKV[:, :, 256:288], [rWDKV], [rWKPE])
        build_rot(WKROT[:, :, 64:96], WDKV[:, :, 256:288], "dve", [rWDKV], [rWKROT])

        NT = KB(104)

        def out_A(c, tb, T, rT):
            act(A[:, c, tb * 512:(tb + 1) * 512], T, AF.Identity, [rT, rDER[0], rMODX[0]], [rA[c][tb]],
                scale=DER[:, 0, 0, c:c + 1], bias=MODX[:, 0, c:c + 1])

        rmsnorm_fm(lambda c, tb: X[:, c, tb * 512:(tb + 1) * 512], lambda c, tb: [rX[c][tb]], 8, D, 512, 4, out_A, NT)

        def out_AC(c, tb, T, rT):
            act(AC[:, c, :], T, AF.Identity, [rT, rDER[0], rMODC], [rAC[c]], scale=DERC[:, c:c + 1], bias=MODC[:, c:c + 1])

        rmsnorm_fm(lambda c, tb: XC[:, c, :], lambda c, tb: [rXC[c]], 8, D, LC, 1, out_AC, NT)

        dbg_dump_fm("hx0", lambda c, tb: A[:, c, tb * 512:(tb + 1) * 512], lambda c, tb: [rA[c][tb]])

        CQN = bf16v(KB(140), [4, L])
        rCQN = regs(4)
        CKVN = bf16v(KB(156), [2, LK])
        rCKVN = regs(5)
        KT = bf16v(KB(165.5), [LK])
        rKT = Reg()
        T1 = f32v(KB(195.5), [512])
        T2 = f32v(KB(197.5), [512])
        rT1, rT2 = Reg(), Reg()

        def key_slice(kb):
            return (0, LC) if kb == 0 else (LC + (kb - 1) * 512, LC + kb * 512)

        if not dbg_done[0]:
            for tb in range(4):
                banks = []
                for m in range(4):
                    bk = nb()
                    banks.append(bk)
                    for kc in range(8):
                        mm(PSb(bk), WDQ[:, kc, m * 128:(m + 1) * 128], A[:, kc, tb * 512:(tb + 1) * 512], kc == 0, kc == 7,
                           [rWDQ, rA[kc][tb]], [rPS[bk]])

                def out_cq(c, tb_, T, rT, tb=tb):
                    act(CQN[:, c, tb * 512:(tb + 1) * 512], T, AF.Identity, [rT, rPV], [rCQN[c]], scale=pv("gq", c, 1), bias=0.0)

                rmsnorm_fm(lambda c, tb_, banks=banks: PSb(banks[c]), lambda c, tb_, banks=banks: [rPS[banks[c]]], 4, 512, 512, 1, out_cq, NT)

            for kb in range(5):
                k0, k1 = key_slice(kb)
                N = k1 - k0
                if kb == 0:
                    def rhs(kc):
                        return AC[:, kc, :], [rAC[kc]]
                else:
                    def rhs(kc, kb=kb):
                        return A[:, kc, (kb - 1) * 512:kb * 512], [rA[kc][kb - 1]]
                banks = []
                for m in range(2):
                    bk = nb()
                    banks.append(bk)
                    for kc in range(8):
                        r_ap, r_rg = rhs(kc)
                        mm(PSb(bk)[:, :N], WDKV[:, kc, m * 128:(m + 1) * 128], r_ap, kc == 0, kc == 7, [rWDKV] + r_rg, [rPS[bk]])
                bpe = nb()
                for kc in range(8):
                    r_ap, r_rg = rhs(kc)
                    mm(PSb(bpe)[0:96, :N], WKPE[:, kc, :], r_ap, kc == 0, kc == 7, [rWKPE] + r_rg, [rPS[bpe]])
                if kb > 0:
                    brot = nb()
                    for kc in range(8):
                        r_ap, r_rg = rhs(kc)
                        mm(PSb(brot)[0:96, :N], WKROT[:, kc, :], r_ap, kc == 0, kc == 7, [rWKROT] + r_rg, [rPS[brot]])
                    t0 = (kb - 1) * 512
                    tt("dve", T1[64:96, :], PSb(bpe)[64:96, :], TBL[64:96, 0, t0:t0 + 512], ALU.mult, [rPS[bpe], rTBL], [rT1])
                    tt("dve", T2[64:96, :], PSb(brot)[64:96, :], TBL[64:96, 1, t0:t0 + 512], ALU.mult, [rPS[brot], rTBL], [rT2])
                    tt("dve", KT[64:96, k0:k1], T1[64:96, :], T2[64:96, :], ALU.add, [rT1, rT2], [rKT])
                else:
                    cp("dve", KT[64:96, k0:k1], PSb(bpe)[64:96, :N], [rPS[bpe]], [rKT])

                def out_ckv(c, tb_, T, rT, k0=k0, k1=k1, kb=kb):
                    act(CKVN[:, c, k0:k1], T, AF.Identity, [rT, rPV], [rCKVN[kb]], scale=pv("gkv", c, 1), bias=0.0)

                rmsnorm_fm(lambda c, tb_, banks=banks, N=N: PSb(banks[c])[:, :N], lambda c, tb_, banks=banks: [rPS[banks[c]]],
                           2, 256, N, 1, out_ckv, NT)

            mod_groups(0, 8, 24, False)
            mod_derive(0, 2)
            P.barrier()

            blks = []
            for m in range(2):
                for tb in range(4):
                    blks.append((m * 128, tb * 512, CKVN[:, m, 256 + tb * 512:256 + (tb + 1) * 512]))
            for tb in range(4):
                blks.append((256 + 64, tb * 512, KT[64:96, 256 + tb * 512:256 + (tb + 1) * 512]))
            blks.append((384, 0, CKVN[:, 0, 0:256]))
            blks.append((384, 256, CKVN[:, 1, 0:256]))
            blks.append((384 + 64, 512, KT[64:96, 0:256]))
            blks.append((384, 768, AC[:, 0, :]))
            for m in range(4):
                for tb in range(4):
                    blks.append((512 + m * 128, tb * 512, CQN[:, m, tb * 512:(tb + 1) * 512]))
            dbg_blocks("kvdbg", blks)
        if not dbg_done[0]:
            WUQ = bf16v(KB(170), [4, 1536])
            WUK = bf16v(KB(182), [2, 1024])
            WUV = bf16v(KB(186), [2, 1024])
            ROT = [bf16v(KB(190) + s * KB(0.75), [4, 96]) for s in range(2)]
            QT = bf16v(KB(191.5), [L])
            VP = [bf16v(KB(104) + s * KB(4.5), [18, 128]) for s in range(2)]
            PT = [bf16v(KB(113) + s * KB(1), [512]) for s in range(3)]
            OA = f32v(KB(116), [512])
            RC = f32v(KB(118), [512])
            rWUQ, rWUK, rWUV, rROT, rQT, rVP, rPT, rOA, rRC = Reg(), Reg(), Reg(), regs(2), regs(4), regs(2), regs(3), Reg(), Reg()
            P.dma("pool", d_w, [(WUQ, Wd["mla_w_uq"][0].rearrange("(kc p) n -> p kc n", p=128))], writes=[rWUQ])
            P.dma("pool", d_w, [(WUK, Wd["mla_w_uk"][0].rearrange("(kc p) n -> p kc n", p=128))], writes=[rWUK])
            P.dma("pool", d_w, [(WUV, Wd["mla_w_uv"][0].rearrange("(kc p) n -> p kc n", p=128))], writes=[rWUV])
            for s in range(2):
                memset("dve", ROT[s], 0.0, [rROT[s]])
                memset("dve", VP[s], 0.0, [rVP[s]])
            memset("dve", VP[0][:, :, 64:65], 1.0, [rVP[0]])
            memset("dve", VP[1][:, :, 0:1], 1.0, [rVP[1]])
            WUQh = WUQ.rearrange("p k (h d) -> p k h d", h=NH)
            scale = 1.0 / math.sqrt(96.0)
            mod1_next = [0]

            for h in range(NH):
                par = h % 2
                po = 64 * par
                build_rot(ROT[par][:, :, 64:96], WUQh[:, :, h, 64:96], "dve", [rWUQ], [rROT[par]])
                for kb in range(5):
                    k0, k1 = key_slice(kb)
                    bk = nb()
                    for kc in range(2):
                        mm(PSb(bk)[0:64, :k1 - k0], WUK[:, kc, h * 64:(h + 1) * 64], CKVN[:, kc, k0:k1], kc == 0, kc == 1,
                           [rWUK, rCKVN[kb]], [rPS[bk]])
                    cp("dve", KT[0:64, k0:k1], PSb(bk)[0:64, :k1 - k0], [rPS[bk]], [rKT])
                for g in range(5):
                    kts = list(range(g * 4, min(18, g * 4 + 4)))
                    bk = nb()
                    for i, kt in enumerate(kts):
                        kb = 0 if kt < 2 else 1 + (kt - 2) // 4
                        for kc in range(2):
                            mm(PSb(bk)[:, i * 64:(i + 1) * 64], CKVN[:, kc, kt * 128:(kt + 1) * 128], WUV[:, kc, h * 64:(h + 1) * 64],
                               kc == 0, kc == 1, [rWUV, rCKVN[kb]], [rPS[bk]])
                    n = len(kts)
                    cp("dve", VP[par][:, kts[0]:kts[0] + n, po:po + 64],
                       PSb(bk)[:, 0:n * 64].rearrange("p (a b) -> p a b", a=n), [rPS[bk]], [rVP[par]])
                for qb in range(4):
                    q0 = qb * 512
                    bq, br = nb(), nb()
                    for kc in range(4):
                        mm(PSb(bq)[0:96, :], WUQh[:, kc, h, :], CQN[:, kc, q0:q0 + 512], kc == 0, kc == 3, [rWUQ, rCQN[kc]], [rPS[bq]])
                    for kc in range(4):
                        mm(PSb(br)[0:96, :], ROT[par][:, kc, :], CQN[:, kc, q0:q0 + 512], kc == 0, kc == 3, [rROT[par], rCQN[kc]], [rPS[br]])
                    act(QT[0:64, q0:q0 + 512], PSb(bq)[0:64, :], AF.Copy, [rPS[bq]], [rQT[qb]])
                    tt("dve", T1[64:96, :], PSb(bq)[64:96, :], TBL[64:96, 0, q0:q0 + 512], ALU.mult, [rPS[bq], rTBL], [rT1])
                    tt("dve", T2[64:96, :], PSb(br)[64:96, :], TBL[64:96, 1, q0:q0 + 512], ALU.mult, [rPS[br], rTBL], [rT2])
                    tt("dve", QT[64:96, q0:q0 + 512], T1[64:96, :], T2[64:96, :], ALU.add, [rT1, rT2], [rQT[qb]])
                    bo = nbl()
                    LA = 2
                    sb = {}
                    for it in range(18 + LA):
                        if it < 18:
                            kt = it
                            bs = nb()
                            sb[kt] = bs
                            mm(PSb(bs), KT[0:96, kt * 128:(kt + 1) * 128], QT[0:96, q0:q0 + 512], True, True, [rKT, rQT[qb]], [rPS[bs]])
                            act(PT[kt % 3], PSb(bs), AF.Exp, [rPS[bs]], [rPT[kt % 3]], scale=scale)
                        if it >= LA:
                            kt = it - LA
                            mm(PSb(bo), VP[par][:, kt, :], PT[kt % 3], kt == 0, kt == 17, [rVP[par], rPT[kt % 3]], [rPS[bo]])
                    srow = 64 - po
                    act(OA, PSb(bo), AF.Copy, [rPS[bo]], [rOA])
                    recip(RC[srow:srow + 1, :], OA[srow:srow + 1, :], [rOA], [rRC])
                    bb = nb()
                    mm(PSb(bb), ONESF[srow:srow + 1, :], RC[srow:srow + 1, :], True, True, [rRC, rCONST], [rPS[bb]])
                    tt("dve", A[po:po + 64, h // 2, q0:q0 + 512], OA[po:po + 64, :], PSb(bb)[po:po + 64, :], ALU.mult,
                       [rOA, rPS[bb]], [rA[h // 2][qb]])
                if mod1_next[0] < 24:
                    mod_groups(1, mod1_next[0], mod1_next[0] + 2, False)
                    mod1_next[0] += 2
            while mod1_next[0] < 24:
                mod_groups(1, mod1_next[0], mod1_next[0] + 2, False)
                mod1_next[0] += 2
            mod_derive(1, 1)
            mod_derive(1, 2)

            dbg_dump_fm("ox", lambda c, tb: A[:, c, tb * 512:(tb + 1) * 512], lambda c, tb: [rA[c][tb]])

        if not dbg_done[0]:
            P.barrier()
            WO = bf16v(KB(170), [8, 1024])
            rWO = Reg()
            P.dma("pool", d_w, [(WO, Wd["mla_w_o"][0].rearrange("(kc p) n -> p kc n", p=128))], writes=[rWO])
            for m in range(8):
                for tb in range(4):
                    bk = nb()
                    for kc in range(8):
                        mm(PSb(bk), WO[:, kc, m * 128:(m + 1) * 128], A[:, kc, tb * 512:(tb + 1) * 512], kc == 0, kc == 7,
                           [rWO, rA[kc][tb]], [rPS[bk]])
                    xs = X[:, m, tb * 512:(tb + 1) * 512]
                    stt("dve", xs, PSb(bk), MODX[:, 0, 16 + m:17 + m], xs, ALU.mult, ALU.add, [rPS[bk], rMODX[0], rX[m][tb]], [rX[m][tb]])
            dbg_dump_fm("xmix0", lambda c, tb: X[:, c, tb * 512:(tb + 1) * 512], lambda c, tb: [rX[c][tb]])

        def ffn(i, Xv, rXv, Av, rAv, Uo, To, Wo_):
            P.barrier()

            def out_Af(c, tb, T, rT):
                act(Av[:, c, tb * 512:(tb + 1) * 512], T, AF.Identity, [rT, rDER[i], rMODX[i]], [rAv[c][tb]],
                    scale=DER[:, i, 1, c:c + 1], bias=MODX[:, i, 24 + c:25 + c])

            rmsnorm_fm(lambda c, tb: Xv[:, c, tb * 512:(tb + 1) * 512], lambda c, tb: [rXv[c][tb]], 8, D, 512, 4, out_Af, To + KB(24.5))
            U = bf16v(Uo, [11, L])
            rU = regs(11)
            Z = f32v(To, [L + 2])
            YA = f32v(To + KB(8) + 16, [L])
            YG = f32v(To + KB(16) + 32, [L])
            rZ, rYA, rYG = Reg(), Reg(), Reg()
            WUP = [bf16v(Wo_ + s * KB(4), [8, 256]) for s in range(3)]
            rWUP = regs(3)
            d_wup = [P.dsem(f"d_wup{i}{s}") for s in range(3)]
            WDN = [bf16v(Wo_ + KB(12) + s * KB(2.75), [11, 128]) for s in range(2)]
            rWDN = regs(2)
            d_wdn = [P.dsem(f"d_wdn{i}{s}") for s in range(2)]
            memset("dve", Z[:, 0:1], 0.0, [rZ])
            memset("dve", Z[:, L + 1:L + 2], 0.0, [rZ])
            wup = Wd["ffn_w_up"][i]
            wdn = Wd["ffn_w_down"][i]
            pair = 0
            for hf in range(2):
                for jj in range(11):
                    j = hf * 11 + jj
                    s = pair % 3
                    pair += 1
                    P.dma("pool", d_wup[s], [
                        (WUP[s][:, :, 0:128], wup[:, j * 128:(j + 1) * 128].rearrange("(kc p) n -> p kc n", p=128)),
                        (WUP[s][:, :, 128:256], wup[:, DFF + j * 128:DFF + (j + 1) * 128].rearrange("(kc p) n -> p kc n", p=128)),
                    ], writes=[rWUP[s]])
                    for part, Y, rY in ((0, YA, rYA), (1, YG, rYG)):
                        ch = j if part == 0 else 22 + j
                        banks = []
                        for tb in range(4):
                            bk = nb()
                            banks.append(bk)
                            for kc in range(8):
                                mm(PSb(bk), WUP[s][:, kc, part * 128:(part + 1) * 128], Av[:, kc, tb * 512:(tb + 1) * 512],
                                   kc == 0, kc == 7, [rWUP[s], rAv[kc][tb]], [rPS[bk]])
                        w0 = pv(f"fcw{i}0", ch)
                        w1 = pv(f"fcw{i}1", ch)
                        w2 = pv(f"fcw{i}2", ch)
                        bb_ = pv(f"fcb{i}", ch)
                        for tb in range(4):
                            act(Z[:, 1 + tb * 512:1 + (tb + 1) * 512], PSb(banks[tb]), AF.Copy, [rPS[banks[tb]]], [rZ])
                        act(Y, Z[:, 1:L + 1], AF.Identity, [rZ, rPV], [rY], scale=w1, bias=bb_)
                        stt("dve", Y, Z[:, 0:L], w0, Y, ALU.mult, ALU.add, [rZ, rPV, rY], [rY])
                        stt("dve", Y, Z[:, 2:L + 2], w2, Y, ALU.mult, ALU.add, [rZ, rPV, rY], [rY])
                    act(YG, YG, AF.Silu, [rYG], [rYG])
                    tt("dve", U[:, jj, :], YG, YA, ALU.mult, [rYG, rYA], [rU[jj]])
                for m in range(8):
                    s = m % 2
                    P.dma("pool", d_wdn[s], [(WDN[s], wdn[hf * 1408:(hf + 1) * 1408, m * 128:(m + 1) * 128].rearrange("(kc p) n -> p kc n", p=128))],
                          writes=[rWDN[s]])
                    for tb in range(4):
                        bk = nb()
                        for kc in range(11):
                            mm(PSb(bk), WDN[s][:, kc, :], U[:, kc, tb * 512:(tb + 1) * 512], kc == 0, kc == 10, [rWDN[s], rU[kc]], [rPS[bk]])
                        xs = Xv[:, m, tb * 512:(tb + 1) * 512]
                        stt("dve", xs, PSb(bk), MODX[:, i, 40 + m:41 + m], xs, ALU.mult, ALU.add, [rPS[bk], rMODX[i], rXv[m][tb]], [rXv[m][tb]])

        if not dbg_done[0]:
            ffn(0, X, rX, A, rA, KB(96), KB(140), KB(173))
            dbg_dump_fm("xffn0", lambda c, tb: X[:, c, tb * 512:(tb + 1) * 512], lambda c, tb: [rX[c][tb]])

        X2 = f32v(KB(128), [8, L])
        rX2 = regs(8, 4)
        A2 = bf16v(KB(96), [8, L])
        rA2 = regs(8, 4)
        if not dbg_done[0]:
            hyena_ok = True
            P.barrier()

            def out_Ah(c, tb, T, rT):
                act(A[:, c, tb * 512:(tb + 1) * 512], T, AF.Identity, [rT, rDER[1], rMODX[1]], [rA[c][tb]],
                    scale=DER[:, 1, 0, c:c + 1], bias=MODX[:, 1, c:c + 1])

            rmsnorm_fm(lambda c, tb: X[:, c, tb * 512:(tb + 1) * 512], lambda c, tb: [rX[c][tb]], 8, D, 512, 4, out_Ah, KB(160))
            dbg_dump_fm("hx1", lambda c, tb: A[:, c, tb * 512:(tb + 1) * 512], lambda c, tb: [rA[c][tb]])
            P.barrier()

        if not dbg_done[0]:
            d_park = P.dsem("d_park")
            P.dma("sp", d_park, [(xpark[:, c * L:(c + 1) * L], X[:, c, :]) for c in range(8)], reads=allr(rX))

            X1 = bf16v(KB(96), [8, L])
            rX1 = regs(8, 4)
            VTOK = bf16v(KB(128), [16, D])
            rVTOK = regs(16)
            Z = f32v(KB(160), [L + 2])
            Y1 = f32v(KB(168) + 16, [L])
            Y2 = f32v(KB(176) + 32, [L])
            VT = [bf16v(KB(184) + 64 + s * KB(1), [512]) for s in range(2)]
            WIN = [bf16v(KB(186.5) + s * KB(6), [8, 384]) for s in range(2)]
            rZ, rY1, rY2, rVT, rWIN = Reg(), Reg(), Reg(), regs(2), regs(2)
            d_win = [P.dsem("d_win0"), P.dsem("d_win1")]
            memset("dve", Z[:, 0:1], 0.0, [rZ])
            memset("dve", Z[:, L + 1:L + 2], 0.0, [rZ])
            win = Wd["hy_w_in"][0]
            for j in range(8):
                s = j % 2
                P.dma("pool", d_win[s], [(WIN[s][:, :, k * 128:(k + 1) * 128],
                                          win[:, k * 1024 + j * 128:k * 1024 + (j + 1) * 128].rearrange("(kc p) n -> p kc n", p=128))
                                         for k in range(3)], writes=[rWIN[s]])
                for k, Y, rY in ((0, Y1, rY1), (1, Y1, rY1), (2, Y2, rY2)):
                    ch = k * 8 + j
                    banks = []
                    for tb in range(4):
                        bk = nb()
                        banks.append(bk)
                        for kc in range(8):
                            mm(PSb(bk), WIN[s][:, kc, k * 128:(k + 1) * 128], A[:, kc, tb * 512:(tb + 1) * 512], kc == 0, kc == 7,
                               [rWIN[s], rA[kc][tb]], [rPS[bk]])
                    for tb in range(4):
                        act(Z[:, 1 + tb * 512:1 + (tb + 1) * 512], PSb(banks[tb]), AF.Identity, [rPS[banks[tb]], rPV], [rZ],
                            scale=1.0, bias=pv("hbin", ch))
                    act(Y, Z[:, 1:L + 1], AF.Identity, [rZ, rPV], [rY], scale=pv("hcw1", ch), bias=pv("hcb", ch))
                    stt("dve", Y, Z[:, 0:L], pv("hcw0", ch), Y, ALU.mult, ALU.add, [rZ, rPV, rY], [rY])
                    if k == 0:
                        for tb in range(4):
                            sl = slice(tb * 512, (tb + 1) * 512)
                            stt("dve", X1[:, j, sl], Z[:, 2 + tb * 512:2 + (tb + 1) * 512], pv("hcw2", ch), Y[:, sl], ALU.mult, ALU.add,
                                [rZ, rPV, rY], [rX1[j][tb]])
                    else:
                        stt("dve", Y, Z[:, 2:L + 2], pv("hcw2", ch), Y, ALU.mult, ALU.add, [rZ, rPV, rY], [rY])
                for tb in range(4):
                    sl = slice(tb * 512, (tb + 1) * 512)
                    tt("dve", VT[tb % 2], Y2[:, sl], Y1[:, sl], ALU.mult, [rY1, rY2], [rVT[tb % 2]])
                    bk = nb()
                    pbf = PSb(bk).bitcast(BF16)
                    for q in range(4):
                        P.op("pe", lambda e, q=q, pbf=pbf, tb=tb: e.transpose(pbf[:, q * 128:(q + 1) * 128], VT[tb % 2][:, q * 128:(q + 1) * 128], IDN),
                             [rVT[tb % 2], rCONST], [rPS[bk]])
                    cp("dve", VTOK[:, tb * 4:tb * 4 + 4, j * 128:(j + 1) * 128], pbf[:, 0:512].rearrange("p (a b) -> p a b", a=4),
                       [rPS[bk]], [rVTOK[tb * 4 + q] for q in range(4)])
            dbg_dump_fm("hy_x1", lambda c, tb: X1[:, c, tb * 512:(tb + 1) * 512], lambda c, tb: [rX1[c][tb]])

        if not dbg_done[0]:
            P.barrier()
            H2 = bf16v(KB(160), [L])
            FW3 = bf16v(KB(164), [2048])
            ABSD = f32v(KB(168), [512])
            DBR = f32v(KB(170), [512])
            WINT = f32v(KB(172), [512])
            HTMP = f32v(KB(174), [512])
            ZT = f32v(KB(176), [L])
            H1 = f32v(KB(184), [L])
            TMPA = f32v(KB(192), [512])
            TMPB = f32v(KB(194), [512])
            FW1 = f32v(KB(196), [64])
            FW2 = f32v(KB(196.25), [64])
            NEGT = f32v(KB(196.5), [16])
            FB = f32v(KB(196.5) + 16, [2])
            rZT, rH1, rH2, rTA, rTB, rFW, rNEGT, rFB, rABSD, rDBR, rWINt, rHT = [Reg() for _ in range(12)]
            P.dma("sp", d_misc, [(ZT[0:33, :], zT_d), (FW1[0:33, :], Wd["hy_f_w1"][0]), (FW2[0:64, :], Wd["hy_f_w2"][0]),
                                 (NEGT, negt_d)], writes=[rZT, rFW, rNEGT])
            P.dma("pool", d_w, [(FW3[0:64, :], Wd["hy_f_w3"][0])], writes=[rFW])
            tt("dve", FB[0:64, 0:1], pv("fb1")[0:64, :], pv("ffr1")[0:64, :], ALU.mult, [rPV], [rFB])
            tt("dve", FB[0:64, 1:2], pv("fb2")[0:64, :], pv("ffr2")[0:64, :], ALU.mult, [rPV], [rFB])

            def sin_layer(lhsT, rhs_fn, rrhs, frq, fbcol, out_fn, rout):
                for tb in range(4):
                    sl = slice(tb * 512, (tb + 1) * 512)
                    bk = nb()
                    mm(PSb(bk)[0:64, :], lhsT, rhs_fn(sl), True, True, [rFW, rrhs], [rPS[bk]])
                    act(TMPA[0:64, :], PSb(bk)[0:64, :], AF.Identity, [rPS[bk], rPV, rFB], [rTA], scale=frq, bias=fbcol)
                    ts("dve", TMPB[0:64, :], TMPA[0:64, :], 1.0 / (2 * math.pi), MAGIC, ALU.mult, ALU.add, [rTA], [rTB])
                    ts("dve", TMPB[0:64, :], TMPB[0:64, :], MAGIC, -2.0 * math.pi, ALU.subtract, ALU.mult, [rTB], [rTB])
                    tt("dve", TMPA[0:64, :], TMPA[0:64, :], TMPB[0:64, :], ALU.add, [rTA, rTB], [rTA])
                    ts("dve", TMPA[0:64, :], TMPA[0:64, :], 3.1415925, -3.1415925, ALU.min, ALU.max, [rTA], [rTA])
                    act(out_fn(sl), TMPA[0:64, :], AF.Sin, [rTA], [rout])

            sin_layer(FW1[0:33, :], lambda sl: ZT[0:33, sl], rZT, pv("ffr1")[0:64, :], FB[0:64, 0:1], lambda sl: H1[0:64, sl], rH1)
            sin_layer(FW2[0:64, :], lambda sl: H1[0:64, sl], rH1, pv("ffr2")[0:64, :], FB[0:64, 1:2], lambda sl: H2[0:64, sl], rH2)
            P.barrier()

            HFB = bf16v(KB(64), [16, 2, 512])
            rHFB = regs(16)
            Y = bf16v(0, [32, D])
            rY = regs(32, 2)
            FWS = [bf16v(KB(176) + s * KB(4), [16, 128]) for s in range(2)]
            rFWS = regs(2)
            d_fw = [P.dsem("d_fw0"), P.dsem("d_fw1")]
            MT = [f32v(KB(184) + s * KB(2), [512]) for s in range(6)]
            rMT = regs(6)
            fwk = 0

            for cb in range(2):
                c0 = cb * 512
                P.dma("sp", d_misc, [(DBR[0:1, :], rows_d[1:2, c0:c0 + 512]),
                                     (ABSD, rows_d[0:1, c0:c0 + 512].to_broadcast([128, 512]))], writes=[rDBR, rABSD])
                act(ABSD, ABSD, AF.Abs, [rABSD], [rABSD])
                for st_ in range(16):
                    bk, bk2 = nb(), nb()
                    hs = H2[0:64, st_ * 128:(st_ + 1) * 128]
                    mm(PSb(bk), hs, FW3[0:64, c0:c0 + 512], True, True, [rH2, rFW], [rPS[bk]])
                    mm(PSb(bk2), hs, FW3[0:64, 1024 + c0:1024 + c0 + 512], True, True, [rH2, rFW], [rPS[bk2]])
                    act(WINT, ABSD, AF.Exp, [rABSD, rNEGT], [rWINt], scale=NEGT[:, st_:st_ + 1])
                    if st_ == 0:
                        tt("dve", HTMP, PSb(bk), WINT, ALU.mult, [rPS[bk], rWINt], [rHT])
                        tt("dve", HTMP[0:1, :], HTMP[0:1, :], DBR[0:1, :], ALU.add, [rHT, rDBR], [rHT])
                        cp("dve", HFB[:, st_, 0, :], HTMP, [rHT], [rHFB[st_]])
                        tt("dve", HFB[:, st_, 1, :], PSb(bk2), WINT, ALU.mult, [rPS[bk2], rWINt], [rHFB[st_]])
                        memset("dve", HFB[0:1, st_, 1, :], 0.0, [rHFB[st_]])
                    else:
                        tt("dve", HFB[:, st_, 0, :], PSb(bk), WINT, ALU.mult, [rPS[bk], rWINt], [rHFB[st_]])
                        tt("dve", HFB[:, st_, 1, :], PSb(bk2), WINT, ALU.mult, [rPS[bk2], rWINt], [rHFB[st_]])
                for c in range(16):
                    pb = {}
                    for part in range(2):
                        fm = part * 16 + c
                        s = fwk % 2
                        fwk += 1
                        P.dma("sp", d_fw[s], [(FWS[s].rearrange("p a b -> p (a b)"), fw_d[fm])], writes=[rFWS[s]])
                        bv, bf_, bb_ = nb(), nb(), nb()
                        pb[part] = (bv, bf_, bb_)
                        for tc in range(16):
                            mm(PSb(bv), FWS[s][:, tc, :], VTOK[:, tc, c0:c0 + 512], tc == 0, tc == 15, [rFWS[s], rVTOK[tc]], [rPS[bv]])
                        for tc in range(16):
                            mm(PSb(bf_), FWS[s][:, tc, :], HFB[:, tc, 0, :], tc == 0, tc == 15, [rFWS[s], rHFB[tc]], [rPS[bf_]])
                        for tc in range(16):
                            mm(PSb(bb_), FWS[s][:, tc, :], HFB[:, tc, 1, :], tc == 0, tc == 15, [rFWS[s], rHFB[tc]], [rPS[bb_]])
                    (vc, fc, bc), (vs, fs, bs) = pb[0], pb[1]
                    S1, S2, KRE, KIM, M1, M2 = MT
                    r1, r2, rkre, rkim, m1, m2 = rMT
                    act(S1, PSb(fc), AF.Copy, [rPS[fc]], [r1])
                    act(S2, PSb(fs), AF.Copy, [rPS[fs]], [r2])
                    tt("dve", KRE, PSb(bc), S1, ALU.add, [rPS[bc], r1], [rkre])
                    tt("dve", KIM, PSb(bs), S2, ALU.subtract, [rPS[bs], r2], [rkim])
                    tt("dve", M1, PSb(vc), KRE, ALU.mult, [rPS[vc], rkre], [m1])
                    tt("dve", M2, PSb(vs), KIM, ALU.mult, [rPS[vs], rkim], [m2])
                    tt("dve", Y[:, c, c0:c0 + 512], M1, M2, ALU.add, [m1, m2], [rY[c][cb]])
                    tt("dve", M1, PSb(vc), KIM, ALU.mult, [rPS[vc], rkim, m1], [m1])
                    tt("dve", M2, PSb(vs), KRE, ALU.mult, [rPS[vs], rkre, m2], [m2])
                    tt("dve", Y[:, 16 + c, c0:c0 + 512], M1, M2, ALU.subtract, [m1, m2], [rY[16 + c][cb]])

            P.barrier()
            P.dma("sp", d_x, [(X2[:, c, :], xpark[:, c * L:(c + 1) * L]) for c in range(8)], writes=allr(rX2))
            IVS = [bf16v(KB(64) + s * KB(16), [32, 256]) for s in range(2)]
            rIVS = regs(2)
            d_iv = [P.dsem("d_iv0"), P.dsem("d_iv1")]
            for tq in range(8):
                s = tq % 2
                P.dma("sp", d_iv[s], [(IVS[s].rearrange("p a b -> p (a b)"), iv_d[tq])], writes=[rIVS[s]])
                for chc in range(8):
                    bk = nb()
                    for kc in range(32):
                        mm(PSb(bk)[:, 0:256], Y[:, kc, chc * 128:(chc + 1) * 128], IVS[s][:, kc, :], kc == 0, kc == 31,
                           [rY[kc][chc // 4], rIVS[s]], [rPS[bk]])
                    xs = X1[:, chc, tq * 256:(tq + 1) * 256]
                    stt("dve", xs, PSb(bk)[:, 0:256], 1.0 / 2048.0, xs, ALU.mult, ALU.mult, [rPS[bk], rX1[chc][tq // 2]], [rX1[chc][tq // 2]])
            dbg_dump_fm("hy_y", lambda c, tb: X1[:, c, tb * 512:(tb + 1) * 512], lambda c, tb: [rX1[c][tb]])

        if not dbg_done[0]:
            P.barrier()
            WOUT = bf16v(KB(64), [8, 1024])
            rWOUT = Reg()
            TO = [f32v(KB(80) + s * KB(2), [512]) for s in range(2)]
            rTO = regs(2)
            P.dma("pool", d_w, [(WOUT, Wd["hy_w_out"][0].rearrange("(kc p) n -> p kc n", p=128))], writes=[rWOUT])
            k = 0
            for m in range(8):
                for tb in range(4):
                    bk = nb()
                    for kc in range(8):
                        mm(PSb(bk), WOUT[:, kc, m * 128:(m + 1) * 128], X1[:, kc, tb * 512:(tb + 1) * 512], kc == 0, kc == 7,
                           [rWOUT, rX1[kc][tb]], [rPS[bk]])
                    s = k % 2
                    k += 1
                    act(TO[s], PSb(bk), AF.Identity, [rPS[bk], rMODX[1], rDER[1]], [rTO[s]], scale=MODX[:, 1, 16 + m:17 + m], bias=DER[:, 1, 2, m:m + 1])
                    xs = X2[:, m, tb * 512:(tb + 1) * 512]
                    tt("dve", xs, xs, TO[s], ALU.add, [rTO[s], rX2[m][tb]], [rX2[m][tb]])
            dbg_dump_fm("xmix1", lambda c, tb: X2[:, c, tb * 512:(tb + 1) * 512], lambda c, tb: [rX2[c][tb]])

        if not dbg_done[0]:
            ffn(1, X2, rX2, A2, rA2, 0, KB(44), KB(77))
            dbg_dump_fm("xffn1", lambda c, tb: X2[:, c, tb * 512:(tb + 1) * 512], lambda c, tb: [rX2[c][tb]])

        if not dbg_done[0]:
            P.barrier()
            OS = [f32v(KB(16) + s * KB(2), [512]) for s in range(4)]
            rOS = regs(4)
            kk = [0]

            def out_final(c, tb, T, rT):
                s = kk[0] % 4
                kk[0] += 1
                act(OS[s], T, AF.Identity, [rT, rPV], [rOS[s]], scale=pv("gfin", c), bias=0.0)
                P.dma("sp", d_out, [(outT[c * 128:(c + 1) * 128, tb * 512:(tb + 1) * 512], OS[s])], reads=[rOS[s]])

            rmsnorm_fm(lambda c, tb: X2[:, c, tb * 512:(tb + 1) * 512], lambda c, tb: [rX2[c][tb]], 8, D, 512, 4, out_final, 0)

        P.finish()
    return nc


_NC_CACHE = {}


def _get_nc(stage="all"):
    if stage not in _NC_CACHE:
        _NC_CACHE[stage] = build(stage)
    return _NC_CACHE[stage]


def make_in_maps(inp):
    cst = _consts()
    shared = {n: np.ascontiguousarray(np.asarray(inp[n], np.float32)) for n in WEIGHT_NAMES}
    rows = np.ascontiguousarray(np.stack([np.asarray(inp["hy_decay"][0], np.float32), np.asarray(inp["hy_d_bias"][0], np.float32)]))
    big = dict(shared)
    big.update({k: cst[k] for k in ("tbl", "fw", "iv")})
    small = {k: cst[k] for k in cst if k not in big}
    maps = []
    for b in range(NCORES):
        m = dict(small)
        for k, v in big.items():
            tag = np.full(v.shape[:-2] + (1, v.shape[-1]), b, dtype=v.dtype)
            m[k] = np.concatenate([v, tag], axis=-2)
        m["rows"] = rows
        m["xT"] = np.ascontiguousarray(np.asarray(inp["x"][b], np.float32).T)
        m["ctxT"] = np.ascontiguousarray(np.asarray(inp["ctx"][b], np.float32).T)
        m["pvec"] = _pack_pvec(inp, b)
        maps.append(m)
    return maps


def kernel(**inputs):
    inp = {k: np.asarray(v) for k, v in inputs.items()}
    nc = _get_nc("all")
    res = run_bass_kernel_spmd(nc, make_in_maps(inp), core_ids=list(range(NCORES)))
    out = np.stack([np.ascontiguousarray(r["outT"].T) for r in res.results], axis=0)
    return out.astype(np.float32)
```

```python
import contextlib
import math
import numpy as np
import ml_dtypes
import concourse.bass as bass
import concourse.mybir as mybir
from concourse.bass_utils import run_bass_kernel_spmd

F32 = mybir.dt.float32
BF16 = mybir.dt.bfloat16
ALU = mybir.AluOpType
AF = mybir.ActivationFunctionType

D = 1024
L = 2048
LC = 256
LK = LC + L
NH = 16
DFF = 2816
EPS = 1e-6
NCORES = 8
MAGIC = 12582912.0


class Reg:
    __slots__ = ("w", "rd")

    def __init__(self):
        self.w = None
        self.rd = []


def regs(*shape):
    if len(shape) == 1:
        return [Reg() for _ in range(shape[0])]
    return [regs(*shape[1:]) for _ in range(shape[0])]


class DSem:
    def __init__(self, sem):
        self.sem = sem
        self.count = 0


class Prog:
    ENG = ("pe", "act", "dve", "pool", "sp")

    def __init__(self, nc, stack):
        self.nc = nc
        self.stack = stack
        self.ops = {e: [] for e in self.ENG}
        self.cnt = {e: 0 for e in self.ENG}
        self.seen = {e: {} for e in self.ENG}
        self.esem = {e: stack.enter_context(nc.semaphore("s_" + e)) for e in self.ENG}
        self.dsems = []

    def dsem(self, name):
        d = DSem(self.stack.enter_context(self.nc.semaphore(name)))
        self.dsems.append(d)
        return d

    def _deps(self, eng, reads, writes, skip_self):
        need = {}

        def add(ev):
            key, val = ev
            if skip_self and key == eng:
                return
            if self.seen[eng].get(key, 0) >= val:
                return
            if need.get(key, 0) < val:
                need[key] = val

        for r in reads:
            if r.w is not None:
                add(r.w)
        for w in writes:
            if w.w is not None:
                add(w.w)
            for ev in w.rd:
                add(ev)
        waits = []
        for key, val in need.items():
            self.seen[eng][key] = val
            waits.append((self.esem[key] if isinstance(key, str) else key.sem, val))
        return waits

    def _commit(self, ev, reads, writes):
        for r in reads:
            if len(r.rd) > 24:
                r.rd = r.rd[-24:] if False else r.rd
            r.rd.append(ev)
        for w in writes:
            w.w = ev
            w.rd = []

    def op(self, eng, fn, reads=(), writes=()):
        waits = self._deps(eng, reads, writes, eng == "pe")
        self.cnt[eng] += 1
        ev = (eng, self.cnt[eng])
        sem = self.esem[eng]

        def emit(e):
            for s, v in waits:
                e.wait_ge(s, v)
            fn(e).then_inc(sem, 1)

        self.ops[eng].append(emit)
        self._commit(ev, reads, writes)

    def dma(self, q, dsem, pairs, reads=(), writes=()):
        waits = self._deps(q, reads, writes, False)
        dsem.count += 16 * len(pairs)
        ev = (dsem, dsem.count)
        sem = dsem.sem

        def emit(e):
            for s, v in waits:
                e.wait_ge(s, v)
            for out, in_ in pairs:
                e.dma_start(out=out, in_=in_).then_inc(sem, 16)

        self.ops[q].append(emit)
        self._commit(ev, reads, writes)

    def barrier(self):
        for e in self.ENG:
            waits = []
            for e2 in self.ENG:
                v = self.cnt[e2]
                if v > self.seen[e].get(e2, 0):
                    self.seen[e][e2] = v
                    waits.append((self.esem[e2], v))
            for d in self.dsems:
                if d.count > self.seen[e].get(d, 0):
                    self.seen[e][d] = d.count
                    waits.append((d.sem, d.count))

            def emit(eh, waits=waits):
                for s, v in waits:
                    eh.wait_ge(s, v)

            self.ops[e].append(emit)

    def finish(self):
        self.barrier()
        ops = self.ops
        with self.nc.Block() as block:
            @block.tensor
            def _(e):
                for f in ops["pe"]:
                    f(e)

            @block.scalar
            def _(e):
                for f in ops["act"]:
                    f(e)

            @block.vector
            def _(e):
                for f in ops["dve"]:
                    f(e)

            @block.gpsimd
            def _(e):
                for f in ops["pool"]:
                    f(e)

            @block.sync
            def _(e):
                for f in ops["sp"]:
                    f(e)


def _layout():
    cols = {}
    off = [0]

    def add(name, n):
        cols[name] = off[0]
        off[0] += n

    add("c", 8)
    add("cc", 8)
    for i in range(2):
        add(f"modb{i}", 48)
    for i in range(2):
        add(f"gmix{i}", 8)
        add(f"gffn{i}", 8)
    add("gfin", 8)
    add("gq", 4)
    add("gkv", 2)
    add("hbin", 24)
    for j in range(3):
        add(f"hcw{j}", 24)
    add("hcb", 24)
    add("hbout", 8)
    for i in range(2):
        for j in range(3):
            add(f"fcw{i}{j}", 44)
        add(f"fcb{i}", 44)
    for n in ("fb1", "ffr1", "fb2", "ffr2"):
        add(n, 1)
    return cols, off[0]


PCOL, PR = _layout()


def _colsof(v):
    v = np.asarray(v, np.float32).reshape(-1)
    if v.size < 128:
        v = np.concatenate([v, np.zeros(128 - v.size, np.float32)])
    return v.reshape(-1, 128).T


def _pack_pvec(inp, b):
    pv = np.zeros((128, PR), np.float32)

    def put(name, v):
        c = _colsof(v)
        pv[:, PCOL[name]:PCOL[name] + c.shape[1]] = c

    put("c", inp["c"][b])
    put("cc", inp["c_ctx"])
    for i in range(2):
        put(f"modb{i}", inp["mod_b"][i])
        put(f"gmix{i}", inp["norm_mix_g"][i])
        put(f"gffn{i}", inp["norm_ffn_g"][i])
    put("gfin", inp["final_g"])
    put("gq", inp["mla_g_q"][0])
    put("gkv", inp["mla_g_kv"][0])
    put("hbin", inp["hy_b_in"][0])
    for j in range(3):
        put(f"hcw{j}", inp["hy_conv_w"][0, j])
    put("hcb", inp["hy_conv_b"][0])
    put("hbout", inp["hy_b_out"][0])
    for i in range(2):
        for j in range(3):
            put(f"fcw{i}{j}", inp["ffn_conv_w"][i, j])
        put(f"fcb{i}", inp["ffn_conv_b"][i])
    put("fb1", inp["hy_f_b1"][0])
    put("ffr1", inp["hy_f_freq1"][0])
    put("fb2", inp["hy_f_b2"][0])
    put("ffr2", inp["hy_f_freq2"][0])
    return pv


_CONST = {}


def _consts():
    if _CONST:
        return _CONST
    f32 = np.float32
    rows = L // 64
    row = np.repeat(np.arange(rows, dtype=f32), 64)
    col = np.tile(np.arange(64, dtype=f32), rows)
    half = 16
    inv = (f32(10000.0) ** (-np.arange(0, half, 2, dtype=f32) / f32(half))).astype(f32)
    ar = row[:, None] * inv
    ac = col[:, None] * inv
    ang = np.concatenate([ar, ar, ac, ac], axis=-1).astype(f32)
    tbl = np.zeros((128, 2, L), f32)
    tbl[64:96, 0, :] = np.cos(ang).T
    tbl[64:96, 1, :] = np.sin(ang).T
    t = np.linspace(0.0, 1.0, L, dtype=f32)
    w = (f32(2.0 * math.pi) * np.arange(L, dtype=f32) / f32(L)).astype(f32)
    fr = np.linspace(1e-4, 15, 16, dtype=f32)
    z = np.concatenate([t[:, None], np.cos(w[:, None] * fr), -np.sin(w[:, None] * fr)], axis=-1).astype(f32)
    zT = np.ascontiguousarray(z.T)
    negt = np.ascontiguousarray((-t).reshape(16, 128).T)
    tt = np.arange(L, dtype=np.float64)
    ff = np.arange(L, dtype=np.float64) + 0.5
    th = 2.0 * np.pi * np.outer(tt, ff) / 4096.0
    C = np.cos(th)
    S = np.sin(th)
    fwd = np.concatenate([C, S], axis=1)
    FWm = fwd.reshape(16, 128, 32, 128).transpose(2, 1, 0, 3)
    FWm = np.ascontiguousarray(FWm).reshape(32, 128, 2048).astype(ml_dtypes.bfloat16)
    inv_m = np.concatenate([C.T, -S.T], axis=0)
    IVm = inv_m.reshape(32, 128, 8, 256).transpose(2, 1, 0, 3)
    IVm = np.ascontiguousarray(IVm).reshape(8, 128, 32 * 256).astype(ml_dtypes.bfloat16)
    ident = np.eye(128, dtype=f32).astype(ml_dtypes.bfloat16)
    _CONST.update(tbl=tbl, zT=zT, negt=negt, fw=FWm, iv=IVm, ident=ident)
    return _CONST


WEIGHT_NAMES = ["mod_w", "mla_w_dq", "mla_w_uq", "mla_w_dkv", "mla_w_uk", "mla_w_uv", "mla_w_o",
                "hy_w_in", "hy_f_w1", "hy_f_w2", "hy_f_w3", "hy_w_out", "ffn_w_up", "ffn_w_down"]
WEIGHT_SHAPES = {
    "mod_w": (2, 1024, 6144), "mla_w_dq": (1, 1024, 512), "mla_w_uq": (1, 512, 1536),
    "mla_w_dkv": (1, 1024, 288), "mla_w_uk": (1, 256, 1024), "mla_w_uv": (1, 256, 1024),
    "mla_w_o": (1, 1024, 1024), "hy_w_in": (1, 1024, 3072), "hy_f_w1": (1, 33, 64),
    "hy_f_w2": (1, 64, 64), "hy_f_w3": (1, 64, 2048), "hy_w_out": (1, 1024, 1024),
    "ffn_w_up": (2, 1024, 5632), "ffn_w_down": (2, 2816, 1024),
}


def KB(x):
    return int(round(x * 256))


def build(stage="all"):
    nc = bass.Bass("TRN2", target_bir_lowering=False)

    def din(name, shape, dt=F32):
        return nc.dram_tensor(name, list(shape), dt, kind="ExternalInput").ap()

    xT = din("xT", [D, L])
    ctxT = din("ctxT", [D, LC])
    pvec = din("pvec", [128, PR])
    rows_d = din("rows", [2, D])
    tbl_d = din("tbl", [128, 3, L])[:, 0:2, :]
    zT_d = din("zT", [33, L])
    negt_d = din("negt", [128, 16])
    fw_d = din("fw", [32, 129, 2048], BF16)[:, 0:128, :]
    iv_d = din("iv", [8, 129, 32 * 256], BF16)[:, 0:128, :]
    ident_d = din("ident", [128, 128], BF16)
    Wd = {}
    for n in WEIGHT_NAMES:
        a, r, c = WEIGHT_SHAPES[n]
        Wd[n] = din(n, [a, r + 1, c])[:, 0:r, :]
    outs_d = [nc.dram_tensor(f"o{k}", [64, L], F32, kind="ExternalOutput").ap() for k in range(16)]

    def out_pairs(row0, nrows, col0, ncols, src, src_p0=0):
        pairs = []
        r = row0
        while r < row0 + nrows:
            k, off = divmod(r, 64)
            n = min(64 - off, row0 + nrows - r)
            p0 = src_p0 + (r - row0)
            pairs.append((outs_d[k][off:off + n, col0:col0 + ncols], src[p0:p0 + n]))
            r += n
        return pairs
    xpark = nc.dram_tensor("xpark", [128, 8 * L], F32, kind="Internal").ap()

    with contextlib.ExitStack() as st:
        P = Prog(nc, st)
        NW = KB(207.5)
        AR = st.enter_context(nc.sbuf_tensor("arena", [128, NW], F32))
        PS = st.enter_context(nc.psum_tensor("ps", [128, 8, 512], F32))

        def _shape(ap, shape):
            if len(shape) == 1:
                return ap
            if len(shape) == 2:
                return ap.rearrange("p (a b) -> p a b", a=shape[0])
            if len(shape) == 3:
                return ap.rearrange("p (a b c) -> p a b c", a=shape[0], b=shape[1])
            raise ValueError

        def f32v(off, shape):
            n = int(np.prod(shape))
            assert off + n <= NW, (off, n)
            return _shape(AR[:, off:off + n], shape)

        def bf16v(off, shape):
            n = int(np.prod(shape))
            assert n % 2 == 0 and off + n // 2 <= NW, (off, n)
            return _shape(AR[:, off:off + n // 2].bitcast(BF16), shape)

        rPS = regs(8)
        bank_ctr = [0]

        def nb():
            b = bank_ctr[0] % 6
            bank_ctr[0] += 1
            return b

        lbank_ctr = [0]

        def nbl():
            b = 6 + lbank_ctr[0] % 2
            lbank_ctr[0] += 1
            return b

        def PSb(b):
            return PS[:, b, :]

        def mm(out, lhsT, rhs, start, stop, reads, writes):
            P.op("pe", lambda e: e.matmul(out, lhsT=lhsT, rhs=rhs, start=start, stop=stop), reads, writes)

        def act(out, in_, func, reads, writes, eng="act", **kw):
            P.op(eng, lambda e: e.activation(out=out, in_=in_, func=func, **kw), reads, writes)

        def tt(eng, out, in0, in1, op, reads, writes):
            P.op(eng, lambda e: e.tensor_tensor(out=out, in0=in0, in1=in1, op=op), reads, writes)

        def ts(eng, out, in0, s1, s2, op0, op1, reads, writes):
            P.op(eng, lambda e: e.tensor_scalar(out=out, in0=in0, scalar1=s1, scalar2=s2, op0=op0, op1=op1), reads, writes)

        def stt(eng, out, in0, scalar, in1, op0, op1, reads, writes):
            P.op(eng, lambda e: e.scalar_tensor_tensor(out=out, in0=in0, scalar=scalar, in1=in1, op0=op0, op1=op1), reads, writes)

        def cp(eng, out, in_, reads, writes):
            P.op(eng, lambda e: e.tensor_copy(out=out, in_=in_), reads, writes)

        def memset(eng, ap, val, writes):
            P.op(eng, lambda e: e.memset(ap, val), (), writes)

        def recip(out, in_, reads, writes):
            P.op("dve", lambda e: e.reciprocal(out=out, in_=in_), reads, writes)

        o = KB(200)
        PV = f32v(o, [PR]); o += PR
        MODX = f32v(o, [2, 48]); o += 96
        MODC = f32v(o, [48]); o += 48
        DER = f32v(o, [2, 4, 8]); o += 64
        DERC = f32v(o, [8]); o += 8
        ONESF = f32v(o, [128]); o += 128
        o = (o + 15) // 16 * 16
        SC = bf16v(o, [8, 2]); o += 8
        ONESB = bf16v(o, [128]); o += 64
        IDN = bf16v(o, [128]); o += 64
        assert o <= NW
        rPV, rMODX, rMODC, rDER, rSC, rCONST = Reg(), regs(2), Reg(), regs(2), Reg(), Reg()

        def pv(name, j=0, n=1):
            return PV[:, PCOL[name] + j:PCOL[name] + j + n]

        d_misc = P.dsem("d_misc")
        d_x = P.dsem("d_x")
        d_out = P.dsem("d_out")

        X = f32v(0, [8, L])
        rX = regs(8, 4)
        A = bf16v(KB(64), [8, L])
        rA = regs(8, 4)

        def allr(rr):
            return [r for row in rr for r in row]

        P.dma("sp", d_misc, [(PV, pvec), (IDN, ident_d)], writes=[rPV, rCONST])
        P.dma("sp", d_x, [(X[:, c, :], xT[c * 128:(c + 1) * 128, :]) for c in range(8)], writes=allr(rX))
        memset("dve", ONESF, 1.0, [rCONST])
        memset("dve", ONESB, 1.0, [rCONST])
        act(SC[:, :, 0], pv("c", 0, 8), AF.Silu, [rPV], [rSC])
        act(SC[:, :, 1], pv("cc", 0, 8), AF.Silu, [rPV], [rSC])

        MWS = [bf16v(KB(96) + s * KB(4), [8, 256]) for s in range(2)]
        rMWS = regs(2)
        d_mws = [P.dsem("d_mw0"), P.dsem("d_mw1")]
        mod_state = {"slot": 0, "slots": MWS}

        def mod_groups(i, g0, g1, with_ctx, gw=256):
            bank = nb()
            npg = gw // 128
            psm = PSb(bank)[:, 0:96].rearrange("p (j v) -> p j v", v=2)
            MW = mod_state["slots"]
            for g in range(g0, g1):
                s = mod_state["slot"] % 2
                mod_state["slot"] += 1
                src = Wd["mod_w"][i, :, g * gw:(g + 1) * gw].rearrange("(kc p) n -> p kc n", p=128)
                P.dma("pool", d_mws[s], [(MW[s][:, :, 0:gw], src)], writes=[rMWS[s]])
                for jj in range(npg):
                    j = g * npg + jj
                    for kc in range(8):
                        mm(psm[:, j, :], MW[s][:, kc, jj * 128:(jj + 1) * 128], SC[:, kc, :], kc == 0, kc == 7,
                           [rMWS[s], rSC], [rPS[bank]])
            j0, j1 = g0 * npg, g1 * npg
            tt("dve", MODX[:, i, j0:j1], psm[:, j0:j1, 0], pv(f"modb{i}", j0, j1 - j0), ALU.add, [rPS[bank], rPV], [rMODX[i]])
            if with_ctx:
                tt("dve", MODC[:, j0:j1], psm[:, j0:j1, 1], pv(f"modb{i}", j0, j1 - j0), ALU.add, [rPS[bank], rPV], [rMODC])

        def mod_derive(i, part):
            if part == 1:
                stt("dve", DER[:, i, 0, :], MODX[:, i, 8:16], 1.0, pv(f"gmix{i}", 0, 8), ALU.add, ALU.mult, [rMODX[i], rPV], [rDER[i]])
                if i == 0:
                    stt("dve", DERC, MODC[:, 8:16], 1.0, pv("gmix0", 0, 8), ALU.add, ALU.mult, [rMODC, rPV], [rDER[i]])
            else:
                stt("dve", DER[:, i, 1, :], MODX[:, i, 32:40], 1.0, pv(f"gffn{i}", 0, 8), ALU.add, ALU.mult, [rMODX[i], rPV], [rDER[i]])
                if i == 1:
                    tt("dve", DER[:, i, 2, :], MODX[:, i, 16:24], pv("hbout", 0, 8), ALU.mult, [rMODX[i], rPV], [rDER[i]])

        def rmsnorm_fm(src, rsrc, nch, Dn, N, ntb, emit_out, tmp_off):
            SQ = [bf16v(tmp_off + s * 256, [512]) for s in range(2)]
            RS = f32v(tmp_off + 512, [512])
            TT = [f32v(tmp_off + 1024 + s * 512, [512]) for s in range(2)]
            rSQ, rRS, rTT = regs(2), Reg(), regs(2)
            for tb in range(ntb):
                bank = nb()
                for c in range(nch):
                    s = c % 2
                    act(SQ[s][:, :N], src(c, tb), AF.Square, rsrc(c, tb), [rSQ[s]])
                    mm(PSb(bank)[:, :N], ONESB, SQ[s][:, :N], c == 0, c == nch - 1, [rSQ[s], rCONST], [rPS[bank]])
                ts("dve", RS[:, :N], PSb(bank)[:, :N], 1.0 / Dn, EPS, ALU.mult, ALU.add, [rPS[bank]], [rRS])
                act(RS[:, :N], RS[:, :N], AF.Sqrt, [rRS], [rRS])
                recip(RS[:, :N], RS[:, :N], [rRS], [rRS])
                for c in range(nch):
                    s = c % 2
                    tt("dve", TT[s][:, :N], src(c, tb), RS[:, :N], ALU.mult, rsrc(c, tb) + [rRS], [rTT[s]])
                    emit_out(c, tb, TT[s][:, :N], rTT[s])

        dbg_done = [False]

        def dbg_dump_fm(name, tile_fn, regs_list):
            if stage != name:
                return
            P.barrier()
            stg = f32v(KB(196), [512])
            rS = Reg()
            for c in range(8):
                for tb in range(4):
                    cp("dve", stg, tile_fn(c, tb), regs_list(c, tb), [rS])
                    P.dma("sp", d_out, out_pairs(c * 128, 128, tb * 512, 512, stg), reads=[rS])
            dbg_done[0] = True

        def dbg_blocks(name, blocks):
            if stage != name:
                return
            P.barrier()
            stg = f32v(KB(196), [512])
            rS = Reg()
            for row0, col0, src in blocks:
                p, n = src.shape[0], src.shape[1]
                bp = src.base_partition() if callable(getattr(src, "base_partition", None)) else 0
                cp("dve", stg[bp:bp + p, 0:n], src, [], [rS])
                P.dma("sp", d_out, out_pairs(row0, p, col0, n, stg[:, 0:n], src_p0=bp), reads=[rS])
            dbg_done[0] = True

        mod_groups(0, 0, 8, True)
        mod_derive(0, 1)

        TBL = f32v(KB(124), [2, L])
        rTBL = Reg()
        XC = f32v(KB(112), [8, LC])
        AC = bf16v(KB(120), [8, LC])
        rXC, rAC = regs(8), regs(8)
        P.dma("sp", d_misc, [(TBL[64:96], tbl_d[64:96])], writes=[rTBL])
        P.dma("sp", d_misc, [(XC[:, c, :], ctxT[c * 128:(c + 1) * 128, :]) for c in range(8)], writes=rXC)

        WDQ = bf16v(KB(170), [8, 512])
        WDKV = bf16v(KB(178), [8, 288])
        WKPE = bf16v(KB(182.5), [8, 96])
        WKROT = bf16v(KB(184), [8, 96])
        rWDQ, rWDKV, rWKPE, rWKROT = Reg(), Reg(), Reg(), Reg()
        d_w = P.dsem("d_w")
        P.dma("pool", d_w, [(WDQ, Wd["mla_w_dq"][0].rearrange("(kc p) n -> p kc n", p=128))], writes=[rWDQ])
        P.dma("pool", d_w, [(WDKV, Wd["mla_w_dkv"][0].rearrange("(kc p) n -> p kc n", p=128))], writes=[rWDKV])

        def build_rot(dst, src, eng, reads, writes):
            def neg(o, i):
                P.op(eng, lambda e: e.tensor_scalar_mul(out=o, in0=i, scalar1=-1.0), reads, writes)
            neg(dst[..., 0:8], src[..., 8:16])
            cp(eng, dst[..., 8:16], src[..., 0:8], reads, writes)
            neg(dst[..., 16:24], src[..., 24:32])
            cp(eng, dst[..., 24:32], src[..., 16:24], reads, writes)

        memset("dve", WKPE, 0.0, [rWKPE])
        memset("dve", WKROT, 0.0, [rWKROT])
        cp("dve", WKPE[:, :, 64:96], WDKV[:, :, 256:288], [rWDKV], [rWKPE])
        build_rot(WKROT[:, :, 64:96], WDKV[:, :, 256:288], "dve", [rWDKV], [rWKROT])

        NT = KB(104)

        def out_A(c, tb, T, rT):
            act(A[:, c, tb * 512:(tb + 1) * 512], T, AF.Identity, [rT, rDER[0], rMODX[0]], [rA[c][tb]],
                scale=DER[:, 0, 0, c:c + 1], bias=MODX[:, 0, c:c + 1])

        rmsnorm_fm(lambda c, tb: X[:, c, tb * 512:(tb + 1) * 512], lambda c, tb: [rX[c][tb]], 8, D, 512, 4, out_A, NT)

        def out_AC(c, tb, T, rT):
            act(AC[:, c, :], T, AF.Identity, [rT, rDER[0], rMODC], [rAC[c]], scale=DERC[:, c:c + 1], bias=MODC[:, c:c + 1])

        rmsnorm_fm(lambda c, tb: XC[:, c, :], lambda c, tb: [rXC[c]], 8, D, LC, 1, out_AC, NT)

        dbg_dump_fm("hx0", lambda c, tb: A[:, c, tb * 512:(tb + 1) * 512], lambda c, tb: [rA[c][tb]])

        CQN = bf16v(KB(140), [4, L])
        rCQN = regs(4)
        CKVN = bf16v(KB(156), [2, LK])
        rCKVN = regs(5)
        KT = bf16v(KB(165.5), [LK])
        rKT = Reg()
        T1 = f32v(KB(195.5), [512])
        T2 = f32v(KB(197.5), [512])
        rT1, rT2 = Reg(), Reg()

        def key_slice(kb):
            return (0, LC) if kb == 0 else (LC + (kb - 1) * 512, LC + kb * 512)

        if not dbg_done[0]:
            for tb in range(4):
                banks = []
                for m in range(4):
                    bk = nb()
                    banks.append(bk)
                    for kc in range(8):
                        mm(PSb(bk), WDQ[:, kc, m * 128:(m + 1) * 128], A[:, kc, tb * 512:(tb + 1) * 512], kc == 0, kc == 7,
                           [rWDQ, rA[kc][tb]], [rPS[bk]])

                def out_cq(c, tb_, T, rT, tb=tb):
                    act(CQN[:, c, tb * 512:(tb + 1) * 512], T, AF.Identity, [rT, rPV], [rCQN[c]], scale=pv("gq", c, 1), bias=0.0)

                rmsnorm_fm(lambda c, tb_, banks=banks: PSb(banks[c]), lambda c, tb_, banks=banks: [rPS[banks[c]]], 4, 512, 512, 1, out_cq, NT)

            for kb in range(5):
                k0, k1 = key_slice(kb)
                N = k1 - k0
                if kb == 0:
                    def rhs(kc):
                        return AC[:, kc, :], [rAC[kc]]
                else:
                    def rhs(kc, kb=kb):
                        return A[:, kc, (kb - 1) * 512:kb * 512], [rA[kc][kb - 1]]
                banks = []
                for m in range(2):
                    bk = nb()
                    banks.append(bk)
                    for kc in range(8):
                        r_ap, r_rg = rhs(kc)
                        mm(PSb(bk)[:, :N], WDKV[:, kc, m * 128:(m + 1) * 128], r_ap, kc == 0, kc == 7, [rWDKV] + r_rg, [rPS[bk]])
                bpe = nb()
                for kc in range(8):
                    r_ap, r_rg = rhs(kc)
                    mm(PSb(bpe)[0:96, :N], WKPE[:, kc, :], r_ap, kc == 0, kc == 7, [rWKPE] + r_rg, [rPS[bpe]])
                if kb > 0:
                    brot = nb()
                    for kc in range(8):
                        r_ap, r_rg = rhs(kc)
                        mm(PSb(brot)[0:96, :N], WKROT[:, kc, :], r_ap, kc == 0, kc == 7, [rWKROT] + r_rg, [rPS[brot]])
                    t0 = (kb - 1) * 512
                    tt("dve", T1[64:96, :], PSb(bpe)[64:96, :], TBL[64:96, 0, t0:t0 + 512], ALU.mult, [rPS[bpe], rTBL], [rT1])
                    tt("dve", T2[64:96, :], PSb(brot)[64:96, :], TBL[64:96, 1, t0:t0 + 512], ALU.mult, [rPS[brot], rTBL], [rT2])
                    tt("dve", KT[64:96, k0:k1], T1[64:96, :], T2[64:96, :], ALU.add, [rT1, rT2], [rKT])
                else:
                    cp("dve", KT[64:96, k0:k1], PSb(bpe)[64:96, :N], [rPS[bpe]], [rKT])

                def out_ckv(c, tb_, T, rT, k0=k0, k1=k1, kb=kb):
                    act(CKVN[:, c, k0:k1], T, AF.Identity, [rT, rPV], [rCKVN[kb]], scale=pv("gkv", c, 1), bias=0.0)

                rmsnorm_fm(lambda c, tb_, banks=banks, N=N: PSb(banks[c])[:, :N], lambda c, tb_, banks=banks: [rPS[banks[c]]],
                           2, 256, N, 1, out_ckv, NT)

            mod_groups(0, 8, 24, False)
            mod_derive(0, 2)
            P.barrier()

            blks = []
            for m in range(2):
                for tb in range(4):
                    blks.append((m * 128, tb * 512, CKVN[:, m, 256 + tb * 512:256 + (tb + 1) * 512]))
            for tb in range(4):
                blks.append((256 + 64, tb * 512, KT[64:96, 256 + tb * 512:256 + (tb + 1) * 512]))
            blks.append((384, 0, CKVN[:, 0, 0:256]))
            blks.append((384, 256, CKVN[:, 1, 0:256]))
            blks.append((384 + 64, 512, KT[64:96, 0:256]))
            blks.append((384, 768, AC[:, 0, :]))
            for m in range(4):
                for tb in range(4):
                    blks.append((512 + m * 128, tb * 512, CQN[:, m, tb * 512:(tb + 1) * 512]))
            dbg_blocks("kvdbg", blks)
        if not dbg_done[0]:
            WUQ = bf16v(KB(170), [4, 1536])
            WUK = bf16v(KB(182), [2, 1024])
            WUV = bf16v(KB(186), [2, 1024])
            ROT = [bf16v(KB(190) + s * KB(0.75), [4, 96]) for s in range(2)]
            KT2 = [KT, bf16v(KB(96), [LK])]
            OAs = [f32v(KB(116.5), [512]), f32v(KB(100.5), [512])]
            RCs = [f32v(KB(118.5), [512]), f32v(KB(102.5), [512])]
            VP = [bf16v(KB(104.5) + s * KB(4.5), [18, 128]) for s in range(2)]
            PT = [bf16v(KB(113.5) + s * KB(1), [512]) for s in range(3)]
            QTB = [bf16v(KB(120.5) + s * KB(1), [512]) for s in range(3)]
            rWUQ, rWUK, rWUV, rROT, rQTB, rVP, rPT, rOA, rRC = Reg(), Reg(), Reg(), regs(2), regs(3), regs(2), regs(3), regs(2), regs(2)
            rKT2 = [rKT, Reg()]
            P.dma("pool", d_w, [(WUQ, Wd["mla_w_uq"][0].rearrange("(kc p) n -> p kc n", p=128))], writes=[rWUQ])
            P.dma("pool", d_w, [(WUK, Wd["mla_w_uk"][0].rearrange("(kc p) n -> p kc n", p=128))], writes=[rWUK])
            P.dma("pool", d_w, [(WUV, Wd["mla_w_uv"][0].rearrange("(kc p) n -> p kc n", p=128))], writes=[rWUV])
            cp("dve", KT2[1][64:96, :], KT[64:96, :], [rKT], [rKT2[1]])
            for s in range(2):
                memset("dve", ROT[s], 0.0, [rROT[s]])
                memset("dve", VP[s], 0.0, [rVP[s]])
            memset("dve", VP[0][:, :, 64:65], 1.0, [rVP[0]])
            memset("dve", VP[1][:, :, 0:1], 1.0, [rVP[1]])
            WUQh = WUQ.rearrange("p k (h d) -> p k h d", h=NH)
            scale = 1.0 / math.sqrt(96.0)

            def build_head_k(h):
                par = h % 2
                build_rot(ROT[par][:, :, 64:96], WUQh[:, :, h, 64:96], "dve", [rWUQ], [rROT[par]])
                for kb in range(5):
                    k0, k1 = key_slice(kb)
                    bk = nb()
                    for kc in range(2):
                        mm(PSb(bk)[0:64, :k1 - k0], WUK[:, kc, h * 64:(h + 1) * 64], CKVN[:, kc, k0:k1], kc == 0, kc == 1,
                           [rWUK, rCKVN[kb]], [rPS[bk]])
                    cp("dve", KT2[par][0:64, k0:k1], PSb(bk)[0:64, :k1 - k0], [rPS[bk]], [rKT2[par]])

            def build_head_v(h):
                par = h % 2
                po = 64 * par
                for g in range(5):
                    kts = list(range(g * 4, min(18, g * 4 + 4)))
                    bk = nb()
                    for i, kt in enumerate(kts):
                        kb = 0 if kt < 2 else 1 + (kt - 2) // 4
                        for kc in range(2):
                            mm(PSb(bk)[:, i * 64:(i + 1) * 64], CKVN[:, kc, kt * 128:(kt + 1) * 128], WUV[:, kc, h * 64:(h + 1) * 64],
                               kc == 0, kc == 1, [rWUV, rCKVN[kb]], [rPS[bk]])
                    n = len(kts)
                    cp("dve", VP[par][:, kts[0]:kts[0] + n, po:po + 64],
                       PSb(bk)[:, 0:n * 64].rearrange("p (a b) -> p a b", a=n), [rPS[bk]], [rVP[par]])

            def build_q(i):
                h, qb = divmod(i, 4)
                par = h % 2
                q0 = qb * 512
                qt, rq = QTB[i % 3], rQTB[i % 3]
                bq, br = nb(), nb()
                for kc in range(4):
                    mm(PSb(bq)[0:96, :], WUQh[:, kc, h, :], CQN[:, kc, q0:q0 + 512], kc == 0, kc == 3, [rWUQ, rCQN[kc]], [rPS[bq]])
                for kc in range(4):
                    mm(PSb(br)[0:96, :], ROT[par][:, kc, :], CQN[:, kc, q0:q0 + 512], kc == 0, kc == 3, [rROT[par], rCQN[kc]], [rPS[br]])
                cp("dve", qt[0:64, :], PSb(bq)[0:64, :], [rPS[bq]], [rq])
                tt("dve", T1[64:96, :], PSb(bq)[64:96, :], TBL[64:96, 0, q0:q0 + 512], ALU.mult, [rPS[bq], rTBL], [rT1])
                tt("dve", T2[64:96, :], PSb(br)[64:96, :], TBL[64:96, 1, q0:q0 + 512], ALU.mult, [rPS[br], rTBL], [rT2])
                tt("dve", qt[64:96, :], T1[64:96, :], T2[64:96, :], ALU.add, [rT1, rT2], [rq])

            bo_of = {}

            def norm_part1(i):
                h, qb = divmod(i, 4)
                po = 64 * (h % 2)
                srow = 64 - po
                OA, RC = OAs[i % 2], RCs[i % 2]
                cp("dve", OA, PSb(bo_of[i]), [rPS[bo_of[i]]], [rOA[i % 2]])
                recip(RC[srow:srow + 1, :], OA[srow:srow + 1, :], [rOA[i % 2]], [rRC[i % 2]])

            def norm_part2(i):
                h, qb = divmod(i, 4)
                po = 64 * (h % 2)
                srow = 64 - po
                q0 = qb * 512
                OA, RC = OAs[i % 2], RCs[i % 2]
                bb = nb()
                mm(PSb(bb), ONESF[srow:srow + 1, :], RC[srow:srow + 1, :], True, True, [rRC[i % 2], rCONST], [rPS[bb]])
                tt("dve", A[po:po + 64, h // 2, q0:q0 + 512], OA[po:po + 64, :], PSb(bb)[po:po + 64, :], ALU.mult,
                   [rOA[i % 2], rPS[bb]], [rA[h // 2][qb]])

            mod_state["slots"] = [bf16v(KB(191.5) + s * KB(2), [8, 128]) for s in range(2)]
            mod1_next = [0]
            build_head_k(0)
            build_head_v(0)
            build_q(0)
            NBLK = NH * 4
            LA = 2
            for i in range(NBLK):
                h, qb = divmod(i, 4)
                par = h % 2
                qt, rq = QTB[i % 3], rQTB[i % 3]
                bo = nbl()
                bo_of[i] = bo
                for it in range(18 + LA):
                    if it < 18:
                        kt = it
                        bs = nb()
                        mm(PSb(bs), KT2[par][0:96, kt * 128:(kt + 1) * 128], qt[0:96, :], True, True, [rKT2[par], rq], [rPS[bs]])
                        act(PT[kt % 3], PSb(bs), AF.Exp, [rPS[bs]], [rPT[kt % 3]], scale=scale)
                    if it >= LA:
                        kt = it - LA
                        mm(PSb(bo), VP[par][:, kt, :], PT[kt % 3], kt == 0, kt == 17, [rVP[par], rPT[kt % 3]], [rPS[bo]])
                    if it == 3 and i >= 1:
                        norm_part1(i - 1)
                    if it == 6 and i + 1 < NBLK and (i + 1) % 4 != 0:
                        build_q(i + 1)
                    if it == 8 and qb == 1 and h + 1 < NH:
                        build_head_k(h + 1)
                    if it == 8 and qb == 2 and h + 1 < NH:
                        build_head_v(h + 1)
                    if it == 6 and qb == 3 and i + 1 < NBLK:
                        build_q(i + 1)
                    if it == 13 and i >= 1:
                        norm_part2(i - 1)
                    if it == 16 and mod1_next[0] < 48:
                        mod_groups(1, mod1_next[0], mod1_next[0] + 1, False, gw=128)
                        mod1_next[0] += 1
            norm_part1(NBLK - 1)
            norm_part2(NBLK - 1)
            assert mod1_next[0] == 48
            mod_derive(1, 1)
            mod_derive(1, 2)

            dbg_dump_fm("ox", lambda c, tb: A[:, c, tb * 512:(tb + 1) * 512], lambda c, tb: [rA[c][tb]])

        if not dbg_done[0]:
            P.barrier()
            WO = bf16v(KB(170), [8, 1024])
            rWO = Reg()
            P.dma("pool", d_w, [(WO, Wd["mla_w_o"][0].rearrange("(kc p) n -> p kc n", p=128))], writes=[rWO])
            for m in range(8):
                for tb in range(4):
                    bk = nb()
                    for kc in range(8):
                        mm(PSb(bk), WO[:, kc, m * 128:(m + 1) * 128], A[:, kc, tb * 512:(tb + 1) * 512], kc == 0, kc == 7,
                           [rWO, rA[kc][tb]], [rPS[bk]])
                    xs = X[:, m, tb * 512:(tb + 1) * 512]
                    stt("dve", xs, PSb(bk), MODX[:, 0, 16 + m:17 + m], xs, ALU.mult, ALU.add, [rPS[bk], rMODX[0], rX[m][tb]], [rX[m][tb]])
            dbg_dump_fm("xmix0", lambda c, tb: X[:, c, tb * 512:(tb + 1) * 512], lambda c, tb: [rX[c][tb]])

        def ffn(i, Xv, rXv, Av, rAv, Uo, To, Wo_, hook=None):
            P.barrier()

            def out_Af(c, tb, T, rT):
                act(Av[:, c, tb * 512:(tb + 1) * 512], T, AF.Identity, [rT, rDER[i], rMODX[i]], [rAv[c][tb]],
                    scale=DER[:, i, 1, c:c + 1], bias=MODX[:, i, 24 + c:25 + c])

            rmsnorm_fm(lambda c, tb: Xv[:, c, tb * 512:(tb + 1) * 512], lambda c, tb: [rXv[c][tb]], 8, D, 512, 4, out_Af, To + KB(24.5))
            U = bf16v(Uo, [11, L])
            rU = regs(11)
            Z = f32v(To, [L + 2])
            YA = f32v(To + KB(8) + 16, [L])
            YG = f32v(To + KB(16) + 32, [L])
            rZ, rYA, rYG = Reg(), Reg(), Reg()
            WUP = [bf16v(Wo_ + s * KB(4), [8, 256]) for s in range(3)]
            rWUP = regs(3)
            d_wup = [P.dsem(f"d_wup{i}{s}") for s in range(3)]
            WDN = [bf16v(Wo_ + KB(12) + s * KB(2.75), [11, 128]) for s in range(2)]
            rWDN = regs(2)
            d_wdn = [P.dsem(f"d_wdn{i}{s}") for s in range(2)]
            memset("dve", Z[:, 0:1], 0.0, [rZ])
            memset("dve", Z[:, L + 1:L + 2], 0.0, [rZ])
            wup = Wd["ffn_w_up"][i]
            wdn = Wd["ffn_w_down"][i]
            pair = 0
            for hf in range(2):
                for jj in range(11):
                    j = hf * 11 + jj
                    s = pair % 3
                    pair += 1
                    P.dma("pool", d_wup[s], [
                        (WUP[s][:, :, 0:128], wup[:, j * 128:(j + 1) * 128].rearrange("(kc p) n -> p kc n", p=128)),
                        (WUP[s][:, :, 128:256], wup[:, DFF + j * 128:DFF + (j + 1) * 128].rearrange("(kc p) n -> p kc n", p=128)),
                    ], writes=[rWUP[s]])
                    for part, Y, rY in ((0, YA, rYA), (1, YG, rYG)):
                        ch = j if part == 0 else 22 + j
                        banks = []
                        for tb in range(4):
                            bk = nb()
                            banks.append(bk)
                            for kc in range(8):
                                mm(PSb(bk), WUP[s][:, kc, part * 128:(part + 1) * 128], Av[:, kc, tb * 512:(tb + 1) * 512],
                                   kc == 0, kc == 7, [rWUP[s], rAv[kc][tb]], [rPS[bk]])
                        w0 = pv(f"fcw{i}0", ch)
                        w1 = pv(f"fcw{i}1", ch)
                        w2 = pv(f"fcw{i}2", ch)
                        bb_ = pv(f"fcb{i}", ch)
                        for tb in range(4):
                            act(Z[:, 1 + tb * 512:1 + (tb + 1) * 512], PSb(banks[tb]), AF.Copy, [rPS[banks[tb]]], [rZ])
                        act(Y, Z[:, 1:L + 1], AF.Identity, [rZ, rPV], [rY], scale=w1, bias=bb_)
                        stt("dve", Y, Z[:, 0:L], w0, Y, ALU.mult, ALU.add, [rZ, rPV, rY], [rY])
                        stt("dve", Y, Z[:, 2:L + 2], w2, Y, ALU.mult, ALU.add, [rZ, rPV, rY], [rY])
                    act(YG, YG, AF.Silu, [rYG], [rYG])
                    tt("dve", U[:, jj, :], YG, YA, ALU.mult, [rYG, rYA], [rU[jj]])
                    if hook is not None:
                        hook(j)
                for m in range(8):
                    s = m % 2
                    P.dma("pool", d_wdn[s], [(WDN[s], wdn[hf * 1408:(hf + 1) * 1408, m * 128:(m + 1) * 128].rearrange("(kc p) n -> p kc n", p=128))],
                          writes=[rWDN[s]])
                    for tb in range(4):
                        bk = nb()
                        for kc in range(11):
                            mm(PSb(bk), WDN[s][:, kc, :], U[:, kc, tb * 512:(tb + 1) * 512], kc == 0, kc == 10, [rWDN[s], rU[kc]], [rPS[bk]])
                        xs = Xv[:, m, tb * 512:(tb + 1) * 512]
                        stt("dve", xs, PSb(bk), MODX[:, i, 40 + m:41 + m], xs, ALU.mult, ALU.add, [rPS[bk], rMODX[i], rXv[m][tb]], [rXv[m][tb]])

        if not dbg_done[0]:
            ffn(0, X, rX, A, rA, KB(96), KB(140), KB(173))
            dbg_dump_fm("xffn0", lambda c, tb: X[:, c, tb * 512:(tb + 1) * 512], lambda c, tb: [rX[c][tb]])

        X2 = f32v(KB(128), [8, L])
        rX2 = regs(8, 4)
        A2 = bf16v(KB(96), [8, L])
        rA2 = regs(8, 4)
        if not dbg_done[0]:
            hyena_ok = True
            P.barrier()

            def out_Ah(c, tb, T, rT):
                act(A[:, c, tb * 512:(tb + 1) * 512], T, AF.Identity, [rT, rDER[1], rMODX[1]], [rA[c][tb]],
                    scale=DER[:, 1, 0, c:c + 1], bias=MODX[:, 1, c:c + 1])

            rmsnorm_fm(lambda c, tb: X[:, c, tb * 512:(tb + 1) * 512], lambda c, tb: [rX[c][tb]], 8, D, 512, 4, out_Ah, KB(160))
            dbg_dump_fm("hx1", lambda c, tb: A[:, c, tb * 512:(tb + 1) * 512], lambda c, tb: [rA[c][tb]])
            P.barrier()

        if not dbg_done[0]:
            d_park = P.dsem("d_park")
            P.dma("sp", d_park, [(xpark[:, c * L:(c + 1) * L], X[:, c, :]) for c in range(8)], reads=allr(rX))

            X1 = bf16v(KB(96), [8, L])
            rX1 = regs(8, 4)
            VTOK = bf16v(KB(128), [16, D])
            rVTOK = regs(16)
            Z = f32v(KB(160), [L + 2])
            Y1 = f32v(KB(168) + 16, [L])
            Y2 = f32v(KB(176) + 32, [L])
            VT = [bf16v(KB(184) + 64 + s * KB(1), [512]) for s in range(2)]
            WIN = [bf16v(KB(186.5) + s * KB(6), [8, 384]) for s in range(2)]
            rZ, rY1, rY2, rVT, rWIN = Reg(), Reg(), Reg(), regs(2), regs(2)
            d_win = [P.dsem("d_win0"), P.dsem("d_win1")]
            memset("dve", Z[:, 0:1], 0.0, [rZ])
            memset("dve", Z[:, L + 1:L + 2], 0.0, [rZ])
            win = Wd["hy_w_in"][0]
            for j in range(8):
                s = j % 2
                P.dma("pool", d_win[s], [(WIN[s][:, :, k * 128:(k + 1) * 128],
                                          win[:, k * 1024 + j * 128:k * 1024 + (j + 1) * 128].rearrange("(kc p) n -> p kc n", p=128))
                                         for k in range(3)], writes=[rWIN[s]])
                for k, Y, rY in ((0, Y1, rY1), (1, Y1, rY1), (2, Y2, rY2)):
                    ch = k * 8 + j
                    banks = []
                    for tb in range(4):
                        bk = nb()
                        banks.append(bk)
                        for kc in range(8):
                            mm(PSb(bk), WIN[s][:, kc, k * 128:(k + 1) * 128], A[:, kc, tb * 512:(tb + 1) * 512], kc == 0, kc == 7,
                               [rWIN[s], rA[kc][tb]], [rPS[bk]])
                    for tb in range(4):
                        act(Z[:, 1 + tb * 512:1 + (tb + 1) * 512], PSb(banks[tb]), AF.Identity, [rPS[banks[tb]], rPV], [rZ],
                            scale=1.0, bias=pv("hbin", ch))
                    act(Y, Z[:, 1:L + 1], AF.Identity, [rZ, rPV], [rY], scale=pv("hcw1", ch), bias=pv("hcb", ch))
                    stt("dve", Y, Z[:, 0:L], pv("hcw0", ch), Y, ALU.mult, ALU.add, [rZ, rPV, rY], [rY])
                    if k == 0:
                        for tb in range(4):
                            sl = slice(tb * 512, (tb + 1) * 512)
                            stt("dve", X1[:, j, sl], Z[:, 2 + tb * 512:2 + (tb + 1) * 512], pv("hcw2", ch), Y[:, sl], ALU.mult, ALU.add,
                                [rZ, rPV, rY], [rX1[j][tb]])
                    else:
                        stt("dve", Y, Z[:, 2:L + 2], pv("hcw2", ch), Y, ALU.mult, ALU.add, [rZ, rPV, rY], [rY])
                for tb in range(4):
                    sl = slice(tb * 512, (tb + 1) * 512)
                    tt("dve", VT[tb % 2], Y2[:, sl], Y1[:, sl], ALU.mult, [rY1, rY2], [rVT[tb % 2]])
                    bk = nb()
                    pbf = PSb(bk).bitcast(BF16)
                    for q in range(4):
                        P.op("pe", lambda e, q=q, pbf=pbf, tb=tb: e.transpose(pbf[:, q * 128:(q + 1) * 128], VT[tb % 2][:, q * 128:(q + 1) * 128], IDN),
                             [rVT[tb % 2], rCONST], [rPS[bk]])
                    cp("dve", VTOK[:, tb * 4:tb * 4 + 4, j * 128:(j + 1) * 128], pbf[:, 0:512].rearrange("p (a b) -> p a b", a=4),
                       [rPS[bk]], [rVTOK[tb * 4 + q] for q in range(4)])
            dbg_dump_fm("hy_x1", lambda c, tb: X1[:, c, tb * 512:(tb + 1) * 512], lambda c, tb: [rX1[c][tb]])

        if not dbg_done[0]:
            P.barrier()
            H2 = bf16v(KB(160), [L])
            FW3 = bf16v(KB(164), [2048])
            ABSD = f32v(KB(168), [512])
            DBR = f32v(KB(170), [512])
            WINT = f32v(KB(172), [512])
            HTMP = f32v(KB(174), [512])
            ZT = f32v(KB(176), [L])
            H1 = f32v(KB(184), [L])
            TMPA = f32v(KB(192), [512])
            TMPB = f32v(KB(194), [512])
            FW1 = f32v(KB(196), [64])
            FW2 = f32v(KB(196.25), [64])
            NEGT = f32v(KB(196.5), [16])
            FB = f32v(KB(196.5) + 16, [2])
            rZT, rH1, rH2, rTA, rTB, rFW, rNEGT, rFB, rABSD, rDBR, rWINt, rHT = [Reg() for _ in range(12)]
            P.dma("sp", d_misc, [(ZT[0:33, :], zT_d), (FW1[0:33, :], Wd["hy_f_w1"][0]), (FW2[0:64, :], Wd["hy_f_w2"][0]),
                                 (NEGT, negt_d)], writes=[rZT, rFW, rNEGT])
            P.dma("pool", d_w, [(FW3[0:64, :], Wd["hy_f_w3"][0])], writes=[rFW])
            tt("dve", FB[0:64, 0:1], pv("fb1")[0:64, :], pv("ffr1")[0:64, :], ALU.mult, [rPV], [rFB])
            tt("dve", FB[0:64, 1:2], pv("fb2")[0:64, :], pv("ffr2")[0:64, :], ALU.mult, [rPV], [rFB])

            def sin_layer(lhsT, rhs_fn, rrhs, frq, fbcol, out_fn, rout):
                for tb in range(4):
                    sl = slice(tb * 512, (tb + 1) * 512)
                    bk = nb()
                    mm(PSb(bk)[0:64, :], lhsT, rhs_fn(sl), True, True, [rFW, rrhs], [rPS[bk]])
                    act(TMPA[0:64, :], PSb(bk)[0:64, :], AF.Identity, [rPS[bk], rPV, rFB], [rTA], scale=frq, bias=fbcol)
                    ts("dve", TMPB[0:64, :], TMPA[0:64, :], 1.0 / (2 * math.pi), MAGIC, ALU.mult, ALU.add, [rTA], [rTB])
                    ts("dve", TMPB[0:64, :], TMPB[0:64, :], MAGIC, -2.0 * math.pi, ALU.subtract, ALU.mult, [rTB], [rTB])
                    tt("dve", TMPA[0:64, :], TMPA[0:64, :], TMPB[0:64, :], ALU.add, [rTA, rTB], [rTA])
                    ts("dve", TMPA[0:64, :], TMPA[0:64, :], 3.1415925, -3.1415925, ALU.min, ALU.max, [rTA], [rTA])
                    act(out_fn(sl), TMPA[0:64, :], AF.Sin, [rTA], [rout])

            sin_layer(FW1[0:33, :], lambda sl: ZT[0:33, sl], rZT, pv("ffr1")[0:64, :], FB[0:64, 0:1], lambda sl: H1[0:64, sl], rH1)
            sin_layer(FW2[0:64, :], lambda sl: H1[0:64, sl], rH1, pv("ffr2")[0:64, :], FB[0:64, 1:2], lambda sl: H2[0:64, sl], rH2)
            P.barrier()

            HFB = bf16v(KB(64), [16, 2, 512])
            rHFB = regs(16)
            Y = bf16v(0, [32, D])
            rY = regs(32, 2)
            FWS = [bf16v(KB(176) + s * KB(4), [16, 128]) for s in range(2)]
            rFWS = regs(2)
            d_fw = [P.dsem("d_fw0"), P.dsem("d_fw1")]
            MT = [f32v(KB(184) + s * KB(2), [512]) for s in range(6)]
            rMT = regs(6)
            HTM2 = MT[5]
            rHT2 = rMT[5]
            fwk = 0

            for cb in range(2):
                c0 = cb * 512
                P.dma("sp", d_misc, [(DBR[0:1, :], rows_d[1:2, c0:c0 + 512]),
                                     (ABSD, rows_d[0:1, c0:c0 + 512].to_broadcast([128, 512]))], writes=[rDBR, rABSD])
                act(ABSD, ABSD, AF.Abs, [rABSD], [rABSD])
                for st_ in range(16):
                    bk, bk2 = nb(), nb()
                    hs = H2[0:64, st_ * 128:(st_ + 1) * 128]
                    mm(PSb(bk), hs, FW3[0:64, c0:c0 + 512], True, True, [rH2, rFW], [rPS[bk]])
                    mm(PSb(bk2), hs, FW3[0:64, 1024 + c0:1024 + c0 + 512], True, True, [rH2, rFW], [rPS[bk2]])
                    act(WINT, ABSD, AF.Exp, [rABSD, rNEGT], [rWINt], scale=NEGT[:, st_:st_ + 1])
                    tt("dve", HTMP, PSb(bk), WINT, ALU.mult, [rPS[bk], rWINt], [rHT])
                    tt("dve", HTM2, PSb(bk2), WINT, ALU.mult, [rPS[bk2], rWINt], [rHT2])
                    if st_ == 0:
                        tt("dve", HTMP[0:1, :], HTMP[0:1, :], DBR[0:1, :], ALU.add, [rHT, rDBR], [rHT])
                        memset("dve", HTM2[0:1, :], 0.0, [rHT2])
                    tt("dve", HFB[:, st_, 0, :], HTMP, HTM2, ALU.add, [rHT, rHT2], [rHFB[st_]])
                    tt("dve", HFB[:, st_, 1, :], HTM2, HTMP, ALU.subtract, [rHT, rHT2], [rHFB[st_]])
                for c in range(16):
                    pb = {}
                    for part in range(2):
                        fm = part * 16 + c
                        s = fwk % 2
                        fwk += 1
                        P.dma("sp", d_fw[s], [(FWS[s].rearrange("p a b -> p (a b)"), fw_d[fm])], writes=[rFWS[s]])
                        bv, bf_ = nb(), nb()
                        pb[part] = (bv, bf_)
                        for tc in range(16):
                            mm(PSb(bv), FWS[s][:, tc, :], VTOK[:, tc, c0:c0 + 512], tc == 0, tc == 15, [rFWS[s], rVTOK[tc]], [rPS[bv]])
                        for tc in range(16):
                            mm(PSb(bf_), FWS[s][:, tc, :], HFB[:, tc, part, :], tc == 0, tc == 15, [rFWS[s], rHFB[tc]], [rPS[bf_]])
                    (vc, kre_b), (vs, kim_b) = pb[0], pb[1]
                    KRE, KIM, M1, M2 = MT[0:4]
                    rkre, rkim, m1, m2 = rMT[0:4]
                    act(KRE, PSb(kre_b), AF.Copy, [rPS[kre_b]], [rkre])
                    act(KIM, PSb(kim_b), AF.Copy, [rPS[kim_b]], [rkim])
                    tt("dve", M1, PSb(vc), KRE, ALU.mult, [rPS[vc], rkre], [m1])
                    tt("dve", M2, PSb(vs), KIM, ALU.mult, [rPS[vs], rkim], [m2])
                    tt("dve", Y[:, c, c0:c0 + 512], M1, M2, ALU.add, [m1, m2], [rY[c][cb]])
                    tt("dve", M1, PSb(vc), KIM, ALU.mult, [rPS[vc], rkim, m1], [m1])
                    tt("dve", M2, PSb(vs), KRE, ALU.mult, [rPS[vs], rkre, m2], [m2])
                    tt("dve", Y[:, 16 + c, c0:c0 + 512], M1, M2, ALU.subtract, [m1, m2], [rY[16 + c][cb]])

            P.barrier()
            P.dma("sp", d_x, [(X2[:, c, :], xpark[:, c * L:(c + 1) * L]) for c in range(8)], writes=allr(rX2))
            IVS = [bf16v(KB(64) + s * KB(16), [32, 256]) for s in range(2)]
            rIVS = regs(2)
            d_iv = [P.dsem("d_iv0"), P.dsem("d_iv1")]
            for tq in range(8):
                s = tq % 2
                P.dma("sp", d_iv[s], [(IVS[s].rearrange("p a b -> p (a b)"), iv_d[tq])], writes=[rIVS[s]])
                for chc in range(8):
                    bk = nb()
                    for kc in range(32):
                        mm(PSb(bk)[:, 0:256], Y[:, kc, chc * 128:(chc + 1) * 128], IVS[s][:, kc, :], kc == 0, kc == 31,
                           [rY[kc][chc // 4], rIVS[s]], [rPS[bk]])
                    xs = X1[:, chc, tq * 256:(tq + 1) * 256]
                    stt("dve", xs, PSb(bk)[:, 0:256], 1.0 / 2048.0, xs, ALU.mult, ALU.mult, [rPS[bk], rX1[chc][tq // 2]], [rX1[chc][tq // 2]])
            dbg_dump_fm("hy_y", lambda c, tb: X1[:, c, tb * 512:(tb + 1) * 512], lambda c, tb: [rX1[c][tb]])

        if not dbg_done[0]:
            P.barrier()
            WOUT = bf16v(KB(64), [8, 1024])
            rWOUT = Reg()
            TO = [f32v(KB(80) + s * KB(2), [512]) for s in range(2)]
            rTO = regs(2)
            P.dma("pool", d_w, [(WOUT, Wd["hy_w_out"][0].rearrange("(kc p) n -> p kc n", p=128))], writes=[rWOUT])
            k = 0
            for m in range(8):
                for tb in range(4):
                    bk = nb()
                    for kc in range(8):
                        mm(PSb(bk), WOUT[:, kc, m * 128:(m + 1) * 128], X1[:, kc, tb * 512:(tb + 1) * 512], kc == 0, kc == 7,
                           [rWOUT, rX1[kc][tb]], [rPS[bk]])
                    s = k % 2
                    k += 1
                    act(TO[s], PSb(bk), AF.Identity, [rPS[bk], rMODX[1], rDER[1]], [rTO[s]], scale=MODX[:, 1, 16 + m:17 + m], bias=DER[:, 1, 2, m:m + 1])
                    xs = X2[:, m, tb * 512:(tb + 1) * 512]
                    tt("dve", xs, xs, TO[s], ALU.add, [rTO[s], rX2[m][tb]], [rX2[m][tb]])
            dbg_dump_fm("xmix1", lambda c, tb: X2[:, c, tb * 512:(tb + 1) * 512], lambda c, tb: [rX2[c][tb]])

        if not dbg_done[0]:
            ffn(1, X2, rX2, A2, rA2, 0, KB(44), KB(77))
            dbg_dump_fm("xffn1", lambda c, tb: X2[:, c, tb * 512:(tb + 1) * 512], lambda c, tb: [rX2[c][tb]])

        if not dbg_done[0]:
            P.barrier()
            OS = [f32v(KB(16) + s * KB(2), [512]) for s in range(4)]
            rOS = regs(4)
            kk = [0]

            def out_final(c, tb, T, rT):
                s = kk[0] % 4
                kk[0] += 1
                act(OS[s], T, AF.Identity, [rT, rPV], [rOS[s]], scale=pv("gfin", c), bias=0.0)
                P.dma("sp", d_out, out_pairs(c * 128, 128, tb * 512, 512, OS[s]), reads=[rOS[s]])

            rmsnorm_fm(lambda c, tb: X2[:, c, tb * 512:(tb + 1) * 512], lambda c, tb: [rX2[c][tb]], 8, D, 512, 4, out_final, 0)

        P.finish()
    return nc


_NC_CACHE = {}


def _get_nc(stage="all"):
    if stage not in _NC_CACHE:
        _NC_CACHE[stage] = build(stage)
    return _NC_CACHE[stage]


def make_in_maps(inp):
    cst = _consts()
    shared = {n: np.ascontiguousarray(np.asarray(inp[n], np.float32)) for n in WEIGHT_NAMES}
    rows = np.ascontiguousarray(np.stack([np.asarray(inp["hy_decay"][0], np.float32), np.asarray(inp["hy_d_bias"][0], np.float32)]))
    big = dict(shared)
    big.update({k: cst[k] for k in ("tbl", "fw", "iv")})
    small = {k: cst[k] for k in cst if k not in big}
    maps = []
    for b in range(NCORES):
        m = dict(small)
        for k, v in big.items():
            tag = np.full(v.shape[:-2] + (1, v.shape[-1]), b, dtype=v.dtype)
            m[k] = np.concatenate([v, tag], axis=-2)
        m["rows"] = rows
        m["xT"] = np.ascontiguousarray(np.asarray(inp["x"][b], np.float32).T)
        m["ctxT"] = np.ascontiguousarray(np.asarray(inp["ctx"][b], np.float32).T)
        m["pvec"] = _pack_pvec(inp, b)
        maps.append(m)
    return maps


def gather_out(r):
    return np.ascontiguousarray(np.concatenate([r[f"o{k}"] for k in range(16)], axis=0).T)


def kernel(**inputs):
    inp = {k: np.asarray(v) for k, v in inputs.items()}
    nc = _get_nc("all")
    res = run_bass_kernel_spmd(nc, make_in_maps(inp), core_ids=list(range(NCORES)))
    out = np.stack([gather_out(r) for r in res.results], axis=0)
    return out.astype(np.float32)
```

```python
import contextlib
import math
import numpy as np
import ml_dtypes
import concourse.bass as bass
import concourse.mybir as mybir
from concourse.bass_utils import run_bass_kernel_spmd

F32 = mybir.dt.float32
BF16 = mybir.dt.bfloat16
ALU = mybir.AluOpType
AF = mybir.ActivationFunctionType

D = 1024
L = 2048
LC = 256
LK = LC + L
NH = 16
DFF = 2816
EPS = 1e-6
NCORES = 8
MAGIC = 12582912.0


class Reg:
    __slots__ = ("w", "rd")

    def __init__(self):
        self.w = None
        self.rd = []


def regs(*shape):
    if len(shape) == 1:
        return [Reg() for _ in range(shape[0])]
    return [regs(*shape[1:]) for _ in range(shape[0])]


class DSem:
    def __init__(self, sem):
        self.sem = sem
        self.count = 0


class Prog:
    ENG = ("pe", "act", "dve", "pool", "sp")

    def __init__(self, nc, stack):
        self.nc = nc
        self.stack = stack
        self.ops = {e: [] for e in self.ENG}
        self.cnt = {e: 0 for e in self.ENG}
        self.seen = {e: {} for e in self.ENG}
        self.esem = {e: stack.enter_context(nc.semaphore("s_" + e)) for e in self.ENG}
        self.dsems = []

    def dsem(self, name):
        d = DSem(self.stack.enter_context(self.nc.semaphore(name)))
        self.dsems.append(d)
        return d

    def _deps(self, eng, reads, writes, skip_self):
        need = {}

        def add(ev):
            key, val = ev
            if skip_self and key == eng:
                return
            if self.seen[eng].get(key, 0) >= val:
                return
            if need.get(key, 0) < val:
                need[key] = val

        for r in reads:
            if r.w is not None:
                add(r.w)
        for w in writes:
            if w.w is not None:
                add(w.w)
            for ev in w.rd:
                add(ev)
        waits = []
        for key, val in need.items():
            self.seen[eng][key] = val
            waits.append((self.esem[key] if isinstance(key, str) else key.sem, val))
        return waits

    def _commit(self, ev, reads, writes):
        for r in reads:
            if len(r.rd) > 24:
                r.rd = r.rd[-24:] if False else r.rd
            r.rd.append(ev)
        for w in writes:
            w.w = ev
            w.rd = []

    def op(self, eng, fn, reads=(), writes=()):
        waits = self._deps(eng, reads, writes, eng == "pe")
        self.cnt[eng] += 1
        ev = (eng, self.cnt[eng])
        sem = self.esem[eng]

        def emit(e):
            for s, v in waits:
                e.wait_ge(s, v)
            fn(e).then_inc(sem, 1)

        self.ops[eng].append(emit)
        self._commit(ev, reads, writes)

    def dma(self, q, dsem, pairs, reads=(), writes=()):
        waits = self._deps(q, reads, writes, False)
        dsem.count += 16 * len(pairs)
        ev = (dsem, dsem.count)
        sem = dsem.sem

        def emit(e):
            for s, v in waits:
                e.wait_ge(s, v)
            for out, in_ in pairs:
                e.dma_start(out=out, in_=in_).then_inc(sem, 16)

        self.ops[q].append(emit)
        self._commit(ev, reads, writes)

    def barrier(self):
        for e in self.ENG:
            waits = []
            for e2 in self.ENG:
                v = self.cnt[e2]
                if v > self.seen[e].get(e2, 0):
                    self.seen[e][e2] = v
                    waits.append((self.esem[e2], v))
            for d in self.dsems:
                if d.count > self.seen[e].get(d, 0):
                    self.seen[e][d] = d.count
                    waits.append((d.sem, d.count))

            def emit(eh, waits=waits):
                for s, v in waits:
                    eh.wait_ge(s, v)

            self.ops[e].append(emit)

    def finish(self):
        self.barrier()
        ops = self.ops
        with self.nc.Block() as block:
            @block.tensor
            def _(e):
                for f in ops["pe"]:
                    f(e)

            @block.scalar
            def _(e):
                for f in ops["act"]:
                    f(e)

            @block.vector
            def _(e):
                for f in ops["dve"]:
                    f(e)

            @block.gpsimd
            def _(e):
                for f in ops["pool"]:
                    f(e)

            @block.sync
            def _(e):
                for f in ops["sp"]:
                    f(e)


def _layout():
    cols = {}
    off = [0]

    def add(name, n):
        cols[name] = off[0]
        off[0] += n

    add("c", 8)
    add("cc", 8)
    for i in range(2):
        add(f"modb{i}", 48)
    for i in range(2):
        add(f"gmix{i}", 8)
        add(f"gffn{i}", 8)
    add("gfin", 8)
    add("gq", 4)
    add("gkv", 2)
    add("hbin", 24)
    for j in range(3):
        add(f"hcw{j}", 24)
    add("hcb", 24)
    add("hbout", 8)
    for i in range(2):
        for j in range(3):
            add(f"fcw{i}{j}", 44)
        add(f"fcb{i}", 44)
    for n in ("fb1", "ffr1", "fb2", "ffr2"):
        add(n, 1)
    return cols, off[0]


PCOL, PR = _layout()


def _colsof(v):
    v = np.asarray(v, np.float32).reshape(-1)
    if v.size < 128:
        v = np.concatenate([v, np.zeros(128 - v.size, np.float32)])
    return v.reshape(-1, 128).T


def _pack_pvec(inp, b):
    pv = np.zeros((128, PR), np.float32)

    def put(name, v):
        c = _colsof(v)
        pv[:, PCOL[name]:PCOL[name] + c.shape[1]] = c

    put("c", inp["c"][b])
    put("cc", inp["c_ctx"])
    for i in range(2):
        put(f"modb{i}", inp["mod_b"][i])
        put(f"gmix{i}", inp["norm_mix_g"][i])
        put(f"gffn{i}", inp["norm_ffn_g"][i])
    put("gfin", inp["final_g"])
    put("gq", inp["mla_g_q"][0])
    put("gkv", inp["mla_g_kv"][0])
    put("hbin", inp["hy_b_in"][0])
    for j in range(3):
        put(f"hcw{j}", inp["hy_conv_w"][0, j])
    put("hcb", inp["hy_conv_b"][0])
    put("hbout", inp["hy_b_out"][0])
    for i in range(2):
        for j in range(3):
            put(f"fcw{i}{j}", inp["ffn_conv_w"][i, j])
        put(f"fcb{i}", inp["ffn_conv_b"][i])
    put("fb1", inp["hy_f_b1"][0])
    put("ffr1", inp["hy_f_freq1"][0])
    put("fb2", inp["hy_f_b2"][0])
    put("ffr2", inp["hy_f_freq2"][0])
    return pv


_CONST = {}


def _consts():
    if _CONST:
        return _CONST
    f32 = np.float32
    rows = L // 64
    row = np.repeat(np.arange(rows, dtype=f32), 64)
    col = np.tile(np.arange(64, dtype=f32), rows)
    half = 16
    inv = (f32(10000.0) ** (-np.arange(0, half, 2, dtype=f32) / f32(half))).astype(f32)
    ar = row[:, None] * inv
    ac = col[:, None] * inv
    ang = np.concatenate([ar, ar, ac, ac], axis=-1).astype(f32)
    tbl = np.zeros((128, 2, L), f32)
    tbl[64:96, 0, :] = np.cos(ang).T
    tbl[64:96, 1, :] = np.sin(ang).T
    t = np.linspace(0.0, 1.0, L, dtype=f32)
    w = (f32(2.0 * math.pi) * np.arange(L, dtype=f32) / f32(L)).astype(f32)
    fr = np.linspace(1e-4, 15, 16, dtype=f32)
    z = np.concatenate([t[:, None], np.cos(w[:, None] * fr), -np.sin(w[:, None] * fr)], axis=-1).astype(f32)
    zT = np.ascontiguousarray(z.T)
    negt = np.ascontiguousarray((-t).reshape(16, 128).T)
    tt = np.arange(L, dtype=np.float64)
    ff = np.arange(L, dtype=np.float64) + 0.5
    th = 2.0 * np.pi * np.outer(tt, ff) / 4096.0
    C = np.cos(th)
    S = np.sin(th)
    fwd = np.concatenate([C, S], axis=1)
    FWm = fwd.reshape(16, 128, 32, 128).transpose(2, 1, 0, 3)
    FWm = np.ascontiguousarray(FWm).reshape(32, 128, 2048).astype(ml_dtypes.bfloat16)
    inv_m = np.concatenate([C.T, -S.T], axis=0)
    IVm = inv_m.reshape(32, 128, 8, 256).transpose(2, 1, 0, 3)
    IVm = np.ascontiguousarray(IVm).reshape(8, 128, 32 * 256).astype(ml_dtypes.bfloat16)
    ident = np.eye(128, dtype=f32).astype(ml_dtypes.bfloat16)
    _CONST.update(tbl=tbl, zT=zT, negt=negt, fw=FWm, iv=IVm, ident=ident)
    return _CONST


WEIGHT_NAMES = ["mod_w", "mla_w_dq", "mla_w_uq", "mla_w_dkv", "mla_w_uk", "mla_w_uv", "mla_w_o",
                "hy_w_in", "hy_f_w1", "hy_f_w2", "hy_f_w3", "hy_w_out", "ffn_w_up", "ffn_w_down"]
WEIGHT_SHAPES = {
    "mod_w": (2, 1024, 6144), "mla_w_dq": (1, 1024, 512), "mla_w_uq": (1, 512, 1536),
    "mla_w_dkv": (1, 1024, 288), "mla_w_uk": (1, 256, 1024), "mla_w_uv": (1, 256, 1024),
    "mla_w_o": (1, 1024, 1024), "hy_w_in": (1, 1024, 3072), "hy_f_w1": (1, 33, 64),
    "hy_f_w2": (1, 64, 64), "hy_f_w3": (1, 64, 2048), "hy_w_out": (1, 1024, 1024),
    "ffn_w_up": (2, 1024, 5632), "ffn_w_down": (2, 2816, 1024),
}


def KB(x):
    return int(round(x * 256))


def build(stage="all"):
    nc = bass.Bass("TRN2", target_bir_lowering=False)

    def din(name, shape, dt=F32):
        return nc.dram_tensor(name, list(shape), dt, kind="ExternalInput").ap()

    xT = din("xT", [D, L])
    ctxT = din("ctxT", [D, LC])
    pvec = din("pvec", [128, PR])
    rows_d = din("rows", [2, D])
    tbl_d = din("tbl", [128, 3, L])[:, 0:2, :]
    zT_d = din("zT", [33, L])
    negt_d = din("negt", [128, 16])
    fw_d = din("fw", [32, 129, 2048], BF16)[:, 0:128, :]
    iv_d = din("iv", [8, 129, 32 * 256], BF16)[:, 0:128, :]
    ident_d = din("ident", [128, 128], BF16)
    Wd = {}
    for n in WEIGHT_NAMES:
        a, r, c = WEIGHT_SHAPES[n]
        Wd[n] = din(n, [a, r + 1, c])[:, 0:r, :]
    outs_d = [nc.dram_tensor(f"o{k}", [64, L], F32, kind="ExternalOutput").ap() for k in range(16)]

    def out_pairs(row0, nrows, col0, ncols, src, src_p0=0):
        pairs = []
        r = row0
        while r < row0 + nrows:
            k, off = divmod(r, 64)
            n = min(64 - off, row0 + nrows - r)
            p0 = src_p0 + (r - row0)
            pairs.append((outs_d[k][off:off + n, col0:col0 + ncols], src[p0:p0 + n]))
            r += n
        return pairs
    xpark = nc.dram_tensor("xpark", [128, 8 * L], F32, kind="Internal").ap()

    with contextlib.ExitStack() as st:
        P = Prog(nc, st)
        NW = KB(207.5)
        AR = st.enter_context(nc.sbuf_tensor("arena", [128, NW], F32))
        PS = st.enter_context(nc.psum_tensor("ps", [128, 8, 512], F32))

        def _shape(ap, shape):
            if len(shape) == 1:
                return ap
            if len(shape) == 2:
                return ap.rearrange("p (a b) -> p a b", a=shape[0])
            if len(shape) == 3:
                return ap.rearrange("p (a b c) -> p a b c", a=shape[0], b=shape[1])
            raise ValueError

        def f32v(off, shape):
            n = int(np.prod(shape))
            assert off + n <= NW, (off, n)
            return _shape(AR[:, off:off + n], shape)

        def bf16v(off, shape):
            n = int(np.prod(shape))
            assert n % 2 == 0 and off + n // 2 <= NW, (off, n)
            return _shape(AR[:, off:off + n // 2].bitcast(BF16), shape)

        rPS = regs(8)
        bank_ctr = [0]

        def nb():
            b = bank_ctr[0] % 6
            bank_ctr[0] += 1
            return b

        lbank_ctr = [0]

        def nbl():
            b = 6 + lbank_ctr[0] % 2
            lbank_ctr[0] += 1
            return b

        def PSb(b):
            return PS[:, b, :]

        def mm(out, lhsT, rhs, start, stop, reads, writes):
            P.op("pe", lambda e: e.matmul(out, lhsT=lhsT, rhs=rhs, start=start, stop=stop), reads, writes)

        def act(out, in_, func, reads, writes, eng="act", **kw):
            P.op(eng, lambda e: e.activation(out=out, in_=in_, func=func, **kw), reads, writes)

        def tt(eng, out, in0, in1, op, reads, writes):
            P.op(eng, lambda e: e.tensor_tensor(out=out, in0=in0, in1=in1, op=op), reads, writes)

        def ts(eng, out, in0, s1, s2, op0, op1, reads, writes):
            P.op(eng, lambda e: e.tensor_scalar(out=out, in0=in0, scalar1=s1, scalar2=s2, op0=op0, op1=op1), reads, writes)

        def stt(eng, out, in0, scalar, in1, op0, op1, reads, writes):
            P.op(eng, lambda e: e.scalar_tensor_tensor(out=out, in0=in0, scalar=scalar, in1=in1, op0=op0, op1=op1), reads, writes)

        def cp(eng, out, in_, reads, writes):
            P.op(eng, lambda e: e.tensor_copy(out=out, in_=in_), reads, writes)

        def memset(eng, ap, val, writes):
            P.op(eng, lambda e: e.memset(ap, val), (), writes)

        def recip(out, in_, reads, writes):
            P.op("dve", lambda e: e.reciprocal(out=out, in_=in_), reads, writes)

        o = KB(200)
        PV = f32v(o, [PR]); o += PR
        MODX = f32v(o, [2, 48]); o += 96
        MODC = f32v(o, [48]); o += 48
        DER = f32v(o, [2, 4, 8]); o += 64
        DERC = f32v(o, [8]); o += 8
        ONESF = f32v(o, [128]); o += 128
        o = (o + 15) // 16 * 16
        SC = bf16v(o, [8, 2]); o += 8
        ONESB = bf16v(o, [128]); o += 64
        IDN = bf16v(o, [128]); o += 64
        assert o <= NW
        rPV, rMODX, rMODC, rDER, rSC, rCONST = Reg(), regs(2), Reg(), regs(2), Reg(), Reg()

        def pv(name, j=0, n=1):
            return PV[:, PCOL[name] + j:PCOL[name] + j + n]

        d_misc = P.dsem("d_misc")
        d_x = P.dsem("d_x")
        d_out = P.dsem("d_out")

        X = f32v(0, [8, L])
        rX = regs(8, 4)
        A = bf16v(KB(64), [8, L])
        rA = regs(8, 4)

        def allr(rr):
            return [r for row in rr for r in row]

        P.dma("sp", d_misc, [(PV, pvec), (IDN, ident_d)], writes=[rPV, rCONST])
        P.dma("sp", d_x, [(X[:, c, :], xT[c * 128:(c + 1) * 128, :]) for c in range(8)], writes=allr(rX))
        memset("dve", ONESF, 1.0, [rCONST])
        memset("dve", ONESB, 1.0, [rCONST])
        act(SC[:, :, 0], pv("c", 0, 8), AF.Silu, [rPV], [rSC])
        act(SC[:, :, 1], pv("cc", 0, 8), AF.Silu, [rPV], [rSC])

        MWS = [bf16v(KB(96) + s * KB(4), [8, 256]) for s in range(2)]
        rMWS = regs(2)
        d_mws = [P.dsem("d_mw0"), P.dsem("d_mw1")]
        mod_state = {"slot": 0, "slots": MWS}

        def mod_groups(i, g0, g1, with_ctx, gw=256):
            bank = nb()
            npg = gw // 128
            psm = PSb(bank)[:, 0:96].rearrange("p (j v) -> p j v", v=2)
            MW = mod_state["slots"]
            for g in range(g0, g1):
                s = mod_state["slot"] % 2
                mod_state["slot"] += 1
                src = Wd["mod_w"][i, :, g * gw:(g + 1) * gw].rearrange("(kc p) n -> p kc n", p=128)
                P.dma("pool", d_mws[s], [(MW[s][:, :, 0:gw], src)], writes=[rMWS[s]])
                for jj in range(npg):
                    j = g * npg + jj
                    for kc in range(8):
                        mm(psm[:, j, :], MW[s][:, kc, jj * 128:(jj + 1) * 128], SC[:, kc, :], kc == 0, kc == 7,
                           [rMWS[s], rSC], [rPS[bank]])
            j0, j1 = g0 * npg, g1 * npg
            tt("dve", MODX[:, i, j0:j1], psm[:, j0:j1, 0], pv(f"modb{i}", j0, j1 - j0), ALU.add, [rPS[bank], rPV], [rMODX[i]])
            if with_ctx:
                tt("dve", MODC[:, j0:j1], psm[:, j0:j1, 1], pv(f"modb{i}", j0, j1 - j0), ALU.add, [rPS[bank], rPV], [rMODC])

        def mod_derive(i, part):
            if part == 1:
                stt("dve", DER[:, i, 0, :], MODX[:, i, 8:16], 1.0, pv(f"gmix{i}", 0, 8), ALU.add, ALU.mult, [rMODX[i], rPV], [rDER[i]])
                if i == 0:
                    stt("dve", DERC, MODC[:, 8:16], 1.0, pv("gmix0", 0, 8), ALU.add, ALU.mult, [rMODC, rPV], [rDER[i]])
            else:
                stt("dve", DER[:, i, 1, :], MODX[:, i, 32:40], 1.0, pv(f"gffn{i}", 0, 8), ALU.add, ALU.mult, [rMODX[i], rPV], [rDER[i]])
                if i == 1:
                    tt("dve", DER[:, i, 2, :], MODX[:, i, 16:24], pv("hbout", 0, 8), ALU.mult, [rMODX[i], rPV], [rDER[i]])

        def rmsnorm_fm(src, rsrc, nch, Dn, N, ntb, emit_out, tmp_off):
            SQ = [bf16v(tmp_off + s * 256, [512]) for s in range(2)]
            RS = f32v(tmp_off + 512, [512])
            TT = [f32v(tmp_off + 1024 + s * 512, [512]) for s in range(2)]
            rSQ, rRS, rTT = regs(2), Reg(), regs(2)
            for tb in range(ntb):
                bank = nb()
                for c in range(nch):
                    s = c % 2
                    act(SQ[s][:, :N], src(c, tb), AF.Square, rsrc(c, tb), [rSQ[s]])
                    mm(PSb(bank)[:, :N], ONESB, SQ[s][:, :N], c == 0, c == nch - 1, [rSQ[s], rCONST], [rPS[bank]])
                ts("dve", RS[:, :N], PSb(bank)[:, :N], 1.0 / Dn, EPS, ALU.mult, ALU.add, [rPS[bank]], [rRS])
                act(RS[:, :N], RS[:, :N], AF.Sqrt, [rRS], [rRS])
                recip(RS[:, :N], RS[:, :N], [rRS], [rRS])
                for c in range(nch):
                    s = c % 2
                    tt("dve", TT[s][:, :N], src(c, tb), RS[:, :N], ALU.mult, rsrc(c, tb) + [rRS], [rTT[s]])
                    emit_out(c, tb, TT[s][:, :N], rTT[s])

        dbg_done = [False]

        def dbg_dump_fm(name, tile_fn, regs_list):
            if stage != name:
                return
            P.barrier()
            stg = f32v(KB(196), [512])
            rS = Reg()
            for c in range(8):
                for tb in range(4):
                    cp("dve", stg, tile_fn(c, tb), regs_list(c, tb), [rS])
                    P.dma("sp", d_out, out_pairs(c * 128, 128, tb * 512, 512, stg), reads=[rS])
            dbg_done[0] = True

        def dbg_blocks(name, blocks):
            if stage != name:
                return
            P.barrier()
            stg = f32v(KB(196), [512])
            rS = Reg()
            for row0, col0, src in blocks:
                p, n = src.shape[0], src.shape[1]
                bp = src.base_partition() if callable(getattr(src, "base_partition", None)) else 0
                cp("dve", stg[bp:bp + p, 0:n], src, [], [rS])
                P.dma("sp", d_out, out_pairs(row0, p, col0, n, stg[:, 0:n], src_p0=bp), reads=[rS])
            dbg_done[0] = True

        mod_groups(0, 0, 8, True)
        mod_derive(0, 1)

        TBL = f32v(KB(124), [2, L])
        rTBL = Reg()
        XC = f32v(KB(112), [8, LC])
        AC = bf16v(KB(120), [8, LC])
        rXC, rAC = regs(8), regs(8)
        P.dma("sp", d_misc, [(TBL[64:96], tbl_d[64:96])], writes=[rTBL])
        P.dma("sp", d_misc, [(XC[:, c, :], ctxT[c * 128:(c + 1) * 128, :]) for c in range(8)], writes=rXC)

        WDQ = bf16v(KB(170), [8, 512])
        WDKV = bf16v(KB(178), [8, 288])
        WKPE = bf16v(KB(182.5), [8, 96])
        WKROT = bf16v(KB(184), [8, 96])
        rWDQ, rWDKV, rWKPE, rWKROT = Reg(), Reg(), Reg(), Reg()
        d_w = P.dsem("d_w")
        P.dma("pool", d_w, [(WDQ, Wd["mla_w_dq"][0].rearrange("(kc p) n -> p kc n", p=128))], writes=[rWDQ])
        P.dma("pool", d_w, [(WDKV, Wd["mla_w_dkv"][0].rearrange("(kc p) n -> p kc n", p=128))], writes=[rWDKV])

        def build_rot(dst, src, eng, reads, writes):
            def neg(o, i):
                P.op(eng, lambda e: e.tensor_scalar_mul(out=o, in0=i, scalar1=-1.0), reads, writes)
            neg(dst[..., 0:8], src[..., 8:16])
            cp(eng, dst[..., 8:16], src[..., 0:8], reads, writes)
            neg(dst[..., 16:24], src[..., 24:32])
            cp(eng, dst[..., 24:32], src[..., 16:24], reads, writes)

        memset("dve", WKPE, 0.0, [rWKPE])
        memset("dve", WKROT, 0.0, [rWKROT])
        cp("dve", WKPE[:, :, 64:96], WDKV[:, :, 256:288], [rWDKV], [rWKPE])
        build_rot(WKROT[:, :, 64:96], WDKV[:, :, 256:288], "dve", [rWDKV], [rWKROT])

        NT = KB(104)

        def out_A(c, tb, T, rT):
            act(A[:, c, tb * 512:(tb + 1) * 512], T, AF.Identity, [rT, rDER[0], rMODX[0]], [rA[c][tb]],
                scale=DER[:, 0, 0, c:c + 1], bias=MODX[:, 0, c:c + 1])

        rmsnorm_fm(lambda c, tb: X[:, c, tb * 512:(tb + 1) * 512], lambda c, tb: [rX[c][tb]], 8, D, 512, 4, out_A, NT)

        def out_AC(c, tb, T, rT):
            act(AC[:, c, :], T, AF.Identity, [rT, rDER[0], rMODC], [rAC[c]], scale=DERC[:, c:c + 1], bias=MODC[:, c:c + 1])

        rmsnorm_fm(lambda c, tb: XC[:, c, :], lambda c, tb: [rXC[c]], 8, D, LC, 1, out_AC, NT)

        dbg_dump_fm("hx0", lambda c, tb: A[:, c, tb * 512:(tb + 1) * 512], lambda c, tb: [rA[c][tb]])

        CQN = bf16v(KB(140), [4, L])
        rCQN = regs(4)
        CKVN = bf16v(KB(156), [2, LK])
        rCKVN = regs(5)
        KT = bf16v(KB(165.5), [LK])
        rKT = Reg()
        T1 = f32v(KB(195.5), [512])
        T2 = f32v(KB(197.5), [512])
        rT1, rT2 = Reg(), Reg()

        def key_slice(kb):
            return (0, LC) if kb == 0 else (LC + (kb - 1) * 512, LC + kb * 512)

        if not dbg_done[0]:
            for tb in range(4):
                banks = []
                for m in range(4):
                    bk = nb()
                    banks.append(bk)
                    for kc in range(8):
                        mm(PSb(bk), WDQ[:, kc, m * 128:(m + 1) * 128], A[:, kc, tb * 512:(tb + 1) * 512], kc == 0, kc == 7,
                           [rWDQ, rA[kc][tb]], [rPS[bk]])

                def out_cq(c, tb_, T, rT, tb=tb):
                    act(CQN[:, c, tb * 512:(tb + 1) * 512], T, AF.Identity, [rT, rPV], [rCQN[c]], scale=pv("gq", c, 1), bias=0.0)

                rmsnorm_fm(lambda c, tb_, banks=banks: PSb(banks[c]), lambda c, tb_, banks=banks: [rPS[banks[c]]], 4, 512, 512, 1, out_cq, NT)

            for kb in range(5):
                k0, k1 = key_slice(kb)
                N = k1 - k0
                if kb == 0:
                    def rhs(kc):
                        return AC[:, kc, :], [rAC[kc]]
                else:
                    def rhs(kc, kb=kb):
                        return A[:, kc, (kb - 1) * 512:kb * 512], [rA[kc][kb - 1]]
                banks = []
                for m in range(2):
                    bk = nb()
                    banks.append(bk)
                    for kc in range(8):
                        r_ap, r_rg = rhs(kc)
                        mm(PSb(bk)[:, :N], WDKV[:, kc, m * 128:(m + 1) * 128], r_ap, kc == 0, kc == 7, [rWDKV] + r_rg, [rPS[bk]])
                bpe = nb()
                for kc in range(8):
                    r_ap, r_rg = rhs(kc)
                    mm(PSb(bpe)[0:96, :N], WKPE[:, kc, :], r_ap, kc == 0, kc == 7, [rWKPE] + r_rg, [rPS[bpe]])
                if kb > 0:
                    brot = nb()
                    for kc in range(8):
                        r_ap, r_rg = rhs(kc)
                        mm(PSb(brot)[0:96, :N], WKROT[:, kc, :], r_ap, kc == 0, kc == 7, [rWKROT] + r_rg, [rPS[brot]])
                    t0 = (kb - 1) * 512
                    tt("dve", T1[64:96, :], PSb(bpe)[64:96, :], TBL[64:96, 0, t0:t0 + 512], ALU.mult, [rPS[bpe], rTBL], [rT1])
                    tt("dve", T2[64:96, :], PSb(brot)[64:96, :], TBL[64:96, 1, t0:t0 + 512], ALU.mult, [rPS[brot], rTBL], [rT2])
                    tt("dve", KT[64:96, k0:k1], T1[64:96, :], T2[64:96, :], ALU.add, [rT1, rT2], [rKT])
                else:
                    cp("dve", KT[64:96, k0:k1], PSb(bpe)[64:96, :N], [rPS[bpe]], [rKT])

                def out_ckv(c, tb_, T, rT, k0=k0, k1=k1, kb=kb):
                    act(CKVN[:, c, k0:k1], T, AF.Identity, [rT, rPV], [rCKVN[kb]], scale=pv("gkv", c, 1), bias=0.0)

                rmsnorm_fm(lambda c, tb_, banks=banks, N=N: PSb(banks[c])[:, :N], lambda c, tb_, banks=banks: [rPS[banks[c]]],
                           2, 256, N, 1, out_ckv, NT)

            mod_groups(0, 8, 24, False)
            mod_derive(0, 2)
            P.barrier()

            blks = []
            for m in range(2):
                for tb in range(4):
                    blks.append((m * 128, tb * 512, CKVN[:, m, 256 + tb * 512:256 + (tb + 1) * 512]))
            for tb in range(4):
                blks.append((256 + 64, tb * 512, KT[64:96, 256 + tb * 512:256 + (tb + 1) * 512]))
            blks.append((384, 0, CKVN[:, 0, 0:256]))
            blks.append((384, 256, CKVN[:, 1, 0:256]))
            blks.append((384 + 64, 512, KT[64:96, 0:256]))
            blks.append((384, 768, AC[:, 0, :]))
            for m in range(4):
                for tb in range(4):
                    blks.append((512 + m * 128, tb * 512, CQN[:, m, tb * 512:(tb + 1) * 512]))
            dbg_blocks("kvdbg", blks)
        if not dbg_done[0]:
            WUQ = bf16v(KB(170), [4, 1536])
            WUK = bf16v(KB(182), [2, 1024])
            WUV = bf16v(KB(186), [2, 1024])
            ROT = [bf16v(KB(190) + s * KB(0.75), [4, 96]) for s in range(2)]
            KT2 = [KT, bf16v(KB(96), [LK])]
            OAs = [f32v(KB(116.5), [512]), f32v(KB(100.5), [512])]
            RCs = [f32v(KB(118.5), [512]), f32v(KB(102.5), [512])]
            VP = [bf16v(KB(104.5) + s * KB(4.5), [18, 128]) for s in range(2)]
            PT = [bf16v(KB(113.5) + s * KB(1), [512]) for s in range(3)]
            QTB = [bf16v(KB(120.5) + s * KB(1), [512]) for s in range(3)]
            rWUQ, rWUK, rWUV, rROT, rQTB, rVP, rPT, rOA, rRC = Reg(), Reg(), Reg(), regs(2), regs(3), regs(2), regs(3), regs(2), regs(2)
            rKT2 = [rKT, Reg()]
            P.dma("pool", d_w, [(WUQ, Wd["mla_w_uq"][0].rearrange("(kc p) n -> p kc n", p=128))], writes=[rWUQ])
            P.dma("pool", d_w, [(WUK, Wd["mla_w_uk"][0].rearrange("(kc p) n -> p kc n", p=128))], writes=[rWUK])
            P.dma("pool", d_w, [(WUV, Wd["mla_w_uv"][0].rearrange("(kc p) n -> p kc n", p=128))], writes=[rWUV])
            cp("dve", KT2[1][64:96, :], KT[64:96, :], [rKT], [rKT2[1]])
            for s in range(2):
                memset("dve", ROT[s], 0.0, [rROT[s]])
                memset("dve", VP[s], 0.0, [rVP[s]])
            memset("dve", VP[0][:, :, 64:65], 1.0, [rVP[0]])
            memset("dve", VP[1][:, :, 0:1], 1.0, [rVP[1]])
            WUQh = WUQ.rearrange("p k (h d) -> p k h d", h=NH)
            scale = 1.0 / math.sqrt(96.0)

            def build_head_k(h):
                par = h % 2
                build_rot(ROT[par][:, :, 64:96], WUQh[:, :, h, 64:96], "dve", [rWUQ], [rROT[par]])
                for kb in range(5):
                    k0, k1 = key_slice(kb)
                    bk = nb()
                    for kc in range(2):
                        mm(PSb(bk)[0:64, :k1 - k0], WUK[:, kc, h * 64:(h + 1) * 64], CKVN[:, kc, k0:k1], kc == 0, kc == 1,
                           [rWUK, rCKVN[kb]], [rPS[bk]])
                    cp("dve", KT2[par][0:64, k0:k1], PSb(bk)[0:64, :k1 - k0], [rPS[bk]], [rKT2[par]])

            def build_head_v(h):
                par = h % 2
                po = 64 * par
                for g in range(5):
                    kts = list(range(g * 4, min(18, g * 4 + 4)))
                    bk = nb()
                    for i, kt in enumerate(kts):
                        kb = 0 if kt < 2 else 1 + (kt - 2) // 4
                        for kc in range(2):
                            mm(PSb(bk)[:, i * 64:(i + 1) * 64], CKVN[:, kc, kt * 128:(kt + 1) * 128], WUV[:, kc, h * 64:(h + 1) * 64],
                               kc == 0, kc == 1, [rWUV, rCKVN[kb]], [rPS[bk]])
                    n = len(kts)
                    cp("dve", VP[par][:, kts[0]:kts[0] + n, po:po + 64],
                       PSb(bk)[:, 0:n * 64].rearrange("p (a b) -> p a b", a=n), [rPS[bk]], [rVP[par]])

            def build_q(i):
                h, qb = divmod(i, 4)
                par = h % 2
                q0 = qb * 512
                qt, rq = QTB[i % 3], rQTB[i % 3]
                bq, br = nb(), nb()
                for kc in range(4):
                    mm(PSb(bq)[0:96, :], WUQh[:, kc, h, :], CQN[:, kc, q0:q0 + 512], kc == 0, kc == 3, [rWUQ, rCQN[kc]], [rPS[bq]])
                for kc in range(4):
                    mm(PSb(br)[0:96, :], ROT[par][:, kc, :], CQN[:, kc, q0:q0 + 512], kc == 0, kc == 3, [rROT[par], rCQN[kc]], [rPS[br]])
                cp("dve", qt[0:64, :], PSb(bq)[0:64, :], [rPS[bq]], [rq])
                tt("dve", T1[64:96, :], PSb(bq)[64:96, :], TBL[64:96, 0, q0:q0 + 512], ALU.mult, [rPS[bq], rTBL], [rT1])
                tt("dve", T2[64:96, :], PSb(br)[64:96, :], TBL[64:96, 1, q0:q0 + 512], ALU.mult, [rPS[br], rTBL], [rT2])
                tt("dve", qt[64:96, :], T1[64:96, :], T2[64:96, :], ALU.add, [rT1, rT2], [rq])

            bo_of = {}

            def norm_part1(i):
                h, qb = divmod(i, 4)
                po = 64 * (h % 2)
                srow = 64 - po
                OA, RC = OAs[i % 2], RCs[i % 2]
                cp("dve", OA, PSb(bo_of[i]), [rPS[bo_of[i]]], [rOA[i % 2]])
                recip(RC[srow:srow + 1, :], OA[srow:srow + 1, :], [rOA[i % 2]], [rRC[i % 2]])

            def norm_part2(i):
                h, qb = divmod(i, 4)
                po = 64 * (h % 2)
                srow = 64 - po
                q0 = qb * 512
                OA, RC = OAs[i % 2], RCs[i % 2]
                bb = nb()
                mm(PSb(bb), ONESF[srow:srow + 1, :], RC[srow:srow + 1, :], True, True, [rRC[i % 2], rCONST], [rPS[bb]])
                tt("dve", A[po:po + 64, h // 2, q0:q0 + 512], OA[po:po + 64, :], PSb(bb)[po:po + 64, :], ALU.mult,
                   [rOA[i % 2], rPS[bb]], [rA[h // 2][qb]])

            mod_state["slots"] = [bf16v(KB(191.5) + s * KB(2), [8, 128]) for s in range(2)]
            mod1_next = [0]
            build_head_k(0)
            build_head_v(0)
            build_q(0)
            NBLK = NH * 4
            LA = 2
            for i in range(NBLK):
                h, qb = divmod(i, 4)
                par = h % 2
                qt, rq = QTB[i % 3], rQTB[i % 3]
                bo = nbl()
                bo_of[i] = bo
                for it in range(18 + LA):
                    if it < 18:
                        kt = it
                        bs = nb()
                        mm(PSb(bs), KT2[par][0:96, kt * 128:(kt + 1) * 128], qt[0:96, :], True, True, [rKT2[par], rq], [rPS[bs]])
                        act(PT[kt % 3], PSb(bs), AF.Exp, [rPS[bs]], [rPT[kt % 3]], scale=scale)
                    if it >= LA:
                        kt = it - LA
                        mm(PSb(bo), VP[par][:, kt, :], PT[kt % 3], kt == 0, kt == 17, [rVP[par], rPT[kt % 3]], [rPS[bo]])
                    if it == 3 and i >= 1:
                        norm_part1(i - 1)
                    if it == 6 and i + 1 < NBLK and (i + 1) % 4 != 0:
                        build_q(i + 1)
                    if it == 8 and qb == 1 and h + 1 < NH:
                        build_head_k(h + 1)
                    if it == 8 and qb == 2 and h + 1 < NH:
                        build_head_v(h + 1)
                    if it == 6 and qb == 3 and i + 1 < NBLK:
                        build_q(i + 1)
                    if it == 13 and i >= 1:
                        norm_part2(i - 1)
                    if it == 16 and mod1_next[0] < 48:
                        mod_groups(1, mod1_next[0], mod1_next[0] + 1, False, gw=128)
                        mod1_next[0] += 1
            norm_part1(NBLK - 1)
            norm_part2(NBLK - 1)
            assert mod1_next[0] == 48
            mod_derive(1, 1)
            mod_derive(1, 2)

            dbg_dump_fm("ox", lambda c, tb: A[:, c, tb * 512:(tb + 1) * 512], lambda c, tb: [rA[c][tb]])

        if not dbg_done[0]:
            P.barrier()
            WO = bf16v(KB(170), [8, 1024])
            rWO = Reg()
            P.dma("pool", d_w, [(WO, Wd["mla_w_o"][0].rearrange("(kc p) n -> p kc n", p=128))], writes=[rWO])
            for m in range(8):
                for tb in range(4):
                    bk = nb()
                    for kc in range(8):
                        mm(PSb(bk), WO[:, kc, m * 128:(m + 1) * 128], A[:, kc, tb * 512:(tb + 1) * 512], kc == 0, kc == 7,
                           [rWO, rA[kc][tb]], [rPS[bk]])
                    xs = X[:, m, tb * 512:(tb + 1) * 512]
                    stt("dve", xs, PSb(bk), MODX[:, 0, 16 + m:17 + m], xs, ALU.mult, ALU.add, [rPS[bk], rMODX[0], rX[m][tb]], [rX[m][tb]])
            dbg_dump_fm("xmix0", lambda c, tb: X[:, c, tb * 512:(tb + 1) * 512], lambda c, tb: [rX[c][tb]])

        def ffn(i, Xv, rXv, Av, rAv, Uo, To, Wo_, hook=None):
            P.barrier()

            def out_Af(c, tb, T, rT):
                act(Av[:, c, tb * 512:(tb + 1) * 512], T, AF.Identity, [rT, rDER[i], rMODX[i]], [rAv[c][tb]],
                    scale=DER[:, i, 1, c:c + 1], bias=MODX[:, i, 24 + c:25 + c])

            rmsnorm_fm(lambda c, tb: Xv[:, c, tb * 512:(tb + 1) * 512], lambda c, tb: [rXv[c][tb]], 8, D, 512, 4, out_Af, To + KB(24) + 64)
            U = bf16v(Uo, [11, L])
            rU = regs(11, 4)
            Zs = [f32v(To, [L + 2]), f32v(To + KB(8) + 16, [L + 2])]
            YA = f32v(To + KB(16) + 32, [L])
            YG = f32v(To + KB(24) + 48, [L])
            rZ, rYA, rYG = regs(2, 4), regs(4), regs(4)
            WUP = [bf16v(Wo_ + s * KB(4), [8, 256]) for s in range(3)]
            rWUP = regs(3)
            d_wup = [P.dsem(f"d_wup{i}{s}") for s in range(3)]
            WDN = [bf16v(Wo_ + KB(12) + s * KB(2.75), [11, 128]) for s in range(2)]
            rWDN = regs(2)
            d_wdn = [P.dsem(f"d_wdn{i}{s}") for s in range(2)]
            for p_ in range(2):
                memset("dve", Zs[p_][:, 0:1], 0.0, [rZ[p_][0]])
                memset("dve", Zs[p_][:, L + 1:L + 2], 0.0, [rZ[p_][3]])
            wup = Wd["ffn_w_up"][i]
            wdn = Wd["ffn_w_down"][i]
            pair = 0
            for hf in range(2):
                for jj in range(11):
                    j = hf * 11 + jj
                    s = pair % 3
                    pair += 1
                    P.dma("pool", d_wup[s], [
                        (WUP[s][:, :, 0:128], wup[:, j * 128:(j + 1) * 128].rearrange("(kc p) n -> p kc n", p=128)),
                        (WUP[s][:, :, 128:256], wup[:, DFF + j * 128:DFF + (j + 1) * 128].rearrange("(kc p) n -> p kc n", p=128)),
                    ], writes=[rWUP[s]])
                    for part, Y, rY in ((0, YA, rYA), (1, YG, rYG)):
                        ch = j if part == 0 else 22 + j
                        Zp, rZp = Zs[part], rZ[part]
                        banks = []
                        for tb in range(4):
                            bk = nb()
                            banks.append(bk)
                            for kc in range(8):
                                mm(PSb(bk), WUP[s][:, kc, part * 128:(part + 1) * 128], Av[:, kc, tb * 512:(tb + 1) * 512],
                                   kc == 0, kc == 7, [rWUP[s], rAv[kc][tb]], [rPS[bk]])
                        w0 = pv(f"fcw{i}0", ch)
                        w1 = pv(f"fcw{i}1", ch)
                        w2 = pv(f"fcw{i}2", ch)
                        bb_ = pv(f"fcb{i}", ch)
                        for tb in range(4):
                            act(Zp[:, 1 + tb * 512:1 + (tb + 1) * 512], PSb(banks[tb]), AF.Copy, [rPS[banks[tb]]], [rZp[tb]])
                        for tb in range(4):
                            sl = slice(tb * 512, (tb + 1) * 512)
                            act(Y[:, sl], Zp[:, 1 + tb * 512:1 + (tb + 1) * 512], AF.Identity, [rZp[tb], rPV], [rY[tb]], scale=w1, bias=bb_)
                        for hb in range(2):
                            t0, t1 = hb * 1024, (hb + 1) * 1024
                            zr = [rZp[2 * hb], rZp[2 * hb + 1]] + ([rZp[1]] if hb == 1 else [])
                            stt("dve", Y[:, t0:t1], Zp[:, t0:t1], w0, Y[:, t0:t1], ALU.mult, ALU.add,
                                zr + [rPV, rY[2 * hb], rY[2 * hb + 1]], [rY[2 * hb], rY[2 * hb + 1]])
                        for hb in range(2):
                            t0, t1 = hb * 1024, (hb + 1) * 1024
                            zr = [rZp[2 * hb], rZp[2 * hb + 1]] + ([rZp[2]] if hb == 0 else [])
                            stt("dve", Y[:, t0:t1], Zp[:, 2 + t0:2 + t1], w2, Y[:, t0:t1], ALU.mult, ALU.add,
                                zr + [rPV, rY[2 * hb], rY[2 * hb + 1]], [rY[2 * hb], rY[2 * hb + 1]])
                    for tb in range(4):
                        sl = slice(tb * 512, (tb + 1) * 512)
                        act(YG[:, sl], YG[:, sl], AF.Silu, [rYG[tb]], [rYG[tb]])
                    for hb in range(2):
                        t0, t1 = hb * 1024, (hb + 1) * 1024
                        tt("dve", U[:, jj, t0:t1], YG[:, t0:t1], YA[:, t0:t1], ALU.mult,
                           [rYG[2 * hb], rYG[2 * hb + 1], rYA[2 * hb], rYA[2 * hb + 1]], [rU[jj][2 * hb], rU[jj][2 * hb + 1]])
                for m in range(8):
                    s = m % 2
                    P.dma("pool", d_wdn[s], [(WDN[s], wdn[hf * 1408:(hf + 1) * 1408, m * 128:(m + 1) * 128].rearrange("(kc p) n -> p kc n", p=128))],
                          writes=[rWDN[s]])
                    for tb in range(4):
                        bk = nb()
                        for kc in range(11):
                            mm(PSb(bk), WDN[s][:, kc, :], U[:, kc, tb * 512:(tb + 1) * 512], kc == 0, kc == 10, [rWDN[s], rU[kc][tb]], [rPS[bk]])
                        xs = Xv[:, m, tb * 512:(tb + 1) * 512]
                        stt("dve", xs, PSb(bk), MODX[:, i, 40 + m:41 + m], xs, ALU.mult, ALU.add, [rPS[bk], rMODX[i], rXv[m][tb]], [rXv[m][tb]])

        if not dbg_done[0]:
            ffn(0, X, rX, A, rA, KB(96), KB(140), KB(173))
            dbg_dump_fm("xffn0", lambda c, tb: X[:, c, tb * 512:(tb + 1) * 512], lambda c, tb: [rX[c][tb]])

        X2 = f32v(KB(128), [8, L])
        rX2 = regs(8, 4)
        A2 = bf16v(KB(96), [8, L])
        rA2 = regs(8, 4)
        if not dbg_done[0]:
            hyena_ok = True
            P.barrier()

            def out_Ah(c, tb, T, rT):
                act(A[:, c, tb * 512:(tb + 1) * 512], T, AF.Identity, [rT, rDER[1], rMODX[1]], [rA[c][tb]],
                    scale=DER[:, 1, 0, c:c + 1], bias=MODX[:, 1, c:c + 1])

            rmsnorm_fm(lambda c, tb: X[:, c, tb * 512:(tb + 1) * 512], lambda c, tb: [rX[c][tb]], 8, D, 512, 4, out_Ah, KB(160))
            dbg_dump_fm("hx1", lambda c, tb: A[:, c, tb * 512:(tb + 1) * 512], lambda c, tb: [rA[c][tb]])
            P.barrier()

        if not dbg_done[0]:
            d_park = P.dsem("d_park")
            X1 = bf16v(KB(96), [8, L])
            rX1 = regs(8, 4)
            VTOK = bf16v(KB(128), [16, D])
            rVTOK = regs(16)
            Zs = [f32v(KB(160), [L + 2]), f32v(0, [L + 2])]
            Y1 = f32v(KB(168) + 16, [L])
            Y2 = f32v(KB(176) + 32, [L])
            VT = [bf16v(KB(184) + 64 + s * KB(1), [512]) for s in range(2)]
            WIN = [bf16v(KB(186.5) + s * KB(6), [8, 384]) for s in range(2)]
            rZ, rY1, rY2, rVT, rWIN = regs(2, 4), regs(4), regs(4), regs(2), regs(2)
            P.dma("sp", d_park, [(xpark[:, c * L:(c + 1) * L], X[:, c, :]) for c in range(8)], reads=allr(rX) + rZ[1])
            d_win = [P.dsem("d_win0"), P.dsem("d_win1")]
            for p_ in range(2):
                memset("dve", Zs[p_][:, 0:1], 0.0, [rZ[p_][0]])
                memset("dve", Zs[p_][:, L + 1:L + 2], 0.0, [rZ[p_][3]])
            win = Wd["hy_w_in"][0]
            zsel = 0
            for j in range(8):
                s = j % 2
                P.dma("pool", d_win[s], [(WIN[s][:, :, k * 128:(k + 1) * 128],
                                          win[:, k * 1024 + j * 128:k * 1024 + (j + 1) * 128].rearrange("(kc p) n -> p kc n", p=128))
                                         for k in range(3)], writes=[rWIN[s]])
                for k, Y, rY in ((0, Y1, rY1), (1, Y1, rY1), (2, Y2, rY2)):
                    ch = k * 8 + j
                    Zp, rZp = Zs[zsel % 2], rZ[zsel % 2]
                    zsel += 1
                    banks = []
                    for tb in range(4):
                        bk = nb()
                        banks.append(bk)
                        for kc in range(8):
                            mm(PSb(bk), WIN[s][:, kc, k * 128:(k + 1) * 128], A[:, kc, tb * 512:(tb + 1) * 512], kc == 0, kc == 7,
                               [rWIN[s], rA[kc][tb]], [rPS[bk]])
                    for tb in range(4):
                        act(Zp[:, 1 + tb * 512:1 + (tb + 1) * 512], PSb(banks[tb]), AF.Identity, [rPS[banks[tb]], rPV], [rZp[tb]],
                            scale=1.0, bias=pv("hbin", ch))
                    for tb in range(4):
                        sl = slice(tb * 512, (tb + 1) * 512)
                        act(Y[:, sl], Zp[:, 1 + tb * 512:1 + (tb + 1) * 512], AF.Identity, [rZp[tb], rPV], [rY[tb]],
                            scale=pv("hcw1", ch), bias=pv("hcb", ch))
                    for hb in range(2):
                        t0, t1 = hb * 1024, (hb + 1) * 1024
                        zr = [rZp[2 * hb], rZp[2 * hb + 1]] + ([rZp[1]] if hb == 1 else [])
                        stt("dve", Y[:, t0:t1], Zp[:, t0:t1], pv("hcw0", ch), Y[:, t0:t1], ALU.mult, ALU.add,
                            zr + [rPV, rY[2 * hb], rY[2 * hb + 1]], [rY[2 * hb], rY[2 * hb + 1]])
                    for hb in range(2):
                        t0, t1 = hb * 1024, (hb + 1) * 1024
                        zr = [rZp[2 * hb], rZp[2 * hb + 1]] + ([rZp[2]] if hb == 0 else [])
                        if k == 0:
                            stt("dve", X1[:, j, t0:t1], Zp[:, 2 + t0:2 + t1], pv("hcw2", ch), Y[:, t0:t1], ALU.mult, ALU.add,
                                zr + [rPV, rY[2 * hb], rY[2 * hb + 1]], [rX1[j][2 * hb], rX1[j][2 * hb + 1]])
                        else:
                            stt("dve", Y[:, t0:t1], Zp[:, 2 + t0:2 + t1], pv("hcw2", ch), Y[:, t0:t1], ALU.mult, ALU.add,
                                zr + [rPV, rY[2 * hb], rY[2 * hb + 1]], [rY[2 * hb], rY[2 * hb + 1]])
                for tb in range(4):
                    sl = slice(tb * 512, (tb + 1) * 512)
                    tt("dve", VT[tb % 2], Y2[:, sl], Y1[:, sl], ALU.mult, [rY1[tb], rY2[tb]], [rVT[tb % 2]])
                    bk = nb()
                    pbf = PSb(bk).bitcast(BF16)
                    for q in range(4):
                        P.op("pe", lambda e, q=q, pbf=pbf, tb=tb: e.transpose(pbf[:, q * 128:(q + 1) * 128], VT[tb % 2][:, q * 128:(q + 1) * 128], IDN),
                             [rVT[tb % 2], rCONST], [rPS[bk]])
                    cp("dve", VTOK[:, tb * 4:tb * 4 + 4, j * 128:(j + 1) * 128], pbf[:, 0:512].rearrange("p (a b) -> p a b", a=4),
                       [rPS[bk]], [rVTOK[tb * 4 + q] for q in range(4)])
            dbg_dump_fm("hy_x1", lambda c, tb: X1[:, c, tb * 512:(tb + 1) * 512], lambda c, tb: [rX1[c][tb]])

        if not dbg_done[0]:
            P.barrier()
            H2 = bf16v(KB(160), [L])
            FW3 = bf16v(KB(164), [2048])
            ABSD = f32v(KB(168), [512])
            DBR = f32v(KB(170), [512])
            WINT = f32v(KB(172), [512])
            HTMP = f32v(KB(174), [512])
            ZT = f32v(KB(176), [L])
            H1 = f32v(KB(184), [L])
            TMPA = f32v(KB(192), [512])
            TMPB = f32v(KB(194), [512])
            FW1 = f32v(KB(196), [64])
            FW2 = f32v(KB(196.25), [64])
            NEGT = f32v(KB(196.5), [16])
            FB = f32v(KB(196.5) + 16, [2])
            rZT, rH1, rH2, rTA, rTB, rFW, rNEGT, rFB, rABSD, rDBR, rWINt, rHT = [Reg() for _ in range(12)]
            P.dma("sp", d_misc, [(ZT[0:33, :], zT_d), (FW1[0:33, :], Wd["hy_f_w1"][0]), (FW2[0:64, :], Wd["hy_f_w2"][0]),
                                 (NEGT, negt_d)], writes=[rZT, rFW, rNEGT])
            P.dma("pool", d_w, [(FW3[0:64, :], Wd["hy_f_w3"][0])], writes=[rFW])
            tt("dve", FB[0:64, 0:1], pv("fb1")[0:64, :], pv("ffr1")[0:64, :], ALU.mult, [rPV], [rFB])
            tt("dve", FB[0:64, 1:2], pv("fb2")[0:64, :], pv("ffr2")[0:64, :], ALU.mult, [rPV], [rFB])

            def sin_layer(lhsT, rhs_fn, rrhs, frq, fbcol, out_fn, rout):
                for tb in range(4):
                    sl = slice(tb * 512, (tb + 1) * 512)
                    bk = nb()
                    mm(PSb(bk)[0:64, :], lhsT, rhs_fn(sl), True, True, [rFW, rrhs], [rPS[bk]])
                    act(TMPA[0:64, :], PSb(bk)[0:64, :], AF.Identity, [rPS[bk], rPV, rFB], [rTA], scale=frq, bias=fbcol)
                    ts("dve", TMPB[0:64, :], TMPA[0:64, :], 1.0 / (2 * math.pi), MAGIC, ALU.mult, ALU.add, [rTA], [rTB])
                    ts("dve", TMPB[0:64, :], TMPB[0:64, :], MAGIC, -2.0 * math.pi, ALU.subtract, ALU.mult, [rTB], [rTB])
                    tt("dve", TMPA[0:64, :], TMPA[0:64, :], TMPB[0:64, :], ALU.add, [rTA, rTB], [rTA])
                    ts("dve", TMPA[0:64, :], TMPA[0:64, :], 3.1415925, -3.1415925, ALU.min, ALU.max, [rTA], [rTA])
                    act(out_fn(sl), TMPA[0:64, :], AF.Sin, [rTA], [rout])

            sin_layer(FW1[0:33, :], lambda sl: ZT[0:33, sl], rZT, pv("ffr1")[0:64, :], FB[0:64, 0:1], lambda sl: H1[0:64, sl], rH1)
            sin_layer(FW2[0:64, :], lambda sl: H1[0:64, sl], rH1, pv("ffr2")[0:64, :], FB[0:64, 1:2], lambda sl: H2[0:64, sl], rH2)
            P.barrier()

            HFB = bf16v(KB(64), [16, 2, 512])
            rHFB = regs(16)
            Y = bf16v(0, [32, D])
            rY = regs(32, 2)
            FWS = [bf16v(KB(176) + s * KB(4), [16, 128]) for s in range(2)]
            rFWS = regs(2)
            d_fw = [P.dsem("d_fw0"), P.dsem("d_fw1")]
            MT = [f32v(KB(184) + s * KB(2), [512]) for s in range(6)]
            rMT = regs(6)
            HTM2 = MT[5]
            rHT2 = rMT[5]
            fwk = 0

            for cb in range(2):
                c0 = cb * 512
                P.dma("sp", d_misc, [(DBR[0:1, :], rows_d[1:2, c0:c0 + 512]),
                                     (ABSD, rows_d[0:1, c0:c0 + 512].to_broadcast([128, 512]))], writes=[rDBR, rABSD])
                act(ABSD, ABSD, AF.Abs, [rABSD], [rABSD])
                for st_ in range(16):
                    bk, bk2 = nb(), nb()
                    hs = H2[0:64, st_ * 128:(st_ + 1) * 128]
                    mm(PSb(bk), hs, FW3[0:64, c0:c0 + 512], True, True, [rH2, rFW], [rPS[bk]])
                    mm(PSb(bk2), hs, FW3[0:64, 1024 + c0:1024 + c0 + 512], True, True, [rH2, rFW], [rPS[bk2]])
                    act(WINT, ABSD, AF.Exp, [rABSD, rNEGT], [rWINt], scale=NEGT[:, st_:st_ + 1])
                    tt("dve", HTMP, PSb(bk), WINT, ALU.mult, [rPS[bk], rWINt], [rHT])
                    tt("dve", HTM2, PSb(bk2), WINT, ALU.mult, [rPS[bk2], rWINt], [rHT2])
                    if st_ == 0:
                        tt("dve", HTMP[0:1, :], HTMP[0:1, :], DBR[0:1, :], ALU.add, [rHT, rDBR], [rHT])
                        memset("dve", HTM2[0:1, :], 0.0, [rHT2])
                    tt("dve", HFB[:, st_, 0, :], HTMP, HTM2, ALU.add, [rHT, rHT2], [rHFB[st_]])
                    tt("dve", HFB[:, st_, 1, :], HTM2, HTMP, ALU.subtract, [rHT, rHT2], [rHFB[st_]])
                for c in range(16):
                    pb = {}
                    for part in range(2):
                        fm = part * 16 + c
                        s = fwk % 2
                        fwk += 1
                        P.dma("sp", d_fw[s], [(FWS[s].rearrange("p a b -> p (a b)"), fw_d[fm])], writes=[rFWS[s]])
                        bv, bf_ = nb(), nb()
                        pb[part] = (bv, bf_)
                        for tc in range(16):
                            mm(PSb(bv), FWS[s][:, tc, :], VTOK[:, tc, c0:c0 + 512], tc == 0, tc == 15, [rFWS[s], rVTOK[tc]], [rPS[bv]])
                        for tc in range(16):
                            mm(PSb(bf_), FWS[s][:, tc, :], HFB[:, tc, part, :], tc == 0, tc == 15, [rFWS[s], rHFB[tc]], [rPS[bf_]])
                    (vc, kre_b), (vs, kim_b) = pb[0], pb[1]
                    KRE, KIM, M1, M2 = MT[0:4]
                    rkre, rkim, m1, m2 = rMT[0:4]
                    act(KRE, PSb(kre_b), AF.Copy, [rPS[kre_b]], [rkre])
                    act(KIM, PSb(kim_b), AF.Copy, [rPS[kim_b]], [rkim])
                    tt("dve", M1, PSb(vc), KRE, ALU.mult, [rPS[vc], rkre], [m1])
                    tt("dve", M2, PSb(vs), KIM, ALU.mult, [rPS[vs], rkim], [m2])
                    tt("dve", Y[:, c, c0:c0 + 512], M1, M2, ALU.add, [m1, m2], [rY[c][cb]])
                    tt("dve", M1, PSb(vc), KIM, ALU.mult, [rPS[vc], rkim, m1], [m1])
                    tt("dve", M2, PSb(vs), KRE, ALU.mult, [rPS[vs], rkre, m2], [m2])
                    tt("dve", Y[:, 16 + c, c0:c0 + 512], M1, M2, ALU.subtract, [m1, m2], [rY[16 + c][cb]])

            P.barrier()
            P.dma("sp", d_x, [(X2[:, c, :], xpark[:, c * L:(c + 1) * L]) for c in range(8)], writes=allr(rX2))
            IVS = [bf16v(KB(64) + s * KB(16), [32, 256]) for s in range(2)]
            rIVS = regs(2)
            d_iv = [P.dsem("d_iv0"), P.dsem("d_iv1")]
            for tq in range(8):
                s = tq % 2
                P.dma("sp", d_iv[s], [(IVS[s].rearrange("p a b -> p (a b)"), iv_d[tq])], writes=[rIVS[s]])
                for chc in range(8):
                    bk = nb()
                    for kc in range(32):
                        mm(PSb(bk)[:, 0:256], Y[:, kc, chc * 128:(chc + 1) * 128], IVS[s][:, kc, :], kc == 0, kc == 31,
                           [rY[kc][chc // 4], rIVS[s]], [rPS[bk]])
                    xs = X1[:, chc, tq * 256:(tq + 1) * 256]
                    stt("dve", xs, PSb(bk)[:, 0:256], 1.0 / 2048.0, xs, ALU.mult, ALU.mult, [rPS[bk], rX1[chc][tq // 2]], [rX1[chc][tq // 2]])
            dbg_dump_fm("hy_y", lambda c, tb: X1[:, c, tb * 512:(tb + 1) * 512], lambda c, tb: [rX1[c][tb]])

        if not dbg_done[0]:
            P.barrier()
            WOUT = bf16v(KB(64), [8, 1024])
            rWOUT = Reg()
            TO = [f32v(KB(80) + s * KB(2), [512]) for s in range(2)]
            rTO = regs(2)
            P.dma("pool", d_w, [(WOUT, Wd["hy_w_out"][0].rearrange("(kc p) n -> p kc n", p=128))], writes=[rWOUT])
            k = 0
            for m in range(8):
                for tb in range(4):
                    bk = nb()
                    for kc in range(8):
                        mm(PSb(bk), WOUT[:, kc, m * 128:(m + 1) * 128], X1[:, kc, tb * 512:(tb + 1) * 512], kc == 0, kc == 7,
                           [rWOUT, rX1[kc][tb]], [rPS[bk]])
                    s = k % 2
                    k += 1
                    act(TO[s], PSb(bk), AF.Identity, [rPS[bk], rMODX[1], rDER[1]], [rTO[s]], scale=MODX[:, 1, 16 + m:17 + m], bias=DER[:, 1, 2, m:m + 1])
                    xs = X2[:, m, tb * 512:(tb + 1) * 512]
                    tt("dve", xs, xs, TO[s], ALU.add, [rTO[s], rX2[m][tb]], [rX2[m][tb]])
            dbg_dump_fm("xmix1", lambda c, tb: X2[:, c, tb * 512:(tb + 1) * 512], lambda c, tb: [rX2[c][tb]])

        if not dbg_done[0]:
            ffn(1, X2, rX2, A2, rA2, 0, KB(44), KB(77))
            dbg_dump_fm("xffn1", lambda c, tb: X2[:, c, tb * 512:(tb + 1) * 512], lambda c, tb: [rX2[c][tb]])

        if not dbg_done[0]:
            P.barrier()
            OS = [f32v(KB(16) + s * KB(2), [512]) for s in range(4)]
            rOS = regs(4)
            kk = [0]

            def out_final(c, tb, T, rT):
                s = kk[0] % 4
                kk[0] += 1
                act(OS[s], T, AF.Identity, [rT, rPV], [rOS[s]], scale=pv("gfin", c), bias=0.0)
                P.dma("sp", d_out, out_pairs(c * 128, 128, tb * 512, 512, OS[s]), reads=[rOS[s]])

            rmsnorm_fm(lambda c, tb: X2[:, c, tb * 512:(tb + 1) * 512], lambda c, tb: [rX2[c][tb]], 8, D, 512, 4, out_final, 0)

        P.finish()
    return nc


_NC_CACHE = {}


def _get_nc(stage="all"):
    if stage not in _NC_CACHE:
        _NC_CACHE[stage] = build(stage)
    return _NC_CACHE[stage]


def make_in_maps(inp):
    cst = _consts()
    shared = {n: np.ascontiguousarray(np.asarray(inp[n], np.float32)) for n in WEIGHT_NAMES}
    rows = np.ascontiguousarray(np.stack([np.asarray(inp["hy_decay"][0], np.float32), np.asarray(inp["hy_d_bias"][0], np.float32)]))
    big = dict(shared)
    big.update({k: cst[k] for k in ("tbl", "fw", "iv")})
    small = {k: cst[k] for k in cst if k not in big}
    maps = []
    for b in range(NCORES):
        m = dict(small)
        for k, v in big.items():
            tag = np.full(v.shape[:-2] + (1, v.shape[-1]), b, dtype=v.dtype)
            m[k] = np.concatenate([v, tag], axis=-2)
        m["rows"] = rows
        m["xT"] = np.ascontiguousarray(np.asarray(inp["x"][b], np.float32).T)
        m["ctxT"] = np.ascontiguousarray(np.asarray(inp["ctx"][b], np.float32).T)
        m["pvec"] = _pack_pvec(inp, b)
        maps.append(m)
    return maps


def gather_out(r):
    return np.ascontiguousarray(np.concatenate([r[f"o{k}"] for k in range(16)], axis=0).T)


def kernel(**inputs):
    inp = {k: np.asarray(v) for k, v in inputs.items()}
    nc = _get_nc("all")
    res = run_bass_kernel_spmd(nc, make_in_maps(inp), core_ids=list(range(NCORES)))
    out = np.stack([gather_out(r) for r in res.results], axis=0)
    return out.astype(np.float32)
```
